# Optimizing a Trainium2 kernel written in Bass

```python
import jax, jax.numpy as jnp
from jax import lax
import numpy as np

D_MODEL = 1024
BATCH = 4
SEQ = 4096
DEPTH = 2
DEC_BATCH = 128
DEC_SEQ = 1
PAST_LEN = 16384
PAGE_SIZE = 128

N_A_LAYERS = DEPTH // 2
N_B_LAYERS = DEPTH - N_A_LAYERS
POOL_WINDOWS = (2, 4, 8, 16)
N_POOL_GROUPS = len(POOL_WINDOWS)
POOL_GROUP = D_MODEL // N_POOL_GROUPS
POOL_BUF = max(POOL_WINDOWS) - 1
HEAD_DIM = 64
N_HEADS = D_MODEL // HEAD_DIM
N_KV_HEADS = 4
GROUP = N_HEADS // N_KV_HEADS
KV_DIM = N_KV_HEADS * HEAD_DIM
WINDOW = 128
BLOCK = WINDOW
ROT_DIM = HEAD_DIM // 4
ROPE_THETA = 500000.0
D_FF = 2816
EPS = 1e-5

kernel_name = 'yoco_pool_swa_sink_macaron_step'


def _rmsnorm(x, g):
    xf = x.astype(jnp.float32)
    y = xf * lax.rsqrt(jnp.mean(xf * xf, axis=-1, keepdims=True) + EPS)
    return (y * g.astype(jnp.float32)).astype(x.dtype)


def _swiglu(x, wg, wu, wd):
    return (jax.nn.silu(x @ wg) * (x @ wu)) @ wd


def _rope(x, pos):
    half = ROT_DIM // 2
    inv = jnp.power(ROPE_THETA, -jnp.arange(half, dtype=jnp.float32) * (2.0 / ROT_DIM))
    ang = pos.astype(jnp.float32)[:, None] * inv[None, :]
    shp = (1, ang.shape[0]) + (1,) * (x.ndim - 3) + (half,)
    cos = jnp.cos(ang).reshape(shp)
    sin = jnp.sin(ang).reshape(shp)
    xf = x.astype(jnp.float32)
    x1 = xf[..., :half]
    x2 = xf[..., half:ROT_DIM]
    out = jnp.concatenate([x1 * cos - x2 * sin, x2 * cos + x1 * sin, xf[..., ROT_DIM:]], axis=-1)
    return out.astype(x.dtype)


def _pool_mixer(u, buf, pos0, w_pool, scale):
    B, T, D = u.shape
    ext_raw = jnp.concatenate([buf.astype(u.dtype), u], axis=1)
    ext = ext_raw.astype(jnp.float32)
    cs = jnp.concatenate([jnp.zeros((B, 1, D), jnp.float32), jnp.cumsum(ext, axis=1)], axis=1)
    end = cs[:, POOL_BUF + 1:POOL_BUF + 1 + T]
    pos = pos0 + jnp.arange(T)
    means = []
    for gi, w in enumerate(POOL_WINDOWS):
        c0, c1 = gi * POOL_GROUP, (gi + 1) * POOL_GROUP
        start = cs[:, POOL_BUF + 1 - w:POOL_BUF + 1 - w + T, c0:c1]
        cnt = jnp.minimum(w, pos + 1).astype(jnp.float32)[None, :, None]
        means.append((end[..., c0:c1] - start) / cnt)
    pooled = jnp.concatenate(means, axis=-1)
    diff = (pooled - u.astype(jnp.float32)).reshape(B, T, N_POOL_GROUPS, POOL_GROUP)
    mixed = jnp.einsum('btgc,gcd->btgd', diff, w_pool.astype(jnp.float32)).reshape(B, T, D)
    out = (mixed * scale.astype(jnp.float32)).astype(u.dtype)
    return out, ext_raw[:, -POOL_BUF:]


def _sink_attention(q, k, v, mask, sinks):
    s = jnp.einsum('bnqkgd,bnskd->bnkgqs', q.astype(jnp.float32), k.astype(jnp.float32)) * (HEAD_DIM ** -0.5)
    s = jnp.where(mask[None, :, None, None], s, -jnp.inf)
    sink = sinks.astype(jnp.float32).reshape(1, 1, N_KV_HEADS, GROUP, 1, 1)
    m = jnp.maximum(jnp.max(s, axis=-1, keepdims=True), sink)
    p = jnp.exp(s - m)
    denom = jnp.sum(p, axis=-1, keepdims=True) + jnp.exp(sink - m)
    return jnp.einsum('bnkgqs,bnskd->bnqkgd', p / denom, v.astype(jnp.float32))


def _window_attention(q, kk, vv, wb, pos0, sinks):
    B, T = q.shape[0], q.shape[1]
    qpos = pos0 + jnp.arange(T)
    kpos = pos0 - wb + jnp.arange(wb + T)
    if T % BLOCK == 0 and wb == BLOCK:
        nb = T // BLOCK
        qb = q.reshape(B, nb, BLOCK, N_KV_HEADS, GROUP, HEAD_DIM)
        kb = jnp.concatenate([kk[:, :T].reshape(B, nb, BLOCK, N_KV_HEADS, HEAD_DIM),
                              kk[:, wb:].reshape(B, nb, BLOCK, N_KV_HEADS, HEAD_DIM)], axis=2)
        vb = jnp.concatenate([vv[:, :T].reshape(B, nb, BLOCK, N_KV_HEADS, HEAD_DIM),
                              vv[:, wb:].reshape(B, nb, BLOCK, N_KV_HEADS, HEAD_DIM)], axis=2)
        qp = qpos.reshape(nb, BLOCK)
        kp = jnp.concatenate([kpos[:T].reshape(nb, BLOCK), kpos[wb:].reshape(nb, BLOCK)], axis=1)
    else:
        qb, kb, vb = q[:, None], kk[:, None], vv[:, None]
        qp, kp = qpos[None], kpos[None]
    d = qp[:, :, None] - kp[:, None, :]
    mask = (d >= 0) & (d < WINDOW) & (kp[:, None, :] >= 0)
    o = _sink_attention(qb, kb, vb, mask, sinks)
    return o.reshape(B, T, N_HEADS * HEAD_DIM).astype(q.dtype)


def _trunk(x, pos0, pool_bufs, k_buf, v_buf, norm_g, ffn_w_gate, ffn_w_up, ffn_w_down, pool_w,
           pool_scale, kv_norm_g, w_kv, w_q, w_o, attn_sinks, final_norm_g):
    B, T, _ = x.shape
    pos = pos0 + jnp.arange(T)
    wb = k_buf.shape[1]
    new_pool = []
    kk = vv = None
    for l in range(DEPTH):
        if l == N_A_LAYERS:
            kv = _rmsnorm(x, kv_norm_g) @ w_kv
            k_new = _rope(kv[..., :KV_DIM].reshape(B, T, N_KV_HEADS, HEAD_DIM), pos)
            v_new = kv[..., KV_DIM:].reshape(B, T, N_KV_HEADS, HEAD_DIM)
            kk = jnp.concatenate([k_buf.astype(k_new.dtype), k_new], axis=1)
            vv = jnp.concatenate([v_buf.astype(v_new.dtype), v_new], axis=1)
        x = x + 0.5 * _swiglu(_rmsnorm(x, norm_g[l, 0]), ffn_w_gate[l, 0], ffn_w_up[l, 0], ffn_w_down[l, 0])
        h = _rmsnorm(x, norm_g[l, 1])
        if l < N_A_LAYERS:
            out, nbuf = _pool_mixer(h, pool_bufs[l], pos0, pool_w[l], pool_scale[l])
            new_pool.append(nbuf)
        else:
            j = l - N_A_LAYERS
            q = _rope((h @ w_q[j]).reshape(B, T, N_KV_HEADS, GROUP, HEAD_DIM), pos)
            out = _window_attention(q, kk, vv, wb, pos0, attn_sinks[j]) @ w_o[j]
        x = x + out
        x = x + 0.5 * _swiglu(_rmsnorm(x, norm_g[l, 2]), ffn_w_gate[l, 1], ffn_w_up[l, 1], ffn_w_down[l, 1])
    y = _rmsnorm(x, final_norm_g)
    return y, jnp.stack(new_pool, axis=0), kk[:, -wb:], vv[:, -wb:]


def setup_inputs(seed: int = 0) -> dict:
    key = jax.random.key(seed)
    ks = jax.random.split(key, 20)
    f32 = jnp.float32
    wb = min(WINDOW, PAST_LEN)
    nrm = lambda k, s, sc: jax.random.normal(k, s, f32) * sc
    return {
        'x_prompt': nrm(ks[0], (BATCH, SEQ, D_MODEL), 1.0),
        'x_sample': nrm(ks[1], (DEC_BATCH, DEC_SEQ, D_MODEL), 1.0),
        'state_pool': nrm(ks[2], (N_A_LAYERS, DEC_BATCH, POOL_BUF, D_MODEL), 1.0),
        'cache_k_win': nrm(ks[3], (DEC_BATCH, wb, N_KV_HEADS, HEAD_DIM), 1.0),
        'cache_v_win': nrm(ks[4], (DEC_BATCH, wb, N_KV_HEADS, HEAD_DIM), 1.0),
        'norm_g': 1.0 + nrm(ks[5], (DEPTH, 3, D_MODEL), 0.05),
        'ffn_w_gate': nrm(ks[6], (DEPTH, 2, D_MODEL, D_FF), D_MODEL ** -0.5),
        'ffn_w_up': nrm(ks[7], (DEPTH, 2, D_MODEL, D_FF), D_MODEL ** -0.5),
        'ffn_w_down': nrm(ks[8], (DEPTH, 2, D_FF, D_MODEL), D_FF ** -0.5),
        'pool_w': nrm(ks[9], (N_A_LAYERS, N_POOL_GROUPS, POOL_GROUP, POOL_GROUP), POOL_GROUP ** -0.5),
        'pool_scale': 1.0 + nrm(ks[10], (N_A_LAYERS, D_MODEL), 0.1),
        'kv_norm_g': 1.0 + nrm(ks[11], (D_MODEL,), 0.05),
        'w_kv': nrm(ks[12], (D_MODEL, 2 * KV_DIM), D_MODEL ** -0.5),
        'w_q': nrm(ks[13], (N_B_LAYERS, D_MODEL, N_HEADS * HEAD_DIM), D_MODEL ** -0.5),
        'w_o': nrm(ks[14], (N_B_LAYERS, N_HEADS * HEAD_DIM, D_MODEL), (N_HEADS * HEAD_DIM) ** -0.5),
        'attn_sinks': nrm(ks[15], (N_B_LAYERS, N_HEADS), 0.5),
        'final_norm_g': 1.0 + nrm(ks[16], (D_MODEL,), 0.05),
    }


def reference(x_prompt, x_sample, state_pool, cache_k_win, cache_v_win, norm_g, ffn_w_gate, ffn_w_up,
              ffn_w_down, pool_w, pool_scale, kv_norm_g, w_kv, w_q, w_o, attn_sinks, final_norm_g):
    Bp = x_prompt.shape[0]
    pool0 = jnp.zeros((N_A_LAYERS, Bp, POOL_BUF, D_MODEL), x_prompt.dtype)
    kbuf0 = jnp.zeros((Bp, WINDOW, N_KV_HEADS, HEAD_DIM), x_prompt.dtype)
    y_prompt, pool_prompt, k_win_prompt, v_win_prompt = _trunk(
        x_prompt, 0, pool0, kbuf0, kbuf0, norm_g, ffn_w_gate, ffn_w_up, ffn_w_down, pool_w,
        pool_scale, kv_norm_g, w_kv, w_q, w_o, attn_sinks, final_norm_g)
    y_sample, pool_sample, k_win_sample, v_win_sample = _trunk(
        x_sample, PAST_LEN, state_pool, cache_k_win, cache_v_win, norm_g, ffn_w_gate, ffn_w_up,
        ffn_w_down, pool_w, pool_scale, kv_norm_g, w_kv, w_q, w_o, attn_sinks, final_norm_g)
    return (y_prompt, y_sample, pool_prompt, pool_sample, k_win_prompt, v_win_prompt, k_win_sample, v_win_sample)
```

```python
import numpy as np
from contextlib import ExitStack
import concourse.bass as bass
import concourse.mybir as mybir
from concourse.bass_utils import run_bass_kernel_spmd

F32 = mybir.dt.float32
BF16 = mybir.dt.bfloat16
AF = mybir.ActivationFunctionType
ALU = mybir.AluOpType
AX = mybir.AxisListType

D = 1024
DFF = 2816
NJ = DFF // 128
NC8 = 8
HALO = 143
MAIN = 2048
NS = 16
NCOL = HALO + MAIN + NS
CM0 = HALO
CS0 = HALO + MAIN
EPS = 1e-5
PAST_LEN = 16384
ROPE_THETA = 500000.0
QA = [0, 1, 2, 3, 8, 9, 10, 11]
QB = [4, 5, 6, 7, 12, 13, 14, 15]
G_N = lambda l, i: l * 3 + i
G_KV, G_FIN, G_PS = 6, 7, 8
import os
KVOPT = int(os.environ.get('KVOPT', '9'))
ATTN_S = int(os.environ.get('ATTN_S', '1'))
PIPE = int(os.environ.get('PIPE', '1'))
POOLPE = int(os.environ.get('POOLPE', '1'))
POOLC = int(os.environ.get('POOLC', '8'))
ENGS = ["pe", "act", "dve", "pool", "sp"]


def split(a, b, n):
    tot = b - a
    out = []
    s = a
    for i in range(n):
        e = a + (tot * (i + 1)) // n
        out.append((s, e))
        s = e
    return out


class Buf:
    __slots__ = ("name", "w", "r", "excl")

    def __init__(self, name, excl=False):
        self.name = name
        self.w = None
        self.r = {}
        self.excl = excl


class Prog:
    def __init__(self, nc, stack):
        self.nc = nc
        self.stack = stack
        self.sems = {e: stack.enter_context(nc.semaphore("q_" + e)) for e in ENGS}
        self.cnt = {e: 0 for e in ENGS}
        self.seen = {e: {} for e in ENGS}
        self.ops = {e: [] for e in ENGS}
        self.dsem = {}
        self.dcnt = {}

    def semobj(self, k):
        return self.sems[k] if k in self.sems else self.dsem[k]

    def _waits(self, eng, reads, writes):
        deps = {}

        def add(tok, same_ok):
            if tok is None:
                return
            k, v = tok
            if k == eng and not same_ok:
                return
            if deps.get(k, 0) < v:
                deps[k] = v

        for b in reads:
            add(b.w, True)
            if b.excl:
                for k, v in b.r.items():
                    add((k, v), False)
        for b in writes:
            add(b.w, False)
            for k, v in b.r.items():
                add((k, v), False)
        waits = []
        for k, v in deps.items():
            if self.seen[eng].get(k, 0) >= v:
                continue
            self.seen[eng][k] = v
            waits.append((k, v))
        return waits

    def op(self, eng, fn, reads=(), writes=(), inc=True):
        waits = self._waits(eng, reads, writes)
        val = self.cnt[eng] + 1
        if inc:
            self.cnt[eng] = val
        for b in reads:
            if b.r.get(eng, 0) < val:
                b.r[eng] = val
        for b in writes:
            b.w = (eng, val)
            b.r = {}
        self.ops[eng].append((waits, fn, inc))

    def dma(self, eng, sem, out, in_, reads=(), writes=()):
        if sem not in self.dsem:
            self.dsem[sem] = self.stack.enter_context(self.nc.semaphore("d_" + sem))
            self.dcnt[sem] = 0
        waits = self._waits(eng, reads, writes)
        self.dcnt[sem] += 16
        val = self.dcnt[sem]
        for b in reads:
            if b.r.get(sem, 0) < val:
                b.r[sem] = val
        for b in writes:
            b.w = (sem, val)
            b.r = {}
        so = self.dsem[sem]

        def fn(e, out=out, in_=in_, so=so):
            e.dma_start(out=out, in_=in_).then_inc(so, 16)
            return None

        self.ops[eng].append((waits, fn, False))

    def wait_all_dma(self, eng):
        waits = []
        for k, v in self.dcnt.items():
            if self.seen[eng].get(k, 0) < v:
                self.seen[eng][k] = v
                waits.append((k, v))
        self.ops[eng].append((waits, None, False))

    def flush(self):
        with self.nc.Block() as blk:
            decos = {"pe": blk.tensor, "act": blk.scalar, "dve": blk.vector,
                     "pool": blk.gpsimd, "sp": blk.sync}
            for eng in ENGS:
                ops = self.ops[eng]
                sem = self.sems[eng]

                def body(e, ops=ops, sem=sem):
                    for waits, fn, inc in ops:
                        for k, v in waits:
                            e.wait_ge(self.semobj(k), v)
                        if fn is None:
                            continue
                        ins = fn(e)
                        if inc:
                            ins.then_inc(sem, 1)

                decos[eng](body)
                self.ops[eng] = []


def build_program(stages=99, debug=False):
    nc = bass.Bass("TRN2", target_bir_lowering=False)

    def din(name, shape, dt=F32):
        return nc.dram_tensor(name, list(shape), dt, kind="ExternalInput").ap()

    def dout(name, shape, dt=F32):
        return nc.dram_tensor(name, list(shape), dt, kind="ExternalOutput").ap()

    xin = din("xin", [NCOL, D])
    wg = din("wg", [2, 2, D, DFF])
    wu = din("wu", [2, 2, D, DFF])
    wd = din("wd", [2, 2, DFF, D])
    pw = din("pw", [4, 256, 256])
    wkv = din("wkv", [D, 512])
    wq = din("wq", [D, D])
    wo = din("wo", [D, D])
    gains_d = din("gains", [128, 9 * 8])
    sinks_d = din("sinks", [128, 16])
    state_d = din("state", [NS * 15, D])
    ck_d = din("ck", [NS, 128, 256])
    cv_d = din("cv", [NS, 128, 256])
    ropef_d = din("ropef", [128, 2, NCOL])
    ropet_d = din("ropet", [144, 2, 4 * 8])
    masks_d = din("masks", [128, 3 * 128])
    pcorr_d = din("pcorr", [128, 4 * 15])
    ident_d = din("ident", [128, 128])
    rmat_d = din("rmat", [128, 128])

    y_o = dout("y", [MAIN + NS, D])
    poolo_o = dout("pool_o", [31, D])
    pools_o = dout("pool_s", [NS, 15, D])
    kwp_o = dout("kwp", [128, 256])
    vwp_o = dout("vwp", [128, 256])
    kws_o = dout("kws", [NS, 128, 256])
    vws_o = dout("vws", [NS, 128, 256])
    dbg_o = dout("dbg", [128, 8 * NCOL]) if debug else None

    with ExitStack() as top:
        P = Prog(nc, top)

        uid = [0]

        def sb(st, name, shape, dt):
            uid[0] += 1
            t = st.enter_context(nc.sbuf_tensor(f"s{uid[0]}_{name}", list(shape), dt))
            return t, Buf(name)

        xT, xT_b = sb(top, "xT", [128, NC8, NCOL], F32)
        xTb = [Buf(f"xT{c}") for c in range(NC8)]
        gains, gains_b = sb(top, "gains", [128, 9, 8], F32)
        ident, ident_b = sb(top, "ident", [128, 128], F32)
        onesm, onesm_b = sb(top, "onesm", [128, 128], BF16)
        ones1, ones1_b = sb(top, "ones1", [128, 128], BF16)
        epst, epst_b = sb(top, "epst", [128, 1], F32)
        psum = top.enter_context(nc.psum_tensor("ps", [128, 8, 512], F32))
        pb = [Buf(f"bank{k}", excl=True) for k in range(8)]
        bank_rr = [0]

        def next_bank():
            k = bank_rr[0]
            bank_rr[0] = (k + 1) % 8
            return k

        P.dma("sp", "c_gains", gains[:].rearrange("p a b -> p (a b)"), gains_d[:, :], writes=[gains_b])
        P.dma("sp", "c_ident", ident[:], ident_d[:, :], writes=[ident_b])
        P.op("dve", lambda e: e.memset(onesm[:], 1.0 / 1024.0), writes=[onesm_b])
        P.op("dve", lambda e: e.memset(ones1[:], 1.0), writes=[ones1_b])
        P.op("dve", lambda e: e.memset(epst[:], EPS), writes=[epst_b])

        def p0_tiles(st):
            NST = 3
            stg = [sb(st, f"stg{k}", [128, D], F32) for k in range(NST)]
            nrt = (NCOL + 127) // 128
            out = []
            for r in range(nrt):
                out.append((r * 128, lambda r=r: p0_tile(stg, r)))
            return out

        def p0_tile(stg, r):
            NST = 3
            if True:
                r0 = r * 128
                rows = min(128, NCOL - r0)
                s_t, s_b = stg[r % NST]
                P.dma("sp", f"stg{r % NST}", s_t[0:rows, :], xin[r0:r0 + rows, :], writes=[s_b])
                for hf in range(2):
                    k = next_bank()
                    for cc in range(4):
                        c = hf * 4 + cc
                        P.op("pe", lambda e, k=k, cc=cc, c=c, s_t=s_t, rows=rows: e.transpose(
                            out=psum[:, k, cc * 128:cc * 128 + rows], in_=s_t[0:rows, c * 128:(c + 1) * 128],
                            identity=ident[0:rows, 0:rows]),
                            reads=[s_b, ident_b], writes=[pb[k]], inc=(cc == 3))
                    src = psum[:, k, :].rearrange("p (c n) -> p c n", c=4)[:, :, 0:rows]
                    dst = xT[:, hf * 4:hf * 4 + 4, r0:r0 + rows]
                    wr = [xTb[c] for c in range(hf * 4, hf * 4 + 4)]
                    if (r + hf) % 2 == 0:
                        P.op("dve", lambda e, dst=dst, src=src: e.tensor_copy(out=dst, in_=src),
                             reads=[pb[k]], writes=wr)
                    else:
                        P.op("act", lambda e, dst=dst, src=src: e.copy(out=dst, in_=src),
                             reads=[pb[k]], writes=wr)

        def norm_block(st_bufs, c0, c1, grow, out_t, out_b, out_off):
            norm_sq(st_bufs, c0, c1)
            norm_rest(st_bufs, c0, c1, grow, out_t, out_b, out_off)

        def norm_sq(st_bufs, c0, c1):
            sq_t, sq_b, rs_t, rs_b = st_bufs
            n = c1 - c0
            P.op("act", lambda e: e.activation(out=sq_t[:, :, 0:n], in_=xT[:, :, c0:c1], func=AF.Square),
                 reads=xTb, writes=[sq_b])

        def norm_rest(st_bufs, c0, c1, grow, out_t, out_b, out_off):
            sq_t, sq_b, rs_t, rs_b = st_bufs
            n = c1 - c0
            k = next_bank()
            for c in range(NC8):
                P.op("pe", lambda e, c=c, k=k: e.matmul(psum[:, k, 0:n], lhsT=onesm[:], rhs=sq_t[:, c, 0:n],
                                                        start=(c == 0), stop=(c == NC8 - 1)),
                     reads=[sq_b, onesm_b], writes=[pb[k]], inc=(c == NC8 - 1))
            P.op("act", lambda e, k=k: e.activation(out=rs_t[:, 0:n], in_=psum[:, k, 0:n], func=AF.Ln,
                                                    bias=epst[:, 0:1], scale=1.0),
                 reads=[pb[k], epst_b], writes=[rs_b])
            P.op("act", lambda e: e.activation(out=rs_t[:, 0:n], in_=rs_t[:, 0:n], func=AF.Exp, scale=-0.5),
                 reads=[rs_b], writes=[rs_b])
            for c in range(NC8):
                P.op("dve", lambda e, c=c: e.scalar_tensor_tensor(
                    out=out_t[:, c, out_off:out_off + n], in0=xT[:, c, c0:c1], scalar=gains[:, grow, c:c + 1],
                    in1=rs_t[:, 0:n], op0=ALU.mult, op1=ALU.mult),
                    reads=[xTb[c], rs_b, gains_b], writes=[out_b])

        def ffn_phase(l, i, ca, cb, ngroups=3, nblk=2, pre_tiles=None):
            grow = G_N(l, 0 if i == 0 else 2)
            groups = [split(a, b, nblk) for (a, b) in split(ca, cb, ngroups)]
            gmax = max(g[-1][1] - g[0][0] for g in groups)
            bmax = max(b1 - b0 for g in groups for (b0, b1) in g)
            Wg = wg[l, i].rearrange("(c p) f -> p c f", p=128)
            Wu = wu[l, i].rearrange("(c p) f -> p c f", p=128)
            Wd = wd[l, i].rearrange("(j p) d -> p j d", p=128)
            with ExitStack() as st:
                xn_t, xn_b = sb(st, "xn", [128, NC8, gmax], BF16)
                h_t, h_b0 = sb(st, "h", [128, NJ, gmax], BF16)
                hb = [Buf(f"h{j}") for j in range(NJ)]
                NGU = 5
                gu = [sb(st, f"gu{k}", [128, 2, NC8, 128], BF16) for k in range(NGU)]
                wdr = [sb(st, f"wd{k}", [128, NJ, 256], BF16) for k in range(2)]
                sg = [sb(st, f"sg{k}", [128, bmax], F32) for k in range(4)]
                rs_t, rs_b = sb(st, "rs", [128, bmax], F32)
                nbs_ = []
                for t_ in range(nblk):
                    sq_t, sq_b = sb(st, f"sq{t_}", [128, NC8, bmax], BF16)
                    nbs_.append((sq_t, sq_b, rs_t, rs_b))
                sgi = [0]
                gu_loads = [(g, j) for g in range(ngroups) for j in range(NJ)]
                wd_loads = [(g, cp) for g in range(ngroups) for cp in range(4)]
                gu_next = [0]
                wd_next = [0]

                def issue_gu():
                    idx = gu_next[0]
                    if idx >= len(gu_loads):
                        return
                    gu_next[0] += 1
                    g, j = gu_loads[idx]
                    t, b = gu[idx % NGU]
                    P.dma("pool", f"gu{idx % NGU}", t[:, 0, :, :], Wg[:, :, j * 128:(j + 1) * 128], writes=[b])
                    P.dma("pool", f"gu{idx % NGU}", t[:, 1, :, :], Wu[:, :, j * 128:(j + 1) * 128], writes=[])
                    b.w = (f"gu{idx % NGU}", P.dcnt[f"gu{idx % NGU}"])

                def issue_wd():
                    idx = wd_next[0]
                    if idx >= len(wd_loads):
                        return
                    wd_next[0] += 1
                    g, cp = wd_loads[idx]
                    t, b = wdr[idx % 2]
                    P.dma("pool", f"wd{idx % 2}", t[:, :, :], Wd[:, :, cp * 256:(cp + 1) * 256], writes=[b])

                for _ in range(NGU):
                    issue_gu()

                def do_norm_sq(g):
                    for t_, (b0, b1) in enumerate(groups[g]):
                        norm_sq(nbs_[t_], b0, b1)

                def do_norm_rest(g):
                    off0 = groups[g][0][0]
                    for t_, (b0, b1) in enumerate(groups[g]):
                        norm_rest(nbs_[t_], b0, b1, grow, xn_t, xn_b, b0 - off0)

                pend_tiles = []
                if pre_tiles is not None:
                    g0_end = groups[0][-1][1]
                    for (r0_, em) in pre_tiles(st):
                        if r0_ < g0_end:
                            em()
                        else:
                            pend_tiles.append(em)
                do_norm_sq(0)
                do_norm_rest(0)
                gu_idx = 0
                wd_idx = 0
                for g in range(ngroups):
                    off0 = groups[g][0][0]
                    blks = groups[g]
                    for j in range(NJ):
                        gt, gb = gu[gu_idx % NGU]
                        gu_idx += 1
                        base = (j % 2) * 4
                        for which in range(2):
                            for t, (b0, b1) in enumerate(blks):
                                k = base + which * 2 + t
                                n = b1 - b0
                                for c in range(NC8):
                                    P.op("pe", lambda e, k=k, n=n, gt=gt, which=which, c=c, b0=b0, b1=b1, off0=off0: e.matmul(
                                        psum[:, k, 0:n], lhsT=gt[:, which, c, :], rhs=xn_t[:, c, b0 - off0:b1 - off0],
                                        start=(c == 0), stop=(c == NC8 - 1)),
                                        reads=[gb, xn_b], writes=[pb[k]], inc=(c == NC8 - 1))
                        issue_gu()
                        if g == 0 and j == 5:
                            issue_wd()
                            issue_wd()
                        for t, (b0, b1) in enumerate(blks):
                            n = b1 - b0
                            kg = base + t
                            ku = base + 2 + t
                            s_t, s_b = sg[sgi[0] % 4]
                            sgi[0] += 1
                            P.op("act", lambda e, s_t=s_t, kg=kg, n=n: e.activation(out=s_t[:, 0:n], in_=psum[:, kg, 0:n], func=AF.Silu),
                                 reads=[pb[kg]], writes=[s_b])
                            P.op("dve", lambda e, s_t=s_t, ku=ku, n=n, j=j, b0=b0, b1=b1, off0=off0: e.tensor_tensor(
                                out=h_t[:, j, b0 - off0:b1 - off0], in0=psum[:, ku, 0:n], in1=s_t[:, 0:n], op=ALU.mult),
                                reads=[pb[ku], s_b], writes=[hb[j]])
                        if pend_tiles:
                            pend_tiles.pop(0)()
                    while pend_tiles:
                        pend_tiles.pop(0)()
                    if g + 1 < ngroups:
                        do_norm_sq(g + 1)
                    for cp in range(4):
                        wt, wb = wdr[wd_idx % 2]
                        wd_idx += 1
                        for cc in range(2):
                            c = cp * 2 + cc
                            for t, (b0, b1) in enumerate(blks):
                                n = b1 - b0
                                k = next_bank()
                                for j in range(NJ):
                                    P.op("pe", lambda e, k=k, n=n, wt=wt, j=j, cc=cc, b0=b0, b1=b1, off0=off0: e.matmul(
                                        psum[:, k, 0:n], lhsT=wt[:, j, cc * 128:(cc + 1) * 128], rhs=h_t[:, j, b0 - off0:b1 - off0],
                                        start=(j == 0), stop=(j == NJ - 1)),
                                        reads=[wb, hb[j]], writes=[pb[k]], inc=(j == NJ - 1))
                                P.op("dve", lambda e, k=k, n=n, c=c, b0=b0, b1=b1: e.scalar_tensor_tensor(
                                    out=xT[:, c, b0:b1], in0=psum[:, k, 0:n], scalar=0.5, in1=xT[:, c, b0:b1],
                                    op0=ALU.mult, op1=ALU.add),
                                    reads=[pb[k], xTb[c]], writes=[xTb[c]])
                        issue_wd()
                        if cp == 0 and g + 1 < ngroups:
                            do_norm_rest(g + 1)
                P.flush()

        if stages >= 1:
            ffn_phase(0, 0, 0, NCOL, pre_tiles=p0_tiles)

        def pool_phase():
            grow = G_N(0, 1)
            blocks = split(15, NCOL, 7)
            PW = 336
            with ExitStack() as st:
                pwt, pw_b = sb(st, "pw", [128, 4, 2, 256], BF16)
                P.dma("pool", "pw", pwt[:].rearrange("p g c d -> p (g c) d"),
                      pw.rearrange("g (c p) d -> p (g c) d", p=128), writes=[pw_b])
                pc_t, pc_b = sb(st, "pc", [128, 4, 15], F32)
                P.dma("sp", "pc", pc_t[:].rearrange("p g t -> p (g t)"), pcorr_d[:, :], writes=[pc_b])
                hbs = [sb(st, f"hb{k}", [128, NC8, PW], F32) for k in range(2)]
                tsets = []
                for k_ in range(2):
                    tA_, _ = sb(st, f"tA{k_}", [128, NC8, PW], F32)
                    tB_, _ = sb(st, f"tB{k_}", [128, NC8, PW], F32)
                    db_, _ = sb(st, f"db{k_}", [128, NC8, PW], BF16)
                    tsets.append((tA_, [Buf(f"tA{k_}_{c}") for c in range(NC8)], tB_, [Buf(f"tB{k_}_{c}") for c in range(NC8)],
                                  db_, [Buf(f"db{k_}_{c}") for c in range(NC8)]))
                pwn, pwn_b = sb(st, "pwn", [128, 4, 2, 256], BF16)
                P.op("dve", lambda e: e.tensor_scalar(out=pwn[:].rearrange("p g c d -> p (g c d)"),
                                                      in0=pwt[:].rearrange("p g c d -> p (g c d)"),
                                                      scalar1=-1.0, scalar2=None, op0=ALU.mult), reads=[pw_b], writes=[pwn_b])
                hbfs = [sb(st, f"hbf{k}", [128, NC8, PW], BF16) for k in range(2)]
                sq_t, sq_b = sb(st, "sqp", [128, NC8, PW], BF16)
                rs_t, rs_b = sb(st, "rsp", [128, PW], F32)
                hist_t, hist_b = sb(st, "hist", [128, NC8, NS * 15], F32)
                red_t, red_b = sb(st, "red", [128, NC8, NS], F32)
                po_t, po_b = sb(st, "po", [31, D], F32)
                sst = [sb(st, f"sst{k}", [128, D], F32) for k in range(2)]
                P.dma("sp", "pools", pools_o[:, 0:14, :], state_d.rearrange("(b r) d -> b r d", r=15)[:, 1:15, :])
                for r, (r0, rows) in enumerate(((0, 128), (128, NS * 15 - 128))):
                    s_t, s_b = sst[r]
                    P.dma("sp", f"sst{r}", s_t[0:rows, :], state_d[r0:r0 + rows, :], writes=[s_b])
                    for hf in range(2):
                        k = next_bank()
                        for cc in range(4):
                            c = hf * 4 + cc
                            P.op("pe", lambda e, k=k, cc=cc, c=c, s_t=s_t, rows=rows: e.transpose(
                                out=psum[:, k, cc * 128:cc * 128 + rows], in_=s_t[0:rows, c * 128:(c + 1) * 128],
                                identity=ident[0:rows, 0:rows]),
                                reads=[s_b, ident_b], writes=[pb[k]], inc=(cc == 3))
                        src = psum[:, k, :].rearrange("p (c n) -> p c n", c=4)[:, :, 0:rows]
                        dst = hist_t[:, hf * 4:hf * 4 + 4, r0:r0 + rows]
                        P.op("act", lambda e, dst=dst, src=src: e.copy(out=dst, in_=src), reads=[pb[k]], writes=[hist_b])
                for c in range(NC8):
                    w = 2 << (c // 2)
                    P.op("dve", lambda e, c=c, w=w: e.tensor_reduce(
                        out=red_t[:, c, :], in_=hist_t[:, c, :].rearrange("p (b r) -> p b r", r=15)[:, :, 16 - w:15],
                        axis=AX.X, op=ALU.add), reads=[hist_b], writes=[red_b])
                state = {"prev": None}

                def stage_A(bi):
                    s0, e0 = blocks[bi]
                    hb_t, hb_b = hbs[bi % 2]
                    tA, tAb, tB, tBb, db_t, dbb = tsets[bi % 2]
                    if bi == 0:
                        a = 0
                        n = e0
                        norm_block((sq_t, sq_b, rs_t, rs_b), 0, e0, grow, hb_t, hb_b, 0)
                    else:
                        a = s0 - 15
                        n = e0 - a
                        norm_block((sq_t, sq_b, rs_t, rs_b), s0, e0, grow, hb_t, hb_b, 15)
                        p_t, p_b, pn = state["prev"]
                        P.op("act", lambda e, hb_t=hb_t, p_t=p_t, pn=pn: e.copy(out=hb_t[:, :, 0:15], in_=p_t[:, :, pn - 15:pn]),
                             reads=[p_b], writes=[hb_b])
                    state["prev"] = (hb_t, hb_b, n)
                    hbf_t, hbf_b = hbfs[bi % 2]
                    if POOLPE:
                        P.op("act", lambda e, hbf_t=hbf_t, hb_t=hb_t, n=n: e.copy(out=hbf_t[:, :, 15:n], in_=hb_t[:, :, 15:n]),
                             reads=[hb_b], writes=[hbf_b])
                    for step in range(4):
                        sh = 1 << step
                        for c in range(2 * step, NC8):
                            if step == 0:
                                src_t, src_b = hb_t, hb_b
                            else:
                                src_t, src_b = (tA, tAb[c]) if step % 2 == 1 else (tB, tBb[c])
                            dst_t, dst_b = (tA, tAb[c]) if step % 2 == 0 else (tB, tBb[c])
                            lo = 2 * sh - 1
                            P.op("pool" if (c >= POOLC) else "dve", lambda e, dst_t=dst_t, src_t=src_t, c=c, lo=lo, sh=sh, n=n: e.tensor_tensor(
                                out=dst_t[:, c, lo:n], in0=src_t[:, c, lo:n], in1=src_t[:, c, lo - sh:n - sh], op=ALU.add),
                                reads=[src_b], writes=[dst_b])
                    for c in range(NC8):
                        g = c // 2
                        w = 2 << g
                        S_t, S_b = (tA, tAb[c]) if g % 2 == 0 else (tB, tBb[c])
                        if a <= CM0 and CM0 + 15 <= e0:
                            u0 = CM0 - a
                            P.op("dve", lambda e, S_t=S_t, c=c, u0=u0, g=g: e.tensor_tensor(
                                out=S_t[:, c, u0:u0 + 15], in0=S_t[:, c, u0:u0 + 15], in1=pc_t[:, g, :], op=ALU.mult),
                                reads=[S_b, pc_b], writes=[S_b])
                        if POOLPE:
                            P.op("act", lambda e, S_t=S_t, c=c, w=w, n=n, db_t=db_t: e.activation(
                                out=db_t[:, c, 15:n], in_=S_t[:, c, 15:n], func=AF.Copy, scale=1.0 / w),
                                reads=[S_b], writes=[dbb[c]])
                        else:
                            P.op("dve", lambda e, S_t=S_t, c=c, w=w, n=n, hb_t=hb_t, db_t=db_t: e.scalar_tensor_tensor(
                                out=db_t[:, c, 15:n], in0=S_t[:, c, 15:n], scalar=1.0 / w, in1=hb_t[:, c, 15:n],
                                op0=ALU.mult, op1=ALU.subtract), reads=[S_b, hb_b], writes=[dbb[c]])
                        if e0 == NCOL:
                            us = CS0 - a
                            P.op("dve", lambda e, c=c, us=us, hb_t=hb_t: e.tensor_tensor(
                                out=red_t[:, c, :], in0=red_t[:, c, :], in1=hb_t[:, c, us:us + NS], op=ALU.add),
                                reads=[red_b, hb_b], writes=[red_b])
                            if POOLPE:
                                P.op("dve", lambda e, c=c, us=us, w=w, db_t=db_t: e.tensor_scalar(
                                    out=db_t[:, c, us:us + NS], in0=red_t[:, c, :], scalar1=1.0 / w, scalar2=None, op0=ALU.mult),
                                    reads=[red_b], writes=[dbb[c]])
                            else:
                                P.op("dve", lambda e, c=c, us=us, w=w, hb_t=hb_t, db_t=db_t: e.scalar_tensor_tensor(
                                    out=db_t[:, c, us:us + NS], in0=red_t[:, c, :], scalar=1.0 / w, in1=hb_t[:, c, us:us + NS],
                                    op0=ALU.mult, op1=ALU.subtract), reads=[red_b, hb_b], writes=[dbb[c]])
                    return (hb_t, hb_b, n)

                def stage_B(bi, hb_t, hb_b, n):
                    s0, e0 = blocks[bi]
                    tA, tAb, tB, tBb, db_t, dbb = tsets[bi % 2]
                    m = n - 15
                    for g in range(4):
                        for dc in range(2):
                            c2 = 2 * g + dc
                            k = next_bank()
                            hbf_t, hbf_b = hbfs[bi % 2]
                            for cc in range(2):
                                P.op("pe", lambda e, k=k, m=m, g=g, cc=cc, dc=dc, n=n, db_t=db_t: e.matmul(
                                    psum[:, k, 0:m], lhsT=pwt[:, g, cc, dc * 128:(dc + 1) * 128], rhs=db_t[:, 2 * g + cc, 15:n],
                                    start=(cc == 0), stop=(cc == 1 and not POOLPE)),
                                    reads=[pw_b, dbb[2 * g + cc]], writes=[pb[k]], inc=(cc == 1 and not POOLPE))
                            for cc in range(2 if POOLPE else 0):
                                P.op("pe", lambda e, k=k, m=m, g=g, cc=cc, dc=dc, n=n, hbf_t=hbf_t: e.matmul(
                                    psum[:, k, 0:m], lhsT=pwn[:, g, cc, dc * 128:(dc + 1) * 128], rhs=hbf_t[:, 2 * g + cc, 15:n],
                                    start=False, stop=(cc == 1)),
                                    reads=[pwn_b, hbf_b], writes=[pb[k]], inc=(cc == 1))
                            P.op("dve", lambda e, k=k, m=m, c2=c2, s0=s0, e0=e0: e.scalar_tensor_tensor(
                                out=xT[:, c2, s0:e0], in0=psum[:, k, 0:m], scalar=gains[:, G_PS, c2:c2 + 1], in1=xT[:, c2, s0:e0],
                                op0=ALU.mult, op1=ALU.add), reads=[pb[k], xTb[c2], gains_b], writes=[xTb[c2]])
                    if e0 == NCOL:
                        u31 = n - 31
                        for hf in range(2):
                            k = next_bank()
                            for cc in range(4):
                                c = hf * 4 + cc
                                P.op("pe", lambda e, k=k, cc=cc, c=c, hb_t=hb_t, u31=u31: e.transpose(
                                    out=psum[0:31, k, cc * 128:(cc + 1) * 128], in_=hb_t[:, c, u31:u31 + 31], identity=ident[:]),
                                    reads=[hb_b, ident_b], writes=[pb[k]], inc=(cc == 3))
                            P.op("act", lambda e, k=k, hf=hf: e.copy(out=po_t[0:31, hf * 512:(hf + 1) * 512], in_=psum[0:31, k, :]),
                                 reads=[pb[k]], writes=[po_b])
                        P.dma("sp", "po", poolo_o[:, :], po_t[:, :], reads=[po_b])
                        P.dma("sp", "po", pools_o[:, 14, :], po_t[15:31, :], reads=[po_b])

                infoA = {0: stage_A(0)}
                for bi in range(len(blocks)):
                    if bi + 1 < len(blocks):
                        infoA[bi + 1] = stage_A(bi + 1)
                    stage_B(bi, *infoA[bi])
                P.flush()

        kwsd_b = Buf("kws_dram")
        vwsd_b = Buf("vws_dram")

        def kv_phase(KT, KT_b, V_t, V_b):
            blocks = [(15, 527), (527, 1039), (1039, 1551), (1551, 2063), (2063, NCOL)]
            with ExitStack() as st:
                wkv_t, wkv_b = sb(st, "wkv", [128, NC8, 512], BF16)
                P.dma("pool", "wkv", wkv_t[:], wkv.rearrange("(c p) f -> p c f", p=128), writes=[wkv_b])
                rm_t, rm_b = sb(st, "rmat", [128, 128], F32)
                P.dma("sp", "rmat", rm_t[:], rmat_d[:, :], writes=[rm_b])
                rtm_t, rtm_b = sb(st, "rtm", [128, 64], F32)
                rts_t, rts_b = sb(st, "rts", [NS, 64], F32)
                P.dma("sp", "rtm", rtm_t[:], ropet_d[0:128].rearrange("t a b -> t (a b)"), writes=[rtm_b])
                P.dma("sp", "rts", rts_t[:], ropet_d[128:144].rearrange("t a b -> t (a b)"), writes=[rts_b])
                P.dma("sp", "kws", kws_o[:, 0:127, :], ck_d[:, 1:128, :], writes=[kwsd_b])
                P.dma("sp", "vws", vws_o[:, 0:127, :], cv_d[:, 1:128, :], writes=[vwsd_b])
                kns = [sb(st, f"kn{k}", [128, NC8, 512], BF16) for k in range(2)]
                sq_t, sq_b = sb(st, "sqk", [128, NC8, 512], BF16)
                rs_t, rs_b = sb(st, "rsk", [128, 512], F32)
                rts2 = [sb(st, f"rt{k}", [128, 2, 512], F32) for k in range(2)]
                kfs = [sb(st, f"kf{k}", [128, 512], F32) for k in range(2)]
                t1s = [sb(st, f"t1{k}", [128, 512], F32) for k in range(2)]
                t2s = [sb(st, f"t2{k}", [128, 512], F32) for k in range(2)]
                ko_t, ko_b = sb(st, "ko", [128, 256], F32)
                vo_t, vo_b = sb(st, "vo", [128, 256], F32)
                kso_t, kso_b = sb(st, "kso", [NS, 256], F32)
                vso_t, vso_b = sb(st, "vso", [NS, 256], F32)
                tm = [sb(st, f"tm{k}", [128, 32], F32) for k in range(4)]
                krm_t, krm_b = sb(st, "krm", [128, 256], F32)
                krs_t, krs_b = sb(st, "krs", [NS, 256], F32)
                ri = 0
                nbk = (sq_t, sq_b, rs_t, rs_b)
                norm_block(nbk, blocks[0][0], blocks[0][1], G_KV, kns[0][0], kns[0][1], 0)
                for bi, (b0, b1) in enumerate(blocks):
                    n = b1 - b0
                    kn_t, kn_b = kns[bi % 2]
                    if bi + 1 < len(blocks):
                        norm_sq(nbk, blocks[bi + 1][0], blocks[bi + 1][1])
                    rt_t, rt_b = rts2[bi % 2]
                    P.dma("sp", f"rt{bi % 2}", rt_t[:, :, 0:n], ropef_d[:, :, b0:b1], writes=[rt_b])
                    for kc in range(2):
                        kf_t, kf_b = kfs[kc]
                        k = next_bank()
                        for c in range(NC8):
                            P.op("pe", lambda e, k=k, n=n, c=c, kc=kc, kn_t=kn_t: e.matmul(
                                psum[:, k, 0:n], lhsT=wkv_t[:, c, kc * 128:(kc + 1) * 128], rhs=kn_t[:, c, 0:n],
                                start=(c == 0), stop=(c == NC8 - 1)),
                                reads=[wkv_b, kn_b], writes=[pb[k]], inc=(c == NC8 - 1))
                        P.op("act", lambda e, k=k, n=n, kf_t=kf_t: e.copy(out=kf_t[:, 0:n], in_=psum[:, k, 0:n]),
                             reads=[pb[k]], writes=[kf_b])

                    def k_stage2(kc, n=n, b0=b0, b1=b1, rt_t=rt_t, rt_b=rt_b):
                        kf_t, kf_b = kfs[kc]
                        t1_t, t1_b = t1s[kc]
                        t2_t, t2_b = t2s[kc]
                        k2 = next_bank()
                        P.op("pe", lambda e, k2=k2, n=n, kf_t=kf_t: e.matmul(psum[:, k2, 0:n], lhsT=rm_t[:, :], rhs=kf_t[:, 0:n],
                                                                            start=True, stop=True),
                             reads=[rm_b, kf_b], writes=[pb[k2]])
                        P.op("dve", lambda e, n=n, kf_t=kf_t, t1_t=t1_t, rt_t=rt_t: e.tensor_tensor(
                            out=t1_t[:, 0:n], in0=kf_t[:, 0:n], in1=rt_t[:, 0, 0:n], op=ALU.mult),
                            reads=[kf_b, rt_b], writes=[t1_b])
                        P.op("dve", lambda e, n=n, k2=k2, t2_t=t2_t, rt_t=rt_t: e.tensor_tensor(
                            out=t2_t[:, 0:n], in0=psum[:, k2, 0:n], in1=rt_t[:, 1, 0:n], op=ALU.mult),
                            reads=[pb[k2], rt_b], writes=[t2_b])
                        P.op("dve", lambda e, n=n, t1_t=t1_t, t2_t=t2_t, kc=kc, b0=b0, b1=b1: e.tensor_tensor(
                            out=KT[:, kc, b0:b1], in0=t1_t[:, 0:n], in1=t2_t[:, 0:n], op=ALU.add),
                            reads=[t1_b, t2_b], writes=[KT_b])

                    k_pending = [lambda: k_stage2(0), lambda: k_stage2(1)]
                    assert (b1 - b0 + 127) // 128 >= 2
                    if bi + 1 < len(blocks):
                        norm_rest(nbk, blocks[bi + 1][0], blocks[bi + 1][1], G_KV, kns[(bi + 1) % 2][0], kns[(bi + 1) % 2][1], 0)
                    for t0 in range(b0, b1, 128):
                        if k_pending:
                            k_pending.pop(0)()
                        m = min(128, b1 - t0)
                        vb = (t0 - 15) // 128
                        k = next_bank()
                        for c in range(NC8):
                            P.op("pe", lambda e, k=k, m=m, c=c, t0=t0, b0=b0, kn_t=kn_t: e.matmul(
                                psum[0:m, k, 0:256], lhsT=kn_t[:, c, t0 - b0:t0 - b0 + m], rhs=wkv_t[:, c, 256:512],
                                start=(c == 0), stop=(c == NC8 - 1)),
                                reads=[wkv_b, kn_b], writes=[pb[k]], inc=(c == NC8 - 1))
                        P.op("act", lambda e, k=k, m=m, vb=vb: e.copy(out=V_t[0:m, vb, :, 0:64], in_=psum[0:m, k, 0:256].rearrange("p (h d) -> p h d", h=4)),
                             reads=[pb[k]], writes=[V_b])
                        if t0 >= 2063 and KVOPT >= 2:
                            is_s = (m == NS)
                            o_t, o_b = (vso_t, vso_b) if is_s else (vo_t, vo_b)
                            P.op("dve", lambda e, k=k, m=m, o_t=o_t: e.tensor_copy(out=o_t[0:m, :], in_=psum[0:m, k, 0:256]),
                                 reads=[pb[k]], writes=[o_b])
                            if is_s:
                                P.dma("sp", "vws", vws_o[:, 127, :], o_t[0:m, :], reads=[o_b], writes=[vwsd_b])
                            else:
                                P.dma("sp", "vo", vwp_o[:, :], o_t[0:m, :], reads=[o_b])
                            if KVOPT < 3:
                                continue
                            k = next_bank()
                            for c in range(NC8):
                                P.op("pe", lambda e, k=k, m=m, c=c, t0=t0, b0=b0, kn_t=kn_t: e.matmul(
                                    psum[0:m, k, 0:256], lhsT=kn_t[:, c, t0 - b0:t0 - b0 + m], rhs=wkv_t[:, c, 0:256],
                                    start=(c == 0), stop=(c == NC8 - 1)),
                                    reads=[wkv_b, kn_b], writes=[pb[k]], inc=(c == NC8 - 1))
                            o_t, o_b = (kso_t, kso_b) if is_s else (ko_t, ko_b)
                            tb_t, tb_b = (rts_t, rts_b) if is_s else (rtm_t, rtm_b)
                            P.op("act", lambda e, k=k, m=m, o_t=o_t: e.copy(out=o_t[0:m, :], in_=psum[0:m, k, 0:256]),
                                 reads=[pb[k]], writes=[o_b])
                            kr_t, kr_b = (krs_t, krs_b) if is_s else (krm_t, krm_b)
                            P.op("dve", lambda e, k=k, m=m, kr_t=kr_t: e.tensor_copy(out=kr_t[0:m, :], in_=psum[0:m, k, 0:256]),
                                 reads=[pb[k]], writes=[kr_b])
                            pv = kr_t[0:m, :].rearrange("p (h d) -> p h d", h=4)
                            ov = o_t[0:m, :].rearrange("p (h d) -> p h d", h=4)
                            cosv = tb_t[0:m, 0:32].rearrange("p (h d) -> p h d", h=4)
                            sinv = tb_t[0:m, 32:64].rearrange("p (h d) -> p h d", h=4)
                            tmv = [t[0][0:m, :].rearrange("p (h d) -> p h d", h=4) for t in tm]
                            x1 = pv[:, :, 0:8]
                            x2 = pv[:, :, 8:16]
                            for (ti, xa, tab) in ((0, x1, cosv), (1, x2, sinv), (2, x2, cosv), (3, x1, sinv)) if KVOPT >= 4 else ():
                                P.op("dve", lambda e, ti=ti, xa=xa, tab=tab, tmv=tmv: e.tensor_tensor(
                                    out=tmv[ti], in0=xa, in1=tab, op=ALU.mult),
                                    reads=[kr_b, tb_b], writes=[tm[ti][1]])
                            if KVOPT >= 4:
                                P.op("dve", lambda e, ov=ov, tmv=tmv: e.tensor_tensor(out=ov[:, :, 0:8], in0=tmv[0], in1=tmv[1], op=ALU.subtract),
                                     reads=[tm[0][1], tm[1][1], o_b], writes=[o_b])
                                P.op("dve", lambda e, ov=ov, tmv=tmv: e.tensor_tensor(out=ov[:, :, 8:16], in0=tmv[2], in1=tmv[3], op=ALU.add),
                                     reads=[tm[2][1], tm[3][1], o_b], writes=[o_b])
                            if is_s:
                                P.dma("sp", "kws", kws_o[:, 127, :], o_t[0:m, :], reads=[o_b], writes=[kwsd_b])
                            else:
                                P.dma("sp", "ko", kwp_o[:, :], o_t[0:m, :], reads=[o_b])
                P.flush()

        if stages >= 2:
            pool_phase()
        if stages >= 3:
            ffn_phase(0, 1, 0, NCOL)
        kvst = ExitStack()
        top.enter_context(kvst)
        if stages >= 4:
            KT, KT_b = sb(kvst, "KT", [128, 2, NCOL], BF16)
            V_t, V_b = sb(kvst, "V", [128, 18, 4, 65], BF16)
            P.op("dve", lambda e: e.memset(V_t[:, :, :, 64:65], 1.0), writes=[V_b])
            kv_phase(KT, KT_b, V_t, V_b)


        def attn_phase(KT, KT_b, V_t, V_b):
            grow = G_N(1, 1)
            with ExitStack() as st:
                wq_t, wq_b = sb(st, "wq", [128, NC8, D], BF16)
                Wq = wq.rearrange("(c p) f -> p c f", p=128)
                wq_bs = [Buf(f"wq{qc}") for qc in range(8)]
                for qc in range(8):
                    for half in range(2):
                        head = QA[qc] if half == 0 else QB[qc]
                        P.dma("pool", f"wq{qc}", wq_t[:, :, qc * 128 + half * 64:qc * 128 + half * 64 + 64],
                              Wq[:, :, head * 64:(head + 1) * 64], writes=[])
                    wq_bs[qc].w = (f"wq{qc}", P.dcnt[f"wq{qc}"])
                rm_t, rm_b = sb(st, "rmat2", [128, 128], F32)
                P.dma("sp", "rmat2", rm_t[:], rmat_d[:, :], writes=[rm_b])
                mk_t, mk_b = sb(st, "mk", [128, 3, 128], F32)
                P.dma("sp", "mk", mk_t[:].rearrange("p a b -> p (a b)"), masks_d[:, :], writes=[mk_b])
                sk_t, sk_b = sb(st, "sinks", [128, 16], F32)
                P.dma("sp", "sinks", sk_t[:], sinks_d[:, :], writes=[sk_b])
                es_t, es_b = sb(st, "es", [128, 16], F32)
                P.op("act", lambda e: e.activation(out=es_t[:], in_=sk_t[:], func=AF.Exp), reads=[sk_b], writes=[es_b])
                identb, identb_b = sb(st, "identb", [128, 128], BF16)
                mb4, mb4_b = sb(st, "mb4", [128, 3, 4, 128], BF16)

                def build_masks():
                    P.op("dve", lambda e: e.tensor_copy(out=identb[:], in_=ident[:]), reads=[ident_b], writes=[identb_b])
                    for mi_ in range(3):
                        P.op("dve", lambda e, mi_=mi_: e.tensor_scalar(
                            out=mb4[:, mi_, :, :], in0=mk_t[:, mi_, :].unsqueeze(1).broadcast_to([128, 4, 128]),
                            scalar1=-1.0, scalar2=30000.0, op0=ALU.add, op1=ALU.mult), reads=[mk_b], writes=[mb4_b])
                hq_t, hq_b = sb(st, "hq", [128, NC8, 512], BF16)
                sq_t, sq_b = sb(st, "sqa", [128, NC8, 512], BF16)
                rs_t, rs_b = sb(st, "rsa", [128, 512], F32)
                rt_t, rt_b = sb(st, "art0", [128, 2, 512], F32)
                qfs = [sb(st, f"qf{k}", [128, 512], F32) for k in range(2)]
                t1s = [sb(st, f"at1{k}", [128, 512], F32) for k in range(1)]
                t2s = [sb(st, f"at2{k}", [128, 512], F32) for k in range(1)]
                QZ = [sb(st, f"QZ{k}", [128, 8, 512], BF16) for k in range(2)]
                for hfz in range(2):
                    P.op("pool", lambda e, hfz=hfz: e.memset(QZ[hfz][0][:], 0.0), writes=[QZ[hfz][1]])
                cnt = {"ri": 0, "et": 0, "pt": 0, "ot": 0}

                def q_norm(c0, c1):
                    n = c1 - c0
                    norm_block((sq_t, sq_b, rs_t, rs_b), c0, c1, grow, hq_t, hq_b, 0)
                    P.dma("sp", "art0", rt_t[:, :, 0:n], ropef_d[:, :, c0:c1], writes=[rt_b])

                def q_project(c0, c1):
                    q_norm(c0, c1)
                    q_stages(c0, c1)

                def q_stages(c0, c1):
                    n = c1 - c0

                    def stage1(qc):
                        qf_t, qf_b = qfs[qc % 2]
                        k = next_bank()
                        kc2, hh_ = qc // 4, qc % 4
                        for c in range(NC8):
                            P.op("pe", lambda e, k=k, c=c, kc2=kc2, hh_=hh_: e.matmul(
                                psum[:, k, 0:n],
                                lhsT=wq_t[:, c, (kc2 * 4 + hh_) * 128:(kc2 * 4 + hh_ + 1) * 128],
                                rhs=hq_t[:, c, 0:n], start=(c == 0), stop=(c == NC8 - 1)),
                                reads=[wq_bs[qc], hq_b], writes=[pb[k]], inc=(c == NC8 - 1))
                        P.op("act", lambda e, k=k, qf_t=qf_t: e.copy(out=qf_t[:, 0:n], in_=psum[:, k, 0:n]),
                             reads=[pb[k]], writes=[qf_b])

                    def stage2(qc):
                        qf_t, qf_b = qfs[qc % 2]
                        t1_t, t1_b = t1s[0]
                        t2_t, t2_b = t2s[0]
                        k2 = next_bank()
                        P.op("pe", lambda e, k2=k2, qf_t=qf_t: e.matmul(psum[:, k2, 0:n], lhsT=rm_t[:, :], rhs=qf_t[:, 0:n],
                                                                      start=True, stop=True),
                             reads=[rm_b, qf_b], writes=[pb[k2]])
                        P.op("pool", lambda e, qf_t=qf_t, t1_t=t1_t: e.tensor_tensor(
                            out=t1_t[:, 0:n], in0=qf_t[:, 0:n], in1=rt_t[:, 0, 0:n], op=ALU.mult),
                            reads=[qf_b, rt_b], writes=[t1_b])
                        P.op("dve", lambda e, k2=k2, t2_t=t2_t: e.tensor_tensor(
                            out=t2_t[:, 0:n], in0=psum[:, k2, 0:n], in1=rt_t[:, 1, 0:n], op=ALU.mult),
                            reads=[pb[k2], rt_b], writes=[t2_b])
                        for hfz in range(2):
                            r0_, r1_ = 64 * hfz, 64 * hfz + 64
                            P.op("dve", lambda e, t1_t=t1_t, t2_t=t2_t, qc=qc, hfz=hfz, r0_=r0_, r1_=r1_: e.tensor_tensor(
                                out=QZ[hfz][0][r0_:r1_, qc, 0:n], in0=t1_t[r0_:r1_, 0:n], in1=t2_t[r0_:r1_, 0:n], op=ALU.add),
                                reads=[t1_b, t2_b], writes=[QZ[hfz][1]])

                    for qc in range(8):
                        stage1(qc)
                        if qc >= 1:
                            stage2(qc - 1)
                    stage2(7)

                st2 = ExitStack()
                st.enter_context(st2)
                wo_t, wo_b = sb(st2, "wo", [128, NC8, D], BF16)
                P.dma("pool", "wo", wo_t[:], wo.rearrange("(c p) f -> p c f", p=128), writes=[wo_b])
                OTs = [sb(st2, f"OT{k}", [128, 4, 8, 128], BF16) for k in range(2)]
                pts = [sb(st2, f"pt{k}", [128, 512], BF16) for k in range(6)]
                ots = [sb(st2, f"otm{k}", [128, D], F32) for k in range(2)]
                dn_t, dn_b = sb(st2, "dn", [128, 16], F32)

                def o_project(c0, c1, w_t, w_b, rhs_of, OT_b):
                    for ch in o_project_chunks(c0, c1, w_t, w_b, rhs_of, OT_b):
                        ch()

                def o_project_chunks(c0, c1, w_t, w_b, rhs_of, OT_b):
                    return [(lambda dc=dc: o_project_dc(c0, c1, w_t, w_b, rhs_of, OT_b, dc)) for dc in range(NC8)]

                def o_project_dc(c0, c1, w_t, w_b, rhs_of, OT_b, dc):
                    n = c1 - c0
                    if True:
                        k = nbt()
                        for qc in range(8):
                            P.op("pe", lambda e, k=k, qc=qc, dc=dc: e.matmul(
                                psum[:, k, 0:n], lhsT=w_t[:, qc, dc * 128:(dc + 1) * 128], rhs=rhs_of(qc),
                                start=(qc == 0), stop=(qc == 7)),
                                reads=[w_b, OT_b], writes=[pb[k]], inc=(qc == 7))
                        P.op("dve", lambda e, k=k, dc=dc: e.tensor_tensor(
                            out=xT[:, dc, c0:c1], in0=psum[:, k, 0:n], in1=xT[:, dc, c0:c1], op=ALU.add),
                            reads=[pb[k], xTb[dc]], writes=[xTb[dc]])

                sbank = [0]
                vbank = [0]
                tbank = [0]

                def nbs():
                    sbank[0] = (sbank[0] + 1) % 4
                    return sbank[0]

                def nbv():
                    vbank[0] = (vbank[0] + 1) % 2
                    return 4 + vbank[0]

                def nbt():
                    tbank[0] = (tbank[0] + 1) % 2
                    return 6 + tbank[0]

                def emit_S(u):
                    sbi, i, kv = u
                    nb = sbi * 4 + i
                    qcol0 = CM0 + 512 * sbi + 128 * i
                    ql = 128 * i
                    kc, half = kv // 2, kv % 2
                    ptk = []
                    for kb in range(2):
                        kcol0 = qcol0 - 128 + 128 * kb
                        mi = (2 if nb == 0 else 0) if kb == 0 else 1
                        ks = nbs()
                        P.op("pe", lambda e, ks=ks, kc=kc, half=half, kcol0=kcol0, ql=ql: e.matmul(
                            psum[:, ks, :], lhsT=KT[:, kc, kcol0:kcol0 + 128],
                            rhs=QZ[half][0][:, 4 * kc:4 * kc + 4, ql:ql + 128], start=True, stop=False),
                            reads=[KT_b, QZ[half][1]], writes=[pb[ks]], inc=False)
                        P.op("pe", lambda e, ks=ks, mi=mi: e.matmul(
                            psum[:, ks, :], lhsT=identb[:, :], rhs=mb4[:, mi, :, :].rearrange("p h q -> p (h q)"),
                            start=False, stop=True),
                            reads=[identb_b, mb4_b], writes=[pb[ks]])
                        p_t, p_b = pts[cnt["pt"] % 6]
                        cnt["pt"] += 1
                        P.op("act", lambda e, ks=ks, p_t=p_t: e.activation(out=p_t[:, :], in_=psum[:, ks, :], func=AF.Exp, scale=0.125),
                             reads=[pb[ks]], writes=[p_b])
                        ptk.append((p_t, p_b))
                    return ptk

                def emit_V(u, ptk, o_t, o_b):
                    sbi, i, kv = u
                    OT_t, OT_b = OTs[sbi % 2]
                    nb = sbi * 4 + i
                    ko = nbv()
                    for hh in range(4):
                        for kb in range(2):
                            p_t, p_b = ptk[kb]
                            vb = nb + kb
                            P.op("pe", lambda e, ko=ko, hh=hh, kb=kb, p_t=p_t, vb=vb, kv=kv: e.matmul(
                                psum[:, ko, hh * 65:(hh + 1) * 65], lhsT=p_t[:, hh * 128:(hh + 1) * 128], rhs=V_t[:, vb, kv, :],
                                start=(kb == 0), stop=(kb == 1)),
                                reads=[p_b, V_b], writes=[pb[ko]], inc=(hh == 3 and kb == 1))
                    P.op("dve", lambda e, ko=ko, kv=kv: e.tensor_tensor(
                        out=dn_t[:, 4 * kv:4 * kv + 4],
                        in0=psum[:, ko, 0:260].rearrange("p (h e) -> p h e", e=65)[:, :, 64],
                        in1=es_t[:, 4 * kv:4 * kv + 4], op=ALU.add),
                        reads=[pb[ko], es_b], writes=[dn_b])
                    P.op("dve", lambda e, kv=kv: e.reciprocal(out=dn_t[:, 4 * kv:4 * kv + 4], in_=dn_t[:, 4 * kv:4 * kv + 4]),
                         reads=[dn_b], writes=[dn_b])
                    for hh in range(4):
                        h = 4 * kv + hh
                        if hh % 2 == 0:
                            P.op("dve", lambda e, ko=ko, hh=hh, h=h, o_t=o_t: e.tensor_scalar(
                                out=o_t[:, h * 64:(h + 1) * 64], in0=psum[:, ko, hh * 65:hh * 65 + 64],
                                scalar1=dn_t[:, h:h + 1], scalar2=None, op0=ALU.mult),
                                reads=[pb[ko], dn_b], writes=[o_b])
                        else:
                            P.op("act", lambda e, ko=ko, hh=hh, h=h, o_t=o_t: e.activation(
                                out=o_t[:, h * 64:(h + 1) * 64], in_=psum[:, ko, hh * 65:hh * 65 + 64],
                                func=AF.Copy, scale=dn_t[:, h:h + 1]),
                                reads=[pb[ko], dn_b], writes=[o_b])
                def emit_T(sbi, i, o_t, o_b):
                    OT_t, OT_b = OTs[sbi % 2]
                    if True:
                        for hf in range(2):
                            kT = nbt()
                            for cc in range(4):
                                qc = hf * 4 + cc
                                P.op("pe", lambda e, kT=kT, cc=cc, qc=qc, o_t=o_t: e.transpose(
                                    out=psum[:, kT, cc * 128:(cc + 1) * 128], in_=o_t[:, qc * 128:(qc + 1) * 128], identity=ident[:]),
                                    reads=[o_b, ident_b], writes=[pb[kT]], inc=(cc == 3))
                            P.op("act", lambda e, kT=kT, hf=hf, i=i: e.copy(
                                out=OT_t[:, i, hf * 4:hf * 4 + 4, :].rearrange("p c q -> p (c q)"), in_=psum[:, kT, :]),
                                reads=[pb[kT]], writes=[OT_b])

                q_norm(CM0, CM0 + 512)
                build_masks()
                q_stages(CM0, CM0 + 512)
                oproj_pend = []
                for sbi in range(4):
                    c0 = CM0 + 512 * sbi
                    c1 = c0 + 512
                    if sbi + 1 < 4:
                        q_norm(c1, c1 + 512)
                    units = [(sbi, i, kv) for i in range(4) for kv in range(4)]
                    otl = {}
                    for i in range(4):
                        otl[i] = ots[cnt["ot"] % 2]
                        cnt["ot"] += 1
                    pend = []
                    tpend = []
                    for ui, u in enumerate(units):
                        pend.append((u, emit_S(u)))
                        if len(pend) > PIPE:
                            u0, ptk0 = pend.pop(0)
                            if tpend and u0[2] == 1:
                                tpend.pop(0)()
                            emit_V(u0, ptk0, *otl[u0[1]])
                            if u0[2] == 3:
                                tpend.append(lambda u0=u0: emit_T(u0[0], u0[1], *otl[u0[1]]))
                        if ui % 2 == 1 and oproj_pend:
                            oproj_pend.pop(0)()
                    while pend:
                        u0, ptk0 = pend.pop(0)
                        if tpend and u0[2] == 1:
                            tpend.pop(0)()
                        emit_V(u0, ptk0, *otl[u0[1]])
                        if u0[2] == 3:
                            tpend.append(lambda u0=u0: emit_T(u0[0], u0[1], *otl[u0[1]]))
                    while tpend:
                        tpend.pop(0)()
                    while oproj_pend:
                        oproj_pend.pop(0)()
                    OTc_t, OTc_b = OTs[sbi % 2]
                    oproj_pend = o_project_chunks(c0, c1, wo_t, wo_b, lambda qc, OTc_t=OTc_t: OTc_t[:, :, qc, :], OTc_b)
                    if sbi + 1 < 4:
                        q_stages(c1, c1 + 512)
                while oproj_pend:
                    oproj_pend.pop(0)()

                P.flush()
                st2.close()
                if ATTN_S:
                    q_project(CS0, NCOL)
                    wos_t, wos_b = sb(st, "wos", [128, NC8, D], BF16)
                    for qc in range(8):
                        for half in range(2):
                            head = QA[qc] if half == 0 else QB[qc]
                            P.dma("pool", "wos", wos_t[half * 64:(half + 1) * 64, qc, :], wo[head * 64:(head + 1) * 64, :], writes=[])
                    wos_b.w = ("wos", P.dcnt["wos"])
                    esr_t, esr_b = sb(st, "esr", [128, NS, 16], BF16)
                    P.op("dve", lambda e: e.tensor_scalar(out=esr_t[:, :, :], in0=es_t[:, :].unsqueeze(1).broadcast_to([128, NS, 16]),
                                                          scalar1=1.0 / 128.0, scalar2=None, op0=ALU.mult),
                         reads=[es_b], writes=[esr_b])
                    OS_t, OS_b = sb(st, "OTs", [128, 8, NS], BF16)
                    SK_t, SK_b = sb(st, "SK", [128, NS, 256], F32)
                    SVb_t, SVb_b = sb(st, "SVb", [128, NS, 256], BF16)
                    SKT_t, SKT_b = sb(st, "SKT", [128, NS, 2, 128], BF16)
                    Es_t, Es_b = sb(st, "Es", [128, NS * 16], BF16)
                    os_t, os_b = sb(st, "os", [128, NS * 16], F32)
                    rdS_t, rdS_b = sb(st, "rdS", [128, NS * 16], F32)
                    P.dma("sp", "skl", SK_t[:, :, :], kws_o.rearrange("b k f -> k b f"), reads=[kwsd_b], writes=[SK_b])
                    P.dma("pool", "svl", SVb_t[:, :, :], vws_o.rearrange("b k f -> k b f"), reads=[vwsd_b], writes=[SVb_b])
                    for b in range(NS):
                        k = next_bank()
                        for kc in range(2):
                            P.op("pe", lambda e, k=k, b=b, kc=kc: e.transpose(
                                out=psum[:, k, kc * 128:(kc + 1) * 128], in_=SK_t[:, b, kc * 128:(kc + 1) * 128], identity=ident[:]),
                                reads=[SK_b, ident_b], writes=[pb[k]], inc=(kc == 1))
                        P.op("dve", lambda e, k=k, b=b: e.tensor_copy(
                            out=SKT_t[:, b, :, :].rearrange("p a k -> p (a k)"), in_=psum[:, k, 0:256]),
                            reads=[pb[k]], writes=[SKT_b])
                    kss = next_bank()
                    for b in range(NS):
                        for kv in range(4):
                            kc, half = kv // 2, kv % 2
                            P.op("pe", lambda e, b=b, kv=kv, kc=kc, half=half: e.matmul(
                                psum[:, kss, b * 16 + 4 * kv:b * 16 + 4 * kv + 4], lhsT=SKT_t[:, b, kc, :],
                                rhs=QZ[half][0][:, 4 * kc:4 * kc + 4, b], start=True, stop=True),
                                reads=[SKT_b, QZ[half][1]], writes=[pb[kss]], inc=(b == NS - 1 and kv == 3))
                    P.op("act", lambda e: e.activation(out=Es_t[:, :], in_=psum[:, kss, 0:NS * 16], func=AF.Exp, scale=0.125),
                         reads=[pb[kss]], writes=[Es_b])
                    kds = next_bank()
                    P.op("pe", lambda e: e.matmul(psum[:, kds, 0:NS * 16], lhsT=ones1[:, :], rhs=Es_t[:, :], start=True, stop=False),
                         reads=[ones1_b, Es_b], writes=[pb[kds]], inc=False)
                    P.op("pe", lambda e: e.matmul(psum[:, kds, 0:NS * 16], lhsT=ones1[:, :], rhs=esr_t[:, :, :], start=False, stop=True),
                         reads=[ones1_b, esr_b], writes=[pb[kds]])
                    kos = next_bank()
                    for b in range(NS):
                        for kv in range(4):
                            kc = kv // 2
                            P.op("pe", lambda e, b=b, kv=kv, kc=kc: e.matmul(
                                psum[:, kos, b * 16 + 4 * kv:b * 16 + 4 * kv + 4], lhsT=SVb_t[:, b, kc * 128:(kc + 1) * 128],
                                rhs=Es_t[:, b * 16 + 4 * kv:b * 16 + 4 * kv + 4], start=True, stop=True),
                                reads=[SVb_b, Es_b], writes=[pb[kos]], inc=(b == NS - 1 and kv == 3))
                    P.op("dve", lambda e: e.reciprocal(out=rdS_t[:, :], in_=psum[:, kds, 0:NS * 16]), reads=[pb[kds]], writes=[rdS_b])
                    P.op("act", lambda e: e.copy(out=os_t[:, :], in_=psum[:, kos, 0:NS * 16]), reads=[pb[kos]], writes=[os_b])
                    for kv in range(4):
                        kc, half = kv // 2, kv % 2
                        p0_, p1_ = 64 * half, 64 * half + 64
                        P.op("dve", lambda e, kv=kv, kc=kc, p0_=p0_, p1_=p1_: e.tensor_tensor(
                            out=OS_t[p0_:p1_, 4 * kc:4 * kc + 4, :],
                            in0=os_t[p0_:p1_, :].rearrange("p (b v h) -> p v h b", v=4, h=4)[:, kv],
                            in1=rdS_t[p0_:p1_, :].rearrange("p (b v h) -> p v h b", v=4, h=4)[:, kv], op=ALU.mult),
                            reads=[os_b, rdS_b], writes=[OS_b])
                    o_project(CS0, NCOL, wos_t, wos_b, lambda qc: OS_t[:, qc, :], OS_b)
                P.flush()

        if stages >= 5:
            ffn_phase(1, 0, CM0, NCOL)
        if stages >= 6:
            attn_phase(KT, KT_b, V_t, V_b)
        kvst.close()
        if stages >= 7:
            ffn_phase(1, 1, CM0, NCOL)

        with ExitStack() as st:
            yTs = [sb(st, f"yT{k}", [128, NC8, 512], F32) for k in range(2)]
            sq_t, sq_b = sb(st, "sqf", [128, NC8, 512], BF16)
            rs_t, rs_b = sb(st, "rsf", [128, 512], F32)
            yst = [sb(st, f"yst{k}", [128, D], F32) for k in range(3)]
            yi = 0
            fblocks = split(CM0, NCOL, 5)
            nbf = (sq_t, sq_b, rs_t, rs_b)
            norm_block(nbf, fblocks[0][0], fblocks[0][1], G_FIN, yTs[0][0], yTs[0][1], 0)
            for fbi, (b0, b1) in enumerate(fblocks):
                yT_t, yT_b = yTs[fbi % 2]
                if fbi + 1 < len(fblocks):
                    norm_sq(nbf, fblocks[fbi + 1][0], fblocks[fbi + 1][1])
                for t0 in range(b0, b1, 128):
                    if t0 == b0 + 128 and fbi + 1 < len(fblocks):
                        norm_rest(nbf, fblocks[fbi + 1][0], fblocks[fbi + 1][1], G_FIN, yTs[(fbi + 1) % 2][0], yTs[(fbi + 1) % 2][1], 0)
                    t1 = min(b1, t0 + 128)
                    rows = t1 - t0
                    y_t, y_b = yst[yi % 3]
                    for hf in range(2):
                        k = next_bank()
                        for cc in range(4):
                            c = hf * 4 + cc
                            P.op("pe", lambda e, k=k, cc=cc, c=c, t0=t0, t1=t1, b0=b0, rows=rows, yT_t=yT_t: e.transpose(
                                out=psum[0:rows, k, cc * 128:(cc + 1) * 128], in_=yT_t[:, c, t0 - b0:t1 - b0], identity=ident[:]),
                                reads=[yT_b, ident_b], writes=[pb[k]], inc=(cc == 3))
                        if hf == 0:
                            P.op("dve", lambda e, y_t=y_t, k=k, rows=rows, hf=hf: e.tensor_copy(
                                out=y_t[0:rows, hf * 512:(hf + 1) * 512], in_=psum[0:rows, k, :]),
                                reads=[pb[k]], writes=[y_b])
                        else:
                            P.op("act", lambda e, y_t=y_t, k=k, rows=rows, hf=hf: e.copy(
                                out=y_t[0:rows, hf * 512:(hf + 1) * 512], in_=psum[0:rows, k, :]),
                                reads=[pb[k]], writes=[y_b])
                    P.dma("sp", f"yst{yi % 3}", y_o[t0 - CM0:t1 - CM0, :], y_t[0:rows, :], reads=[y_b])
                    yi += 1
            if debug:
                P.dma("sp", "dbg", dbg_o[:, :], xT[:].rearrange("p c n -> p (c n)"), reads=xTb)
            P.wait_all_dma("sp")
            P.flush()
    return nc


def _rope_tables(pos_cols):
    half = 8
    inv = np.power(np.float32(ROPE_THETA), -np.arange(half, dtype=np.float32) * np.float32(2.0 / 16)).astype(np.float32)
    ang = pos_cols.astype(np.float32)[:, None] * inv[None, :]
    return np.cos(ang).astype(np.float32), np.sin(ang).astype(np.float32)


def make_core_inputs(core, inputs):
    seq, half = core // 2, core % 2
    p0 = half * MAIN
    xp = inputs["x_prompt"]
    xs = inputs["x_sample"]
    b0 = core * NS
    xin = np.zeros((NCOL, D), np.float32)
    if p0 > 0:
        xin[0:HALO] = xp[seq, p0 - HALO:p0]
    xin[CM0:CS0] = xp[seq, p0:p0 + MAIN]
    xin[CS0:] = xs[b0:b0 + NS, 0]
    m = {"xin": xin}
    vecs = [inputs["norm_g"][l, i] for l in range(2) for i in range(3)] + [
        inputs["kv_norm_g"], inputs["final_norm_g"], inputs["pool_scale"][0]]
    g = np.stack(vecs, 0).reshape(9, 8, 128)
    m["gains"] = np.ascontiguousarray(g.transpose(2, 0, 1).reshape(128, 72)).astype(np.float32)
    m["sinks"] = np.ascontiguousarray(np.broadcast_to(inputs["attn_sinks"][0][None, :], (128, 16))).astype(np.float32)
    m["state"] = np.ascontiguousarray(inputs["state_pool"][0, b0:b0 + NS].reshape(NS * 15, D))
    m["ck"] = np.ascontiguousarray(inputs["cache_k_win"][b0:b0 + NS].reshape(NS, 128, 256))
    m["cv"] = np.ascontiguousarray(inputs["cache_v_win"][b0:b0 + NS].reshape(NS, 128, 256))
    pos = np.concatenate([p0 - HALO + np.arange(HALO), p0 + np.arange(MAIN), np.full(NS, PAST_LEN)]).astype(np.int64)
    cos, sin = _rope_tables(pos)
    ropef = np.zeros((128, 2, NCOL), np.float32)
    ropef[:, 0, :] = 1.0
    for hb in (0, 64):
        for i in range(16):
            ropef[hb + i, 0, :] = cos[:, i % 8]
            ropef[hb + i, 1, :] = sin[:, i % 8]
    m["ropef"] = ropef
    ropet = np.zeros((144, 2, 4, 8), np.float32)
    ropet[:, 0] = cos[NCOL - 144:, None, :]
    ropet[:, 1] = sin[NCOL - 144:, None, :]
    m["ropet"] = ropet.reshape(144, 2, 32)
    kk = np.arange(128)[:, None]
    qq = np.arange(128)[None, :]
    mprev = (kk > qq).astype(np.float32)
    mcur = (kk <= qq).astype(np.float32)
    mprev0 = mprev if p0 > 0 else np.zeros_like(mprev)
    m["masks"] = np.ascontiguousarray(np.concatenate([mprev, mcur, mprev0], 1))
    pc = np.ones((128, 4, 15), np.float32)
    if p0 == 0:
        for gi, w in enumerate((2, 4, 8, 16)):
            for t in range(15):
                pc[:, gi, t] = w / min(w, t + 1)
    m["pcorr"] = pc.reshape(128, 60)
    m["ident"] = np.eye(128, dtype=np.float32)
    rm = np.zeros((128, 128), np.float32)
    for hb in (0, 64):
        for i in range(8):
            rm[hb + i + 8, hb + i] = -1.0
            rm[hb + i, hb + i + 8] = 1.0
    m["rmat"] = rm
    return m


_NC_CACHE = {}


def kernel(x_prompt, x_sample, state_pool, cache_k_win, cache_v_win, norm_g, ffn_w_gate, ffn_w_up,
           ffn_w_down, pool_w, pool_scale, kv_norm_g, w_kv, w_q, w_o, attn_sinks, final_norm_g,
           _stages=99, _debug=False):
    inputs = dict(x_prompt=np.asarray(x_prompt), x_sample=np.asarray(x_sample), state_pool=np.asarray(state_pool),
                  cache_k_win=np.asarray(cache_k_win), cache_v_win=np.asarray(cache_v_win), norm_g=np.asarray(norm_g),
                  pool_scale=np.asarray(pool_scale), kv_norm_g=np.asarray(kv_norm_g), attn_sinks=np.asarray(attn_sinks),
                  final_norm_g=np.asarray(final_norm_g))
    shared = {
        "wg": np.ascontiguousarray(np.asarray(ffn_w_gate, dtype=np.float32)),
        "wu": np.ascontiguousarray(np.asarray(ffn_w_up, dtype=np.float32)),
        "wd": np.ascontiguousarray(np.asarray(ffn_w_down, dtype=np.float32)),
        "pw": np.ascontiguousarray(np.asarray(pool_w, dtype=np.float32)[0]),
        "wkv": np.ascontiguousarray(np.asarray(w_kv, dtype=np.float32)),
        "wq": np.ascontiguousarray(np.asarray(w_q, dtype=np.float32)[0]),
        "wo": np.ascontiguousarray(np.asarray(w_o, dtype=np.float32)[0]),
    }
    key = (_stages, _debug)
    if key not in _NC_CACHE:
        _NC_CACHE[key] = build_program(_stages, _debug)
    nc = _NC_CACHE[key]
    in_maps = []
    for core in range(8):
        m = make_core_inputs(core, inputs)
        m.update(shared)
        in_maps.append(m)
    res = run_bass_kernel_spmd(nc, in_maps, core_ids=list(range(8)))
    R = res.results
    y_prompt = np.zeros((4, 4096, D), np.float32)
    y_sample = np.zeros((128, 1, D), np.float32)
    pool_prompt = np.zeros((1, 4, 15, D), np.float32)
    pool_sample = np.zeros((1, 128, 15, D), np.float32)
    kwp = np.zeros((4, 128, 4, 64), np.float32)
    vwp = np.zeros((4, 128, 4, 64), np.float32)
    kws = np.zeros((128, 128, 4, 64), np.float32)
    vws = np.zeros((128, 128, 4, 64), np.float32)
    for core in range(8):
        seq, half = core // 2, core % 2
        p0 = half * MAIN
        b0 = core * NS
        r = R[core]
        y_prompt[seq, p0:p0 + MAIN] = r["y"][0:MAIN]
        y_sample[b0:b0 + NS, 0] = r["y"][MAIN:MAIN + NS]
        pool_sample[0, b0:b0 + NS] = r["pool_s"]
        kws[b0:b0 + NS] = r["kws"].reshape(NS, 128, 4, 64)
        vws[b0:b0 + NS] = r["vws"].reshape(NS, 128, 4, 64)
        if half == 1:
            pool_prompt[0, seq] = r["pool_o"][0:15]
            kwp[seq] = r["kwp"].reshape(128, 4, 64)
            vwp[seq] = r["vwp"].reshape(128, 4, 64)
    if _debug:
        return (y_prompt, y_sample, pool_prompt, pool_sample, kwp, vwp, kws, vws), R
    return (y_prompt, y_sample, pool_prompt, pool_sample, kwp, vwp, kws, vws)
```

```python
import numpy as np
from contextlib import ExitStack
import concourse.bass as bass
import concourse.mybir as mybir
from concourse.bass_utils import run_bass_kernel_spmd

F32 = mybir.dt.float32
BF16 = mybir.dt.bfloat16
AF = mybir.ActivationFunctionType
ALU = mybir.AluOpType
AX = mybir.AxisListType

D = 1024
DFF = 2816
NJ = DFF // 128
NC8 = 8
HALO = 143
MAIN = 2048
NS = 16
NCOL = HALO + MAIN + NS
CM0 = HALO
CS0 = HALO + MAIN
EPS = 1e-5
PAST_LEN = 16384
ROPE_THETA = 500000.0
QA = [0, 1, 2, 3, 8, 9, 10, 11]
QB = [4, 5, 6, 7, 12, 13, 14, 15]
G_N = lambda l, i: l * 3 + i
G_KV, G_FIN, G_PS = 6, 7, 8
import os
KVOPT = int(os.environ.get('KVOPT', '9'))
ATTN_S = int(os.environ.get('ATTN_S', '1'))
PIPE = int(os.environ.get('PIPE', '1'))
POOLPE = int(os.environ.get('POOLPE', '1'))
POOLC = int(os.environ.get('POOLC', '8'))
ENGS = ["pe", "act", "dve", "pool", "sp"]


def split(a, b, n):
    tot = b - a
    out = []
    s = a
    for i in range(n):
        e = a + (tot * (i + 1)) // n
        out.append((s, e))
        s = e
    return out


class Buf:
    __slots__ = ("name", "w", "r", "excl")

    def __init__(self, name, excl=False):
        self.name = name
        self.w = None
        self.r = {}
        self.excl = excl


class Prog:
    def __init__(self, nc, stack):
        self.nc = nc
        self.stack = stack
        self.sems = {e: stack.enter_context(nc.semaphore("q_" + e)) for e in ENGS}
        self.cnt = {e: 0 for e in ENGS}
        self.seen = {e: {} for e in ENGS}
        self.ops = {e: [] for e in ENGS}
        self.dsem = {}
        self.dcnt = {}

    def semobj(self, k):
        return self.sems[k] if k in self.sems else self.dsem[k]

    def _waits(self, eng, reads, writes):
        deps = {}

        def add(tok, same_ok):
            if tok is None:
                return
            k, v = tok
            if k == eng and not same_ok:
                return
            if deps.get(k, 0) < v:
                deps[k] = v

        for b in reads:
            add(b.w, True)
            if b.excl:
                for k, v in b.r.items():
                    add((k, v), False)
        for b in writes:
            add(b.w, False)
            for k, v in b.r.items():
                add((k, v), False)
        waits = []
        for k, v in deps.items():
            if self.seen[eng].get(k, 0) >= v:
                continue
            self.seen[eng][k] = v
            waits.append((k, v))
        return waits

    def op(self, eng, fn, reads=(), writes=(), inc=True):
        waits = self._waits(eng, reads, writes)
        val = self.cnt[eng] + 1
        if inc:
            self.cnt[eng] = val
        for b in reads:
            if b.r.get(eng, 0) < val:
                b.r[eng] = val
        for b in writes:
            b.w = (eng, val)
            b.r = {}
        self.ops[eng].append((waits, fn, inc))

    def dma(self, eng, sem, out, in_, reads=(), writes=()):
        if sem not in self.dsem:
            self.dsem[sem] = self.stack.enter_context(self.nc.semaphore("d_" + sem))
            self.dcnt[sem] = 0
        waits = self._waits(eng, reads, writes)
        self.dcnt[sem] += 16
        val = self.dcnt[sem]
        for b in reads:
            if b.r.get(sem, 0) < val:
                b.r[sem] = val
        for b in writes:
            b.w = (sem, val)
            b.r = {}
        so = self.dsem[sem]

        def fn(e, out=out, in_=in_, so=so):
            e.dma_start(out=out, in_=in_).then_inc(so, 16)
            return None

        self.ops[eng].append((waits, fn, False))

    def wait_all_dma(self, eng):
        waits = []
        for k, v in self.dcnt.items():
            if self.seen[eng].get(k, 0) < v:
                self.seen[eng][k] = v
                waits.append((k, v))
        self.ops[eng].append((waits, None, False))

    def flush(self):
        self.wait_all_dma("sp")
        with self.nc.Block() as blk:
            decos = {"pe": blk.tensor, "act": blk.scalar, "dve": blk.vector,
                     "pool": blk.gpsimd, "sp": blk.sync}
            for eng in ENGS:
                ops = self.ops[eng]
                sem = self.sems[eng]

                def body(e, ops=ops, sem=sem):
                    for waits, fn, inc in ops:
                        for k, v in waits:
                            e.wait_ge(self.semobj(k), v)
                        if fn is None:
                            continue
                        ins = fn(e)
                        if inc:
                            ins.then_inc(sem, 1)

                decos[eng](body)
                self.ops[eng] = []


def build_program(stages=99, debug=False):
    nc = bass.Bass("TRN2", target_bir_lowering=False)

    def din(name, shape, dt=F32):
        return nc.dram_tensor(name, list(shape), dt, kind="ExternalInput").ap()

    def dout(name, shape, dt=F32):
        return nc.dram_tensor(name, list(shape), dt, kind="ExternalOutput").ap()

    xin = din("xin", [NCOL, D])
    wg = din("wg", [2, 2, D, DFF])
    wu = din("wu", [2, 2, D, DFF])
    wd = din("wd", [2, 2, DFF, D])
    pw = din("pw", [4, 256, 256])
    wkv = din("wkv", [D, 512])
    wq = din("wq", [D, D])
    wo = din("wo", [D, D])
    gains_d = din("gains", [128, 9 * 8])
    sinks_d = din("sinks", [128, 16])
    state_d = din("state", [NS * 15, D])
    ck_d = din("ck", [NS, 128, 256])
    cv_d = din("cv", [NS, 128, 256])
    ropef_d = din("ropef", [128, 2, NCOL])
    ropet_d = din("ropet", [144, 2, 4 * 8])
    masks_d = din("masks", [128, 3 * 128])
    pcorr_d = din("pcorr", [128, 4 * 15])
    ident_d = din("ident", [128, 128])
    rmat_d = din("rmat", [128, 128])

    y_o = dout("y", [MAIN + NS, D])
    poolo_o = dout("pool_o", [31, D])
    pools_o = dout("pool_s", [NS, 15, D])
    kwp_o = dout("kwp", [128, 256])
    vwp_o = dout("vwp", [128, 256])
    kws_o = dout("kws", [NS, 128, 256])
    vws_o = dout("vws", [NS, 128, 256])
    dbg_o = dout("dbg", [128, 8 * NCOL]) if debug else None

    with ExitStack() as top:
        P = Prog(nc, top)

        uid = [0]

        def sb(st, name, shape, dt):
            uid[0] += 1
            t = st.enter_context(nc.sbuf_tensor(f"s{uid[0]}_{name}", list(shape), dt))
            return t, Buf(name)

        xT, xT_b = sb(top, "xT", [128, NC8, NCOL], F32)
        xTb = [Buf(f"xT{c}") for c in range(NC8)]
        gains, gains_b = sb(top, "gains", [128, 9, 8], F32)
        ident, ident_b = sb(top, "ident", [128, 128], F32)
        onesm, onesm_b = sb(top, "onesm", [128, 128], BF16)
        ones1, ones1_b = sb(top, "ones1", [128, 128], BF16)
        epst, epst_b = sb(top, "epst", [128, 1], F32)
        psum = top.enter_context(nc.psum_tensor("ps", [128, 8, 512], F32))
        pb = [Buf(f"bank{k}", excl=True) for k in range(8)]
        bank_rr = [0]

        def next_bank():
            k = bank_rr[0]
            bank_rr[0] = (k + 1) % 8
            return k

        P.dma("sp", "c_gains", gains[:].rearrange("p a b -> p (a b)"), gains_d[:, :], writes=[gains_b])
        P.dma("sp", "c_ident", ident[:], ident_d[:, :], writes=[ident_b])
        P.op("dve", lambda e: e.memset(onesm[:], 1.0 / 1024.0), writes=[onesm_b])
        P.op("dve", lambda e: e.memset(ones1[:], 1.0), writes=[ones1_b])
        P.op("dve", lambda e: e.memset(epst[:], EPS), writes=[epst_b])

        def p0_tiles(st):
            NST = 3
            stg = [sb(st, f"stg{k}", [128, D], F32) for k in range(NST)]
            nrt = (NCOL + 127) // 128
            out = []
            for r in range(nrt):
                out.append((r * 128, lambda r=r: p0_tile(stg, r)))
            return out

        def p0_tile(stg, r):
            NST = 3
            if True:
                r0 = r * 128
                rows = min(128, NCOL - r0)
                s_t, s_b = stg[r % NST]
                P.dma("sp", f"stg{r % NST}", s_t[0:rows, :], xin[r0:r0 + rows, :], writes=[s_b])
                for hf in range(2):
                    k = next_bank()
                    for cc in range(4):
                        c = hf * 4 + cc
                        P.op("pe", lambda e, k=k, cc=cc, c=c, s_t=s_t, rows=rows: e.transpose(
                            out=psum[:, k, cc * 128:cc * 128 + rows], in_=s_t[0:rows, c * 128:(c + 1) * 128],
                            identity=ident[0:rows, 0:rows]),
                            reads=[s_b, ident_b], writes=[pb[k]], inc=(cc == 3))
                    src = psum[:, k, :].rearrange("p (c n) -> p c n", c=4)[:, :, 0:rows]
                    dst = xT[:, hf * 4:hf * 4 + 4, r0:r0 + rows]
                    wr = [xTb[c] for c in range(hf * 4, hf * 4 + 4)]
                    if (r + hf) % 2 == 0:
                        P.op("dve", lambda e, dst=dst, src=src: e.tensor_copy(out=dst, in_=src),
                             reads=[pb[k]], writes=wr)
                    else:
                        P.op("act", lambda e, dst=dst, src=src: e.copy(out=dst, in_=src),
                             reads=[pb[k]], writes=wr)

        def norm_block(st_bufs, c0, c1, grow, out_t, out_b, out_off):
            norm_sq(st_bufs, c0, c1)
            norm_rest(st_bufs, c0, c1, grow, out_t, out_b, out_off)

        def norm_sq(st_bufs, c0, c1):
            sq_t, sq_b, rs_t, rs_b = st_bufs
            n = c1 - c0
            P.op("act", lambda e: e.activation(out=sq_t[:, :, 0:n], in_=xT[:, :, c0:c1], func=AF.Square),
                 reads=xTb, writes=[sq_b])

        def norm_rest(st_bufs, c0, c1, grow, out_t, out_b, out_off):
            sq_t, sq_b, rs_t, rs_b = st_bufs
            n = c1 - c0
            k = next_bank()
            for c in range(NC8):
                P.op("pe", lambda e, c=c, k=k: e.matmul(psum[:, k, 0:n], lhsT=onesm[:], rhs=sq_t[:, c, 0:n],
                                                        start=(c == 0), stop=(c == NC8 - 1)),
                     reads=[sq_b, onesm_b], writes=[pb[k]], inc=(c == NC8 - 1))
            P.op("act", lambda e, k=k: e.activation(out=rs_t[:, 0:n], in_=psum[:, k, 0:n], func=AF.Ln,
                                                    bias=epst[:, 0:1], scale=1.0),
                 reads=[pb[k], epst_b], writes=[rs_b])
            P.op("act", lambda e: e.activation(out=rs_t[:, 0:n], in_=rs_t[:, 0:n], func=AF.Exp, scale=-0.5),
                 reads=[rs_b], writes=[rs_b])
            for c in range(NC8):
                P.op("dve", lambda e, c=c: e.scalar_tensor_tensor(
                    out=out_t[:, c, out_off:out_off + n], in0=xT[:, c, c0:c1], scalar=gains[:, grow, c:c + 1],
                    in1=rs_t[:, 0:n], op0=ALU.mult, op1=ALU.mult),
                    reads=[xTb[c], rs_b, gains_b], writes=[out_b])

        def ffn_phase(l, i, ca, cb, ngroups=3, nblk=2, pre_tiles=None):
            grow = G_N(l, 0 if i == 0 else 2)
            groups = [split(a, b, nblk) for (a, b) in split(ca, cb, ngroups)]
            gmax = max(g[-1][1] - g[0][0] for g in groups)
            bmax = max(b1 - b0 for g in groups for (b0, b1) in g)
            Wg = wg[l, i].rearrange("(c p) f -> p c f", p=128)
            Wu = wu[l, i].rearrange("(c p) f -> p c f", p=128)
            Wd = wd[l, i].rearrange("(j p) d -> p j d", p=128)
            with ExitStack() as st:
                xn_t, xn_b = sb(st, "xn", [128, NC8, gmax], BF16)
                h_t, h_b0 = sb(st, "h", [128, NJ, gmax], BF16)
                hb = [Buf(f"h{j}") for j in range(NJ)]
                NGU = 5
                gu = [sb(st, f"gu{k}", [128, 2, NC8, 128], BF16) for k in range(NGU)]
                wdr = [sb(st, f"wd{k}", [128, NJ, 256], BF16) for k in range(2)]
                sg = [sb(st, f"sg{k}", [128, bmax], F32) for k in range(4)]
                rs_t, rs_b = sb(st, "rs", [128, bmax], F32)
                nbs_ = []
                for t_ in range(nblk):
                    sq_t, sq_b = sb(st, f"sq{t_}", [128, NC8, bmax], BF16)
                    nbs_.append((sq_t, sq_b, rs_t, rs_b))
                sgi = [0]
                gu_loads = [(g, j) for g in range(ngroups) for j in range(NJ)]
                wd_loads = [(g, cp) for g in range(ngroups) for cp in range(4)]
                gu_next = [0]
                wd_next = [0]

                def issue_gu():
                    idx = gu_next[0]
                    if idx >= len(gu_loads):
                        return
                    gu_next[0] += 1
                    g, j = gu_loads[idx]
                    t, b = gu[idx % NGU]
                    P.dma("pool", f"gu{idx % NGU}", t[:, 0, :, :], Wg[:, :, j * 128:(j + 1) * 128], writes=[b])
                    P.dma("pool", f"gu{idx % NGU}", t[:, 1, :, :], Wu[:, :, j * 128:(j + 1) * 128], writes=[])
                    b.w = (f"gu{idx % NGU}", P.dcnt[f"gu{idx % NGU}"])

                def issue_wd():
                    idx = wd_next[0]
                    if idx >= len(wd_loads):
                        return
                    wd_next[0] += 1
                    g, cp = wd_loads[idx]
                    t, b = wdr[idx % 2]
                    P.dma("pool", f"wd{idx % 2}", t[:, :, :], Wd[:, :, cp * 256:(cp + 1) * 256], writes=[b])

                for _ in range(NGU):
                    issue_gu()

                def do_norm_sq(g):
                    for t_, (b0, b1) in enumerate(groups[g]):
                        norm_sq(nbs_[t_], b0, b1)

                def do_norm_rest(g):
                    off0 = groups[g][0][0]
                    for t_, (b0, b1) in enumerate(groups[g]):
                        norm_rest(nbs_[t_], b0, b1, grow, xn_t, xn_b, b0 - off0)

                pend_tiles = []
                if pre_tiles is not None:
                    g0_end = groups[0][-1][1]
                    for (r0_, em) in pre_tiles(st):
                        if r0_ < g0_end:
                            em()
                        else:
                            pend_tiles.append(em)
                do_norm_sq(0)
                do_norm_rest(0)
                gu_idx = 0
                wd_idx = 0
                for g in range(ngroups):
                    off0 = groups[g][0][0]
                    blks = groups[g]
                    for j in range(NJ):
                        gt, gb = gu[gu_idx % NGU]
                        gu_idx += 1
                        base = (j % 2) * 4
                        for which in range(2):
                            for t, (b0, b1) in enumerate(blks):
                                k = base + which * 2 + t
                                n = b1 - b0
                                for c in range(NC8):
                                    P.op("pe", lambda e, k=k, n=n, gt=gt, which=which, c=c, b0=b0, b1=b1, off0=off0: e.matmul(
                                        psum[:, k, 0:n], lhsT=gt[:, which, c, :], rhs=xn_t[:, c, b0 - off0:b1 - off0],
                                        start=(c == 0), stop=(c == NC8 - 1)),
                                        reads=[gb, xn_b], writes=[pb[k]], inc=(c == NC8 - 1))
                        issue_gu()
                        if g == 0 and j == 5:
                            issue_wd()
                            issue_wd()
                        for t, (b0, b1) in enumerate(blks):
                            n = b1 - b0
                            kg = base + t
                            ku = base + 2 + t
                            s_t, s_b = sg[sgi[0] % 4]
                            sgi[0] += 1
                            P.op("act", lambda e, s_t=s_t, kg=kg, n=n: e.activation(out=s_t[:, 0:n], in_=psum[:, kg, 0:n], func=AF.Silu),
                                 reads=[pb[kg]], writes=[s_b])
                            P.op("dve", lambda e, s_t=s_t, ku=ku, n=n, j=j, b0=b0, b1=b1, off0=off0: e.tensor_tensor(
                                out=h_t[:, j, b0 - off0:b1 - off0], in0=psum[:, ku, 0:n], in1=s_t[:, 0:n], op=ALU.mult),
                                reads=[pb[ku], s_b], writes=[hb[j]])
                        if pend_tiles:
                            pend_tiles.pop(0)()
                    while pend_tiles:
                        pend_tiles.pop(0)()
                    if g + 1 < ngroups:
                        do_norm_sq(g + 1)
                    for cp in range(4):
                        wt, wb = wdr[wd_idx % 2]
                        wd_idx += 1
                        for cc in range(2):
                            c = cp * 2 + cc
                            for t, (b0, b1) in enumerate(blks):
                                n = b1 - b0
                                k = next_bank()
                                for j in range(NJ):
                                    P.op("pe", lambda e, k=k, n=n, wt=wt, j=j, cc=cc, b0=b0, b1=b1, off0=off0: e.matmul(
                                        psum[:, k, 0:n], lhsT=wt[:, j, cc * 128:(cc + 1) * 128], rhs=h_t[:, j, b0 - off0:b1 - off0],
                                        start=(j == 0), stop=(j == NJ - 1)),
                                        reads=[wb, hb[j]], writes=[pb[k]], inc=(j == NJ - 1))
                                P.op("dve", lambda e, k=k, n=n, c=c, b0=b0, b1=b1: e.scalar_tensor_tensor(
                                    out=xT[:, c, b0:b1], in0=psum[:, k, 0:n], scalar=0.5, in1=xT[:, c, b0:b1],
                                    op0=ALU.mult, op1=ALU.add),
                                    reads=[pb[k], xTb[c]], writes=[xTb[c]])
                        issue_wd()
                        if cp == 0 and g + 1 < ngroups:
                            do_norm_rest(g + 1)
                P.flush()

        if stages >= 1:
            ffn_phase(0, 0, 0, NCOL, pre_tiles=p0_tiles)

        def pool_phase():
            grow = G_N(0, 1)
            blocks = split(15, NCOL, 7)
            PW = 336
            with ExitStack() as st:
                pwt, pw_b = sb(st, "pw", [128, 4, 2, 256], BF16)
                P.dma("pool", "pw", pwt[:].rearrange("p g c d -> p (g c) d"),
                      pw.rearrange("g (c p) d -> p (g c) d", p=128), writes=[pw_b])
                pc_t, pc_b = sb(st, "pc", [128, 4, 15], F32)
                P.dma("sp", "pc", pc_t[:].rearrange("p g t -> p (g t)"), pcorr_d[:, :], writes=[pc_b])
                hbs = [sb(st, f"hb{k}", [128, NC8, PW], F32) for k in range(2)]
                tsets = []
                for k_ in range(2):
                    tA_, _ = sb(st, f"tA{k_}", [128, NC8, PW], F32)
                    tB_, _ = sb(st, f"tB{k_}", [128, NC8, PW], F32)
                    db_, _ = sb(st, f"db{k_}", [128, NC8, PW], BF16)
                    tsets.append((tA_, [Buf(f"tA{k_}_{c}") for c in range(NC8)], tB_, [Buf(f"tB{k_}_{c}") for c in range(NC8)],
                                  db_, [Buf(f"db{k_}_{c}") for c in range(NC8)]))
                pwn, pwn_b = sb(st, "pwn", [128, 4, 2, 256], BF16)
                P.op("dve", lambda e: e.tensor_scalar(out=pwn[:].rearrange("p g c d -> p (g c d)"),
                                                      in0=pwt[:].rearrange("p g c d -> p (g c d)"),
                                                      scalar1=-1.0, scalar2=None, op0=ALU.mult), reads=[pw_b], writes=[pwn_b])
                hbfs = [sb(st, f"hbf{k}", [128, NC8, PW], BF16) for k in range(2)]
                sq_t, sq_b = sb(st, "sqp", [128, NC8, PW], BF16)
                rs_t, rs_b = sb(st, "rsp", [128, PW], F32)
                hist_t, hist_b = sb(st, "hist", [128, NC8, NS * 15], F32)
                red_t, red_b = sb(st, "red", [128, NC8, NS], F32)
                po_t, po_b = sb(st, "po", [31, D], F32)
                sst = [sb(st, f"sst{k}", [128, D], F32) for k in range(2)]
                P.dma("sp", "pools", pools_o[:, 0:14, :], state_d.rearrange("(b r) d -> b r d", r=15)[:, 1:15, :])
                for r, (r0, rows) in enumerate(((0, 128), (128, NS * 15 - 128))):
                    s_t, s_b = sst[r]
                    P.dma("sp", f"sst{r}", s_t[0:rows, :], state_d[r0:r0 + rows, :], writes=[s_b])
                    for hf in range(2):
                        k = next_bank()
                        for cc in range(4):
                            c = hf * 4 + cc
                            P.op("pe", lambda e, k=k, cc=cc, c=c, s_t=s_t, rows=rows: e.transpose(
                                out=psum[:, k, cc * 128:cc * 128 + rows], in_=s_t[0:rows, c * 128:(c + 1) * 128],
                                identity=ident[0:rows, 0:rows]),
                                reads=[s_b, ident_b], writes=[pb[k]], inc=(cc == 3))
                        src = psum[:, k, :].rearrange("p (c n) -> p c n", c=4)[:, :, 0:rows]
                        dst = hist_t[:, hf * 4:hf * 4 + 4, r0:r0 + rows]
                        P.op("act", lambda e, dst=dst, src=src: e.copy(out=dst, in_=src), reads=[pb[k]], writes=[hist_b])
                for c in range(NC8):
                    w = 2 << (c // 2)
                    P.op("dve", lambda e, c=c, w=w: e.tensor_reduce(
                        out=red_t[:, c, :], in_=hist_t[:, c, :].rearrange("p (b r) -> p b r", r=15)[:, :, 16 - w:15],
                        axis=AX.X, op=ALU.add), reads=[hist_b], writes=[red_b])
                state = {"prev": None}

                def stage_A(bi):
                    s0, e0 = blocks[bi]
                    hb_t, hb_b = hbs[bi % 2]
                    tA, tAb, tB, tBb, db_t, dbb = tsets[bi % 2]
                    if bi == 0:
                        a = 0
                        n = e0
                        norm_block((sq_t, sq_b, rs_t, rs_b), 0, e0, grow, hb_t, hb_b, 0)
                    else:
                        a = s0 - 15
                        n = e0 - a
                        norm_block((sq_t, sq_b, rs_t, rs_b), s0, e0, grow, hb_t, hb_b, 15)
                        p_t, p_b, pn = state["prev"]
                        P.op("act", lambda e, hb_t=hb_t, p_t=p_t, pn=pn: e.copy(out=hb_t[:, :, 0:15], in_=p_t[:, :, pn - 15:pn]),
                             reads=[p_b], writes=[hb_b])
                    state["prev"] = (hb_t, hb_b, n)
                    hbf_t, hbf_b = hbfs[bi % 2]
                    if POOLPE:
                        P.op("act", lambda e, hbf_t=hbf_t, hb_t=hb_t, n=n: e.copy(out=hbf_t[:, :, 15:n], in_=hb_t[:, :, 15:n]),
                             reads=[hb_b], writes=[hbf_b])
                    for step in range(4):
                        sh = 1 << step
                        for c in range(2 * step, NC8):
                            if step == 0:
                                src_t, src_b = hb_t, hb_b
                            else:
                                src_t, src_b = (tA, tAb[c]) if step % 2 == 1 else (tB, tBb[c])
                            dst_t, dst_b = (tA, tAb[c]) if step % 2 == 0 else (tB, tBb[c])
                            lo = 2 * sh - 1
                            P.op("pool" if (c >= POOLC) else "dve", lambda e, dst_t=dst_t, src_t=src_t, c=c, lo=lo, sh=sh, n=n: e.tensor_tensor(
                                out=dst_t[:, c, lo:n], in0=src_t[:, c, lo:n], in1=src_t[:, c, lo - sh:n - sh], op=ALU.add),
                                reads=[src_b], writes=[dst_b])
                    for c in range(NC8):
                        g = c // 2
                        w = 2 << g
                        S_t, S_b = (tA, tAb[c]) if g % 2 == 0 else (tB, tBb[c])
                        if a <= CM0 and CM0 + 15 <= e0:
                            u0 = CM0 - a
                            P.op("dve", lambda e, S_t=S_t, c=c, u0=u0, g=g: e.tensor_tensor(
                                out=S_t[:, c, u0:u0 + 15], in0=S_t[:, c, u0:u0 + 15], in1=pc_t[:, g, :], op=ALU.mult),
                                reads=[S_b, pc_b], writes=[S_b])
                        if POOLPE:
                            P.op("act", lambda e, S_t=S_t, c=c, w=w, n=n, db_t=db_t: e.activation(
                                out=db_t[:, c, 15:n], in_=S_t[:, c, 15:n], func=AF.Copy, scale=1.0 / w),
                                reads=[S_b], writes=[dbb[c]])
                        else:
                            P.op("dve", lambda e, S_t=S_t, c=c, w=w, n=n, hb_t=hb_t, db_t=db_t: e.scalar_tensor_tensor(
                                out=db_t[:, c, 15:n], in0=S_t[:, c, 15:n], scalar=1.0 / w, in1=hb_t[:, c, 15:n],
                                op0=ALU.mult, op1=ALU.subtract), reads=[S_b, hb_b], writes=[dbb[c]])
                        if e0 == NCOL:
                            us = CS0 - a
                            P.op("dve", lambda e, c=c, us=us, hb_t=hb_t: e.tensor_tensor(
                                out=red_t[:, c, :], in0=red_t[:, c, :], in1=hb_t[:, c, us:us + NS], op=ALU.add),
                                reads=[red_b, hb_b], writes=[red_b])
                            if POOLPE:
                                P.op("dve", lambda e, c=c, us=us, w=w, db_t=db_t: e.tensor_scalar(
                                    out=db_t[:, c, us:us + NS], in0=red_t[:, c, :], scalar1=1.0 / w, scalar2=None, op0=ALU.mult),
                                    reads=[red_b], writes=[dbb[c]])
                            else:
                                P.op("dve", lambda e, c=c, us=us, w=w, hb_t=hb_t, db_t=db_t: e.scalar_tensor_tensor(
                                    out=db_t[:, c, us:us + NS], in0=red_t[:, c, :], scalar=1.0 / w, in1=hb_t[:, c, us:us + NS],
                                    op0=ALU.mult, op1=ALU.subtract), reads=[red_b, hb_b], writes=[dbb[c]])
                    return (hb_t, hb_b, n)

                def stage_B(bi, hb_t, hb_b, n):
                    s0, e0 = blocks[bi]
                    tA, tAb, tB, tBb, db_t, dbb = tsets[bi % 2]
                    m = n - 15
                    for g in range(4):
                        for dc in range(2):
                            c2 = 2 * g + dc
                            k = next_bank()
                            hbf_t, hbf_b = hbfs[bi % 2]
                            for cc in range(2):
                                P.op("pe", lambda e, k=k, m=m, g=g, cc=cc, dc=dc, n=n, db_t=db_t: e.matmul(
                                    psum[:, k, 0:m], lhsT=pwt[:, g, cc, dc * 128:(dc + 1) * 128], rhs=db_t[:, 2 * g + cc, 15:n],
                                    start=(cc == 0), stop=(cc == 1 and not POOLPE)),
                                    reads=[pw_b, dbb[2 * g + cc]], writes=[pb[k]], inc=(cc == 1 and not POOLPE))
                            for cc in range(2 if POOLPE else 0):
                                P.op("pe", lambda e, k=k, m=m, g=g, cc=cc, dc=dc, n=n, hbf_t=hbf_t: e.matmul(
                                    psum[:, k, 0:m], lhsT=pwn[:, g, cc, dc * 128:(dc + 1) * 128], rhs=hbf_t[:, 2 * g + cc, 15:n],
                                    start=False, stop=(cc == 1)),
                                    reads=[pwn_b, hbf_b], writes=[pb[k]], inc=(cc == 1))
                            P.op("dve", lambda e, k=k, m=m, c2=c2, s0=s0, e0=e0: e.scalar_tensor_tensor(
                                out=xT[:, c2, s0:e0], in0=psum[:, k, 0:m], scalar=gains[:, G_PS, c2:c2 + 1], in1=xT[:, c2, s0:e0],
                                op0=ALU.mult, op1=ALU.add), reads=[pb[k], xTb[c2], gains_b], writes=[xTb[c2]])
                    if e0 == NCOL:
                        u31 = n - 31
                        for hf in range(2):
                            k = next_bank()
                            for cc in range(4):
                                c = hf * 4 + cc
                                P.op("pe", lambda e, k=k, cc=cc, c=c, hb_t=hb_t, u31=u31: e.transpose(
                                    out=psum[0:31, k, cc * 128:(cc + 1) * 128], in_=hb_t[:, c, u31:u31 + 31], identity=ident[:]),
                                    reads=[hb_b, ident_b], writes=[pb[k]], inc=(cc == 3))
                            P.op("act", lambda e, k=k, hf=hf: e.copy(out=po_t[0:31, hf * 512:(hf + 1) * 512], in_=psum[0:31, k, :]),
                                 reads=[pb[k]], writes=[po_b])
                        P.dma("sp", "po", poolo_o[:, :], po_t[:, :], reads=[po_b])
                        P.dma("sp", "po", pools_o[:, 14, :], po_t[15:31, :], reads=[po_b])

                infoA = {0: stage_A(0)}
                for bi in range(len(blocks)):
                    if bi + 1 < len(blocks):
                        infoA[bi + 1] = stage_A(bi + 1)
                    stage_B(bi, *infoA[bi])
                P.flush()

        kwsd_b = Buf("kws_dram")
        vwsd_b = Buf("vws_dram")

        def kv_phase(KT, KT_b, V_t, V_b):
            blocks = [(15, 527), (527, 1039), (1039, 1551), (1551, 2063), (2063, NCOL)]
            with ExitStack() as st:
                wkv_t, wkv_b = sb(st, "wkv", [128, NC8, 512], BF16)
                P.dma("pool", "wkv", wkv_t[:], wkv.rearrange("(c p) f -> p c f", p=128), writes=[wkv_b])
                rm_t, rm_b = sb(st, "rmat", [128, 128], F32)
                P.dma("sp", "rmat", rm_t[:], rmat_d[:, :], writes=[rm_b])
                rtm_t, rtm_b = sb(st, "rtm", [128, 64], F32)
                rts_t, rts_b = sb(st, "rts", [NS, 64], F32)
                P.dma("sp", "rtm", rtm_t[:], ropet_d[0:128].rearrange("t a b -> t (a b)"), writes=[rtm_b])
                P.dma("sp", "rts", rts_t[:], ropet_d[128:144].rearrange("t a b -> t (a b)"), writes=[rts_b])
                P.dma("sp", "kws", kws_o[:, 0:127, :], ck_d[:, 1:128, :], writes=[kwsd_b])
                P.dma("sp", "vws", vws_o[:, 0:127, :], cv_d[:, 1:128, :], writes=[vwsd_b])
                kns = [sb(st, f"kn{k}", [128, NC8, 512], BF16) for k in range(2)]
                sq_t, sq_b = sb(st, "sqk", [128, NC8, 512], BF16)
                rs_t, rs_b = sb(st, "rsk", [128, 512], F32)
                rts2 = [sb(st, f"rt{k}", [128, 2, 512], F32) for k in range(2)]
                kfs = [sb(st, f"kf{k}", [128, 512], F32) for k in range(2)]
                t1s = [sb(st, f"t1{k}", [128, 512], F32) for k in range(2)]
                t2s = [sb(st, f"t2{k}", [128, 512], F32) for k in range(2)]
                ko_t, ko_b = sb(st, "ko", [128, 256], F32)
                vo_t, vo_b = sb(st, "vo", [128, 256], F32)
                kso_t, kso_b = sb(st, "kso", [NS, 256], F32)
                vso_t, vso_b = sb(st, "vso", [NS, 256], F32)
                tm = [sb(st, f"tm{k}", [128, 32], F32) for k in range(4)]
                krm_t, krm_b = sb(st, "krm", [128, 256], F32)
                krs_t, krs_b = sb(st, "krs", [NS, 256], F32)
                ri = 0
                nbk = (sq_t, sq_b, rs_t, rs_b)
                norm_block(nbk, blocks[0][0], blocks[0][1], G_KV, kns[0][0], kns[0][1], 0)
                for bi, (b0, b1) in enumerate(blocks):
                    n = b1 - b0
                    kn_t, kn_b = kns[bi % 2]
                    if bi + 1 < len(blocks):
                        norm_sq(nbk, blocks[bi + 1][0], blocks[bi + 1][1])
                    rt_t, rt_b = rts2[bi % 2]
                    P.dma("sp", f"rt{bi % 2}", rt_t[:, :, 0:n], ropef_d[:, :, b0:b1], writes=[rt_b])
                    for kc in range(2):
                        kf_t, kf_b = kfs[kc]
                        k = next_bank()
                        for c in range(NC8):
                            P.op("pe", lambda e, k=k, n=n, c=c, kc=kc, kn_t=kn_t: e.matmul(
                                psum[:, k, 0:n], lhsT=wkv_t[:, c, kc * 128:(kc + 1) * 128], rhs=kn_t[:, c, 0:n],
                                start=(c == 0), stop=(c == NC8 - 1)),
                                reads=[wkv_b, kn_b], writes=[pb[k]], inc=(c == NC8 - 1))
                        P.op("act", lambda e, k=k, n=n, kf_t=kf_t: e.copy(out=kf_t[:, 0:n], in_=psum[:, k, 0:n]),
                             reads=[pb[k]], writes=[kf_b])

                    def k_stage2(kc, n=n, b0=b0, b1=b1, rt_t=rt_t, rt_b=rt_b):
                        kf_t, kf_b = kfs[kc]
                        t1_t, t1_b = t1s[kc]
                        t2_t, t2_b = t2s[kc]
                        k2 = next_bank()
                        P.op("pe", lambda e, k2=k2, n=n, kf_t=kf_t: e.matmul(psum[:, k2, 0:n], lhsT=rm_t[:, :], rhs=kf_t[:, 0:n],
                                                                            start=True, stop=True),
                             reads=[rm_b, kf_b], writes=[pb[k2]])
                        P.op("dve", lambda e, n=n, kf_t=kf_t, t1_t=t1_t, rt_t=rt_t: e.tensor_tensor(
                            out=t1_t[:, 0:n], in0=kf_t[:, 0:n], in1=rt_t[:, 0, 0:n], op=ALU.mult),
                            reads=[kf_b, rt_b], writes=[t1_b])
                        P.op("dve", lambda e, n=n, k2=k2, t2_t=t2_t, rt_t=rt_t: e.tensor_tensor(
                            out=t2_t[:, 0:n], in0=psum[:, k2, 0:n], in1=rt_t[:, 1, 0:n], op=ALU.mult),
                            reads=[pb[k2], rt_b], writes=[t2_b])
                        P.op("dve", lambda e, n=n, t1_t=t1_t, t2_t=t2_t, kc=kc, b0=b0, b1=b1: e.tensor_tensor(
                            out=KT[:, kc, b0:b1], in0=t1_t[:, 0:n], in1=t2_t[:, 0:n], op=ALU.add),
                            reads=[t1_b, t2_b], writes=[KT_b])

                    k_pending = [lambda: k_stage2(0), lambda: k_stage2(1)]
                    assert (b1 - b0 + 127) // 128 >= 2
                    if bi + 1 < len(blocks):
                        norm_rest(nbk, blocks[bi + 1][0], blocks[bi + 1][1], G_KV, kns[(bi + 1) % 2][0], kns[(bi + 1) % 2][1], 0)
                    for t0 in range(b0, b1, 128):
                        if k_pending:
                            k_pending.pop(0)()
                        m = min(128, b1 - t0)
                        vb = (t0 - 15) // 128
                        k = next_bank()
                        for c in range(NC8):
                            P.op("pe", lambda e, k=k, m=m, c=c, t0=t0, b0=b0, kn_t=kn_t: e.matmul(
                                psum[0:m, k, 0:256], lhsT=kn_t[:, c, t0 - b0:t0 - b0 + m], rhs=wkv_t[:, c, 256:512],
                                start=(c == 0), stop=(c == NC8 - 1)),
                                reads=[wkv_b, kn_b], writes=[pb[k]], inc=(c == NC8 - 1))
                        P.op("act", lambda e, k=k, m=m, vb=vb: e.copy(out=V_t[0:m, vb, :, 0:64], in_=psum[0:m, k, 0:256].rearrange("p (h d) -> p h d", h=4)),
                             reads=[pb[k]], writes=[V_b])
                        if t0 >= 2063 and KVOPT >= 2:
                            is_s = (m == NS)
                            o_t, o_b = (vso_t, vso_b) if is_s else (vo_t, vo_b)
                            P.op("dve", lambda e, k=k, m=m, o_t=o_t: e.tensor_copy(out=o_t[0:m, :], in_=psum[0:m, k, 0:256]),
                                 reads=[pb[k]], writes=[o_b])
                            if is_s:
                                P.dma("sp", "vws", vws_o[:, 127, :], o_t[0:m, :], reads=[o_b], writes=[vwsd_b])
                            else:
                                P.dma("sp", "vo", vwp_o[:, :], o_t[0:m, :], reads=[o_b])
                            if KVOPT < 3:
                                continue
                            k = next_bank()
                            for c in range(NC8):
                                P.op("pe", lambda e, k=k, m=m, c=c, t0=t0, b0=b0, kn_t=kn_t: e.matmul(
                                    psum[0:m, k, 0:256], lhsT=kn_t[:, c, t0 - b0:t0 - b0 + m], rhs=wkv_t[:, c, 0:256],
                                    start=(c == 0), stop=(c == NC8 - 1)),
                                    reads=[wkv_b, kn_b], writes=[pb[k]], inc=(c == NC8 - 1))
                            o_t, o_b = (kso_t, kso_b) if is_s else (ko_t, ko_b)
                            tb_t, tb_b = (rts_t, rts_b) if is_s else (rtm_t, rtm_b)
                            P.op("act", lambda e, k=k, m=m, o_t=o_t: e.copy(out=o_t[0:m, :], in_=psum[0:m, k, 0:256]),
                                 reads=[pb[k]], writes=[o_b])
                            kr_t, kr_b = (krs_t, krs_b) if is_s else (krm_t, krm_b)
                            P.op("dve", lambda e, k=k, m=m, kr_t=kr_t: e.tensor_copy(out=kr_t[0:m, :], in_=psum[0:m, k, 0:256]),
                                 reads=[pb[k]], writes=[kr_b])
                            pv = kr_t[0:m, :].rearrange("p (h d) -> p h d", h=4)
                            ov = o_t[0:m, :].rearrange("p (h d) -> p h d", h=4)
                            cosv = tb_t[0:m, 0:32].rearrange("p (h d) -> p h d", h=4)
                            sinv = tb_t[0:m, 32:64].rearrange("p (h d) -> p h d", h=4)
                            tmv = [t[0][0:m, :].rearrange("p (h d) -> p h d", h=4) for t in tm]
                            x1 = pv[:, :, 0:8]
                            x2 = pv[:, :, 8:16]
                            for (ti, xa, tab) in ((0, x1, cosv), (1, x2, sinv), (2, x2, cosv), (3, x1, sinv)) if KVOPT >= 4 else ():
                                P.op("dve", lambda e, ti=ti, xa=xa, tab=tab, tmv=tmv: e.tensor_tensor(
                                    out=tmv[ti], in0=xa, in1=tab, op=ALU.mult),
                                    reads=[kr_b, tb_b], writes=[tm[ti][1]])
                            if KVOPT >= 4:
                                P.op("dve", lambda e, ov=ov, tmv=tmv: e.tensor_tensor(out=ov[:, :, 0:8], in0=tmv[0], in1=tmv[1], op=ALU.subtract),
                                     reads=[tm[0][1], tm[1][1], o_b], writes=[o_b])
                                P.op("dve", lambda e, ov=ov, tmv=tmv: e.tensor_tensor(out=ov[:, :, 8:16], in0=tmv[2], in1=tmv[3], op=ALU.add),
                                     reads=[tm[2][1], tm[3][1], o_b], writes=[o_b])
                            if is_s:
                                P.dma("sp", "kws", kws_o[:, 127, :], o_t[0:m, :], reads=[o_b], writes=[kwsd_b])
                            else:
                                P.dma("sp", "ko", kwp_o[:, :], o_t[0:m, :], reads=[o_b])
                P.flush()

        if stages >= 2:
            pool_phase()
        if stages >= 3:
            ffn_phase(0, 1, 0, NCOL)
        kvst = ExitStack()
        top.enter_context(kvst)
        if stages >= 4:
            KT, KT_b = sb(kvst, "KT", [128, 2, NCOL], BF16)
            V_t, V_b = sb(kvst, "V", [128, 18, 4, 65], BF16)
            P.op("dve", lambda e: e.memset(V_t[:, :, :, 64:65], 1.0), writes=[V_b])
            kv_phase(KT, KT_b, V_t, V_b)


        def attn_phase(KT, KT_b, V_t, V_b):
            grow = G_N(1, 1)
            with ExitStack() as st:
                wq_t, wq_b = sb(st, "wq", [128, NC8, D], BF16)
                Wq = wq.rearrange("(c p) f -> p c f", p=128)
                wq_bs = [Buf(f"wq{qc}") for qc in range(8)]
                for qc in range(8):
                    for half in range(2):
                        head = QA[qc] if half == 0 else QB[qc]
                        P.dma("pool", f"wq{qc}", wq_t[:, :, qc * 128 + half * 64:qc * 128 + half * 64 + 64],
                              Wq[:, :, head * 64:(head + 1) * 64], writes=[])
                    wq_bs[qc].w = (f"wq{qc}", P.dcnt[f"wq{qc}"])
                rm_t, rm_b = sb(st, "rmat2", [128, 128], F32)
                P.dma("sp", "rmat2", rm_t[:], rmat_d[:, :], writes=[rm_b])
                mk_t, mk_b = sb(st, "mk", [128, 3, 128], F32)
                P.dma("sp", "mk", mk_t[:].rearrange("p a b -> p (a b)"), masks_d[:, :], writes=[mk_b])
                sk_t, sk_b = sb(st, "sinks", [128, 16], F32)
                P.dma("sp", "sinks", sk_t[:], sinks_d[:, :], writes=[sk_b])
                es_t, es_b = sb(st, "es", [128, 16], F32)
                P.op("act", lambda e: e.activation(out=es_t[:], in_=sk_t[:], func=AF.Exp), reads=[sk_b], writes=[es_b])
                identb, identb_b = sb(st, "identb", [128, 128], BF16)
                mb4, mb4_b = sb(st, "mb4", [128, 3, 4, 128], BF16)

                def build_masks():
                    P.op("dve", lambda e: e.tensor_copy(out=identb[:], in_=ident[:]), reads=[ident_b], writes=[identb_b])
                    for mi_ in range(3):
                        P.op("dve", lambda e, mi_=mi_: e.tensor_scalar(
                            out=mb4[:, mi_, :, :], in0=mk_t[:, mi_, :].unsqueeze(1).broadcast_to([128, 4, 128]),
                            scalar1=-1.0, scalar2=30000.0, op0=ALU.add, op1=ALU.mult), reads=[mk_b], writes=[mb4_b])
                hq_t, hq_b = sb(st, "hq", [128, NC8, 512], BF16)
                sq_t, sq_b = sb(st, "sqa", [128, NC8, 512], BF16)
                rs_t, rs_b = sb(st, "rsa", [128, 512], F32)
                rt_t, rt_b = sb(st, "art0", [128, 2, 512], F32)
                qfs = [sb(st, f"qf{k}", [128, 512], F32) for k in range(2)]
                t1s = [sb(st, f"at1{k}", [128, 512], F32) for k in range(1)]
                t2s = [sb(st, f"at2{k}", [128, 512], F32) for k in range(1)]
                QZ = [sb(st, f"QZ{k}", [128, 8, 512], BF16) for k in range(2)]
                for hfz in range(2):
                    P.op("pool", lambda e, hfz=hfz: e.memset(QZ[hfz][0][:], 0.0), writes=[QZ[hfz][1]])
                cnt = {"ri": 0, "et": 0, "pt": 0, "ot": 0}

                def q_norm(c0, c1):
                    n = c1 - c0
                    norm_block((sq_t, sq_b, rs_t, rs_b), c0, c1, grow, hq_t, hq_b, 0)
                    P.dma("sp", "art0", rt_t[:, :, 0:n], ropef_d[:, :, c0:c1], writes=[rt_b])

                def q_project(c0, c1):
                    q_norm(c0, c1)
                    q_stages(c0, c1)

                def q_stages(c0, c1):
                    n = c1 - c0

                    def stage1(qc):
                        qf_t, qf_b = qfs[qc % 2]
                        k = next_bank()
                        kc2, hh_ = qc // 4, qc % 4
                        for c in range(NC8):
                            P.op("pe", lambda e, k=k, c=c, kc2=kc2, hh_=hh_: e.matmul(
                                psum[:, k, 0:n],
                                lhsT=wq_t[:, c, (kc2 * 4 + hh_) * 128:(kc2 * 4 + hh_ + 1) * 128],
                                rhs=hq_t[:, c, 0:n], start=(c == 0), stop=(c == NC8 - 1)),
                                reads=[wq_bs[qc], hq_b], writes=[pb[k]], inc=(c == NC8 - 1))
                        P.op("act", lambda e, k=k, qf_t=qf_t: e.copy(out=qf_t[:, 0:n], in_=psum[:, k, 0:n]),
                             reads=[pb[k]], writes=[qf_b])

                    def stage2(qc):
                        qf_t, qf_b = qfs[qc % 2]
                        t1_t, t1_b = t1s[0]
                        t2_t, t2_b = t2s[0]
                        k2 = next_bank()
                        P.op("pe", lambda e, k2=k2, qf_t=qf_t: e.matmul(psum[:, k2, 0:n], lhsT=rm_t[:, :], rhs=qf_t[:, 0:n],
                                                                      start=True, stop=True),
                             reads=[rm_b, qf_b], writes=[pb[k2]])
                        P.op("pool", lambda e, qf_t=qf_t, t1_t=t1_t: e.tensor_tensor(
                            out=t1_t[:, 0:n], in0=qf_t[:, 0:n], in1=rt_t[:, 0, 0:n], op=ALU.mult),
                            reads=[qf_b, rt_b], writes=[t1_b])
                        P.op("dve", lambda e, k2=k2, t2_t=t2_t: e.tensor_tensor(
                            out=t2_t[:, 0:n], in0=psum[:, k2, 0:n], in1=rt_t[:, 1, 0:n], op=ALU.mult),
                            reads=[pb[k2], rt_b], writes=[t2_b])
                        for hfz in range(2):
                            r0_, r1_ = 64 * hfz, 64 * hfz + 64
                            P.op("dve", lambda e, t1_t=t1_t, t2_t=t2_t, qc=qc, hfz=hfz, r0_=r0_, r1_=r1_: e.tensor_tensor(
                                out=QZ[hfz][0][r0_:r1_, qc, 0:n], in0=t1_t[r0_:r1_, 0:n], in1=t2_t[r0_:r1_, 0:n], op=ALU.add),
                                reads=[t1_b, t2_b], writes=[QZ[hfz][1]])

                    for qc in range(8):
                        stage1(qc)
                        if qc >= 1:
                            stage2(qc - 1)
                    stage2(7)

                st2 = ExitStack()
                st.enter_context(st2)
                wo_t, wo_b = sb(st2, "wo", [128, NC8, D], BF16)
                P.dma("pool", "wo", wo_t[:], wo.rearrange("(c p) f -> p c f", p=128), writes=[wo_b])
                OTs = [sb(st2, f"OT{k}", [128, 4, 8, 128], BF16) for k in range(2)]
                pts = [sb(st2, f"pt{k}", [128, 512], BF16) for k in range(6)]
                ots = [sb(st2, f"otm{k}", [128, D], F32) for k in range(2)]
                dn_t, dn_b = sb(st2, "dn", [128, 16], F32)

                def o_project(c0, c1, w_t, w_b, rhs_of, OT_b):
                    for ch in o_project_chunks(c0, c1, w_t, w_b, rhs_of, OT_b):
                        ch()

                def o_project_chunks(c0, c1, w_t, w_b, rhs_of, OT_b):
                    return [(lambda dc=dc: o_project_dc(c0, c1, w_t, w_b, rhs_of, OT_b, dc)) for dc in range(NC8)]

                def o_project_dc(c0, c1, w_t, w_b, rhs_of, OT_b, dc):
                    n = c1 - c0
                    if True:
                        k = nbt()
                        for qc in range(8):
                            P.op("pe", lambda e, k=k, qc=qc, dc=dc: e.matmul(
                                psum[:, k, 0:n], lhsT=w_t[:, qc, dc * 128:(dc + 1) * 128], rhs=rhs_of(qc),
                                start=(qc == 0), stop=(qc == 7)),
                                reads=[w_b, OT_b], writes=[pb[k]], inc=(qc == 7))
                        P.op("dve", lambda e, k=k, dc=dc: e.tensor_tensor(
                            out=xT[:, dc, c0:c1], in0=psum[:, k, 0:n], in1=xT[:, dc, c0:c1], op=ALU.add),
                            reads=[pb[k], xTb[dc]], writes=[xTb[dc]])

                sbank = [0]
                vbank = [0]
                tbank = [0]

                def nbs():
                    sbank[0] = (sbank[0] + 1) % 4
                    return sbank[0]

                def nbv():
                    vbank[0] = (vbank[0] + 1) % 2
                    return 4 + vbank[0]

                def nbt():
                    tbank[0] = (tbank[0] + 1) % 2
                    return 6 + tbank[0]

                def emit_S(u):
                    sbi, i, kv = u
                    nb = sbi * 4 + i
                    qcol0 = CM0 + 512 * sbi + 128 * i
                    ql = 128 * i
                    kc, half = kv // 2, kv % 2
                    ptk = []
                    for kb in range(2):
                        kcol0 = qcol0 - 128 + 128 * kb
                        mi = (2 if nb == 0 else 0) if kb == 0 else 1
                        ks = nbs()
                        P.op("pe", lambda e, ks=ks, kc=kc, half=half, kcol0=kcol0, ql=ql: e.matmul(
                            psum[:, ks, :], lhsT=KT[:, kc, kcol0:kcol0 + 128],
                            rhs=QZ[half][0][:, 4 * kc:4 * kc + 4, ql:ql + 128], start=True, stop=False),
                            reads=[KT_b, QZ[half][1]], writes=[pb[ks]], inc=False)
                        P.op("pe", lambda e, ks=ks, mi=mi: e.matmul(
                            psum[:, ks, :], lhsT=identb[:, :], rhs=mb4[:, mi, :, :].rearrange("p h q -> p (h q)"),
                            start=False, stop=True),
                            reads=[identb_b, mb4_b], writes=[pb[ks]])
                        p_t, p_b = pts[cnt["pt"] % 6]
                        cnt["pt"] += 1
                        P.op("act", lambda e, ks=ks, p_t=p_t: e.activation(out=p_t[:, :], in_=psum[:, ks, :], func=AF.Exp, scale=0.125),
                             reads=[pb[ks]], writes=[p_b])
                        ptk.append((p_t, p_b))
                    return ptk

                def emit_V(u, ptk, o_t, o_b):
                    sbi, i, kv = u
                    OT_t, OT_b = OTs[sbi % 2]
                    nb = sbi * 4 + i
                    ko = nbv()
                    for hh in range(4):
                        for kb in range(2):
                            p_t, p_b = ptk[kb]
                            vb = nb + kb
                            P.op("pe", lambda e, ko=ko, hh=hh, kb=kb, p_t=p_t, vb=vb, kv=kv: e.matmul(
                                psum[:, ko, hh * 65:(hh + 1) * 65], lhsT=p_t[:, hh * 128:(hh + 1) * 128], rhs=V_t[:, vb, kv, :],
                                start=(kb == 0), stop=(kb == 1)),
                                reads=[p_b, V_b], writes=[pb[ko]], inc=(hh == 3 and kb == 1))
                    P.op("dve", lambda e, ko=ko, kv=kv: e.tensor_tensor(
                        out=dn_t[:, 4 * kv:4 * kv + 4],
                        in0=psum[:, ko, 0:260].rearrange("p (h e) -> p h e", e=65)[:, :, 64],
                        in1=es_t[:, 4 * kv:4 * kv + 4], op=ALU.add),
                        reads=[pb[ko], es_b], writes=[dn_b])
                    P.op("dve", lambda e, kv=kv: e.reciprocal(out=dn_t[:, 4 * kv:4 * kv + 4], in_=dn_t[:, 4 * kv:4 * kv + 4]),
                         reads=[dn_b], writes=[dn_b])
                    for hh in range(4):
                        h = 4 * kv + hh
                        if hh % 2 == 0:
                            P.op("dve", lambda e, ko=ko, hh=hh, h=h, o_t=o_t: e.tensor_scalar(
                                out=o_t[:, h * 64:(h + 1) * 64], in0=psum[:, ko, hh * 65:hh * 65 + 64],
                                scalar1=dn_t[:, h:h + 1], scalar2=None, op0=ALU.mult),
                                reads=[pb[ko], dn_b], writes=[o_b])
                        else:
                            P.op("act", lambda e, ko=ko, hh=hh, h=h, o_t=o_t: e.activation(
                                out=o_t[:, h * 64:(h + 1) * 64], in_=psum[:, ko, hh * 65:hh * 65 + 64],
                                func=AF.Copy, scale=dn_t[:, h:h + 1]),
                                reads=[pb[ko], dn_b], writes=[o_b])
                def emit_T(sbi, i, o_t, o_b):
                    OT_t, OT_b = OTs[sbi % 2]
                    if True:
                        for hf in range(2):
                            kT = nbt()
                            for cc in range(4):
                                qc = hf * 4 + cc
                                P.op("pe", lambda e, kT=kT, cc=cc, qc=qc, o_t=o_t: e.transpose(
                                    out=psum[:, kT, cc * 128:(cc + 1) * 128], in_=o_t[:, qc * 128:(qc + 1) * 128], identity=ident[:]),
                                    reads=[o_b, ident_b], writes=[pb[kT]], inc=(cc == 3))
                            P.op("act", lambda e, kT=kT, hf=hf, i=i: e.copy(
                                out=OT_t[:, i, hf * 4:hf * 4 + 4, :].rearrange("p c q -> p (c q)"), in_=psum[:, kT, :]),
                                reads=[pb[kT]], writes=[OT_b])

                q_norm(CM0, CM0 + 512)
                build_masks()
                q_stages(CM0, CM0 + 512)
                oproj_pend = []
                for sbi in range(4):
                    c0 = CM0 + 512 * sbi
                    c1 = c0 + 512
                    if sbi + 1 < 4:
                        q_norm(c1, c1 + 512)
                    units = [(sbi, i, kv) for i in range(4) for kv in range(4)]
                    otl = {}
                    for i in range(4):
                        otl[i] = ots[cnt["ot"] % 2]
                        cnt["ot"] += 1
                    pend = []
                    tpend = []
                    for ui, u in enumerate(units):
                        pend.append((u, emit_S(u)))
                        if len(pend) > PIPE:
                            u0, ptk0 = pend.pop(0)
                            if tpend and u0[2] == 1:
                                tpend.pop(0)()
                            emit_V(u0, ptk0, *otl[u0[1]])
                            if u0[2] == 3:
                                tpend.append(lambda u0=u0: emit_T(u0[0], u0[1], *otl[u0[1]]))
                        if ui % 2 == 1 and oproj_pend:
                            oproj_pend.pop(0)()
                    while pend:
                        u0, ptk0 = pend.pop(0)
                        if tpend and u0[2] == 1:
                            tpend.pop(0)()
                        emit_V(u0, ptk0, *otl[u0[1]])
                        if u0[2] == 3:
                            tpend.append(lambda u0=u0: emit_T(u0[0], u0[1], *otl[u0[1]]))
                    while tpend:
                        tpend.pop(0)()
                    while oproj_pend:
                        oproj_pend.pop(0)()
                    OTc_t, OTc_b = OTs[sbi % 2]
                    oproj_pend = o_project_chunks(c0, c1, wo_t, wo_b, lambda qc, OTc_t=OTc_t: OTc_t[:, :, qc, :], OTc_b)
                    if sbi + 1 < 4:
                        q_stages(c1, c1 + 512)
                while oproj_pend:
                    oproj_pend.pop(0)()

                P.flush()
                st2.close()
                if ATTN_S:
                    q_project(CS0, NCOL)
                    wos_t, wos_b = sb(st, "wos", [128, NC8, D], BF16)
                    for qc in range(8):
                        for half in range(2):
                            head = QA[qc] if half == 0 else QB[qc]
                            P.dma("pool", "wos", wos_t[half * 64:(half + 1) * 64, qc, :], wo[head * 64:(head + 1) * 64, :], writes=[])
                    wos_b.w = ("wos", P.dcnt["wos"])
                    esr_t, esr_b = sb(st, "esr", [128, NS, 16], BF16)
                    P.op("dve", lambda e: e.tensor_scalar(out=esr_t[:, :, :], in0=es_t[:, :].unsqueeze(1).broadcast_to([128, NS, 16]),
                                                          scalar1=1.0 / 128.0, scalar2=None, op0=ALU.mult),
                         reads=[es_b], writes=[esr_b])
                    OS_t, OS_b = sb(st, "OTs", [128, 8, NS], BF16)
                    SK_t, SK_b = sb(st, "SK", [128, NS, 256], F32)
                    SVb_t, SVb_b = sb(st, "SVb", [128, NS, 256], BF16)
                    SKT_t, SKT_b = sb(st, "SKT", [128, NS, 2, 128], BF16)
                    Es_t, Es_b = sb(st, "Es", [128, NS * 16], BF16)
                    os_t, os_b = sb(st, "os", [128, NS * 16], F32)
                    rdS_t, rdS_b = sb(st, "rdS", [128, NS * 16], F32)
                    P.dma("sp", "skl", SK_t[:, :, :], kws_o.rearrange("b k f -> k b f"), reads=[kwsd_b], writes=[SK_b])
                    P.dma("pool", "svl", SVb_t[:, :, :], vws_o.rearrange("b k f -> k b f"), reads=[vwsd_b], writes=[SVb_b])
                    for b in range(NS):
                        k = next_bank()
                        for kc in range(2):
                            P.op("pe", lambda e, k=k, b=b, kc=kc: e.transpose(
                                out=psum[:, k, kc * 128:(kc + 1) * 128], in_=SK_t[:, b, kc * 128:(kc + 1) * 128], identity=ident[:]),
                                reads=[SK_b, ident_b], writes=[pb[k]], inc=(kc == 1))
                        P.op("dve", lambda e, k=k, b=b: e.tensor_copy(
                            out=SKT_t[:, b, :, :].rearrange("p a k -> p (a k)"), in_=psum[:, k, 0:256]),
                            reads=[pb[k]], writes=[SKT_b])
                    kss = next_bank()
                    for b in range(NS):
                        for kv in range(4):
                            kc, half = kv // 2, kv % 2
                            P.op("pe", lambda e, b=b, kv=kv, kc=kc, half=half: e.matmul(
                                psum[:, kss, b * 16 + 4 * kv:b * 16 + 4 * kv + 4], lhsT=SKT_t[:, b, kc, :],
                                rhs=QZ[half][0][:, 4 * kc:4 * kc + 4, b], start=True, stop=True),
                                reads=[SKT_b, QZ[half][1]], writes=[pb[kss]], inc=(b == NS - 1 and kv == 3))
                    P.op("act", lambda e: e.activation(out=Es_t[:, :], in_=psum[:, kss, 0:NS * 16], func=AF.Exp, scale=0.125),
                         reads=[pb[kss]], writes=[Es_b])
                    kds = next_bank()
                    P.op("pe", lambda e: e.matmul(psum[:, kds, 0:NS * 16], lhsT=ones1[:, :], rhs=Es_t[:, :], start=True, stop=False),
                         reads=[ones1_b, Es_b], writes=[pb[kds]], inc=False)
                    P.op("pe", lambda e: e.matmul(psum[:, kds, 0:NS * 16], lhsT=ones1[:, :], rhs=esr_t[:, :, :], start=False, stop=True),
                         reads=[ones1_b, esr_b], writes=[pb[kds]])
                    kos = next_bank()
                    for b in range(NS):
                        for kv in range(4):
                            kc = kv // 2
                            P.op("pe", lambda e, b=b, kv=kv, kc=kc: e.matmul(
                                psum[:, kos, b * 16 + 4 * kv:b * 16 + 4 * kv + 4], lhsT=SVb_t[:, b, kc * 128:(kc + 1) * 128],
                                rhs=Es_t[:, b * 16 + 4 * kv:b * 16 + 4 * kv + 4], start=True, stop=True),
                                reads=[SVb_b, Es_b], writes=[pb[kos]], inc=(b == NS - 1 and kv == 3))
                    P.op("dve", lambda e: e.reciprocal(out=rdS_t[:, :], in_=psum[:, kds, 0:NS * 16]), reads=[pb[kds]], writes=[rdS_b])
                    P.op("act", lambda e: e.copy(out=os_t[:, :], in_=psum[:, kos, 0:NS * 16]), reads=[pb[kos]], writes=[os_b])
                    for kv in range(4):
                        kc, half = kv // 2, kv % 2
                        p0_, p1_ = 64 * half, 64 * half + 64
                        P.op("dve", lambda e, kv=kv, kc=kc, p0_=p0_, p1_=p1_: e.tensor_tensor(
                            out=OS_t[p0_:p1_, 4 * kc:4 * kc + 4, :],
                            in0=os_t[p0_:p1_, :].rearrange("p (b v h) -> p v h b", v=4, h=4)[:, kv],
                            in1=rdS_t[p0_:p1_, :].rearrange("p (b v h) -> p v h b", v=4, h=4)[:, kv], op=ALU.mult),
                            reads=[os_b, rdS_b], writes=[OS_b])
                    o_project(CS0, NCOL, wos_t, wos_b, lambda qc: OS_t[:, qc, :], OS_b)
                P.flush()

        if stages >= 5:
            ffn_phase(1, 0, CM0, NCOL)
        if stages >= 6:
            attn_phase(KT, KT_b, V_t, V_b)
        kvst.close()
        if stages >= 7:
            ffn_phase(1, 1, CM0, NCOL)

        with ExitStack() as st:
            yTs = [sb(st, f"yT{k}", [128, NC8, 512], F32) for k in range(2)]
            sq_t, sq_b = sb(st, "sqf", [128, NC8, 512], BF16)
            rs_t, rs_b = sb(st, "rsf", [128, 512], F32)
            yst = [sb(st, f"yst{k}", [128, D], F32) for k in range(3)]
            yi = 0
            fblocks = split(CM0, NCOL, 5)
            nbf = (sq_t, sq_b, rs_t, rs_b)
            norm_block(nbf, fblocks[0][0], fblocks[0][1], G_FIN, yTs[0][0], yTs[0][1], 0)
            for fbi, (b0, b1) in enumerate(fblocks):
                yT_t, yT_b = yTs[fbi % 2]
                if fbi + 1 < len(fblocks):
                    norm_sq(nbf, fblocks[fbi + 1][0], fblocks[fbi + 1][1])
                for t0 in range(b0, b1, 128):
                    if t0 == b0 + 128 and fbi + 1 < len(fblocks):
                        norm_rest(nbf, fblocks[fbi + 1][0], fblocks[fbi + 1][1], G_FIN, yTs[(fbi + 1) % 2][0], yTs[(fbi + 1) % 2][1], 0)
                    t1 = min(b1, t0 + 128)
                    rows = t1 - t0
                    y_t, y_b = yst[yi % 3]
                    for hf in range(2):
                        k = next_bank()
                        for cc in range(4):
                            c = hf * 4 + cc
                            P.op("pe", lambda e, k=k, cc=cc, c=c, t0=t0, t1=t1, b0=b0, rows=rows, yT_t=yT_t: e.transpose(
                                out=psum[0:rows, k, cc * 128:(cc + 1) * 128], in_=yT_t[:, c, t0 - b0:t1 - b0], identity=ident[:]),
                                reads=[yT_b, ident_b], writes=[pb[k]], inc=(cc == 3))
                        if hf == 0:
                            P.op("dve", lambda e, y_t=y_t, k=k, rows=rows, hf=hf: e.tensor_copy(
                                out=y_t[0:rows, hf * 512:(hf + 1) * 512], in_=psum[0:rows, k, :]),
                                reads=[pb[k]], writes=[y_b])
                        else:
                            P.op("act", lambda e, y_t=y_t, k=k, rows=rows, hf=hf: e.copy(
                                out=y_t[0:rows, hf * 512:(hf + 1) * 512], in_=psum[0:rows, k, :]),
                                reads=[pb[k]], writes=[y_b])
                    P.dma("sp", f"yst{yi % 3}", y_o[t0 - CM0:t1 - CM0, :], y_t[0:rows, :], reads=[y_b])
                    yi += 1
            if debug:
                P.dma("sp", "dbg", dbg_o[:, :], xT[:].rearrange("p c n -> p (c n)"), reads=xTb)
            P.wait_all_dma("sp")
            P.flush()
    return nc


def _rope_tables(pos_cols):
    half = 8
    inv = np.power(np.float32(ROPE_THETA), -np.arange(half, dtype=np.float32) * np.float32(2.0 / 16)).astype(np.float32)
    ang = pos_cols.astype(np.float32)[:, None] * inv[None, :]
    return np.cos(ang).astype(np.float32), np.sin(ang).astype(np.float32)


def make_core_inputs(core, inputs):
    seq, half = core // 2, core % 2
    p0 = half * MAIN
    xp = inputs["x_prompt"]
    xs = inputs["x_sample"]
    b0 = core * NS
    xin = np.zeros((NCOL, D), np.float32)
    if p0 > 0:
        xin[0:HALO] = xp[seq, p0 - HALO:p0]
    xin[CM0:CS0] = xp[seq, p0:p0 + MAIN]
    xin[CS0:] = xs[b0:b0 + NS, 0]
    m = {"xin": xin}
    vecs = [inputs["norm_g"][l, i] for l in range(2) for i in range(3)] + [
        inputs["kv_norm_g"], inputs["final_norm_g"], inputs["pool_scale"][0]]
    g = np.stack(vecs, 0).reshape(9, 8, 128)
    m["gains"] = np.ascontiguousarray(g.transpose(2, 0, 1).reshape(128, 72)).astype(np.float32)
    m["sinks"] = np.ascontiguousarray(np.broadcast_to(inputs["attn_sinks"][0][None, :], (128, 16))).astype(np.float32)
    m["state"] = np.ascontiguousarray(inputs["state_pool"][0, b0:b0 + NS].reshape(NS * 15, D))
    m["ck"] = np.ascontiguousarray(inputs["cache_k_win"][b0:b0 + NS].reshape(NS, 128, 256))
    m["cv"] = np.ascontiguousarray(inputs["cache_v_win"][b0:b0 + NS].reshape(NS, 128, 256))
    pos = np.concatenate([p0 - HALO + np.arange(HALO), p0 + np.arange(MAIN), np.full(NS, PAST_LEN)]).astype(np.int64)
    cos, sin = _rope_tables(pos)
    ropef = np.zeros((128, 2, NCOL), np.float32)
    ropef[:, 0, :] = 1.0
    for hb in (0, 64):
        for i in range(16):
            ropef[hb + i, 0, :] = cos[:, i % 8]
            ropef[hb + i, 1, :] = sin[:, i % 8]
    m["ropef"] = ropef
    ropet = np.zeros((144, 2, 4, 8), np.float32)
    ropet[:, 0] = cos[NCOL - 144:, None, :]
    ropet[:, 1] = sin[NCOL - 144:, None, :]
    m["ropet"] = ropet.reshape(144, 2, 32)
    kk = np.arange(128)[:, None]
    qq = np.arange(128)[None, :]
    mprev = (kk > qq).astype(np.float32)
    mcur = (kk <= qq).astype(np.float32)
    mprev0 = mprev if p0 > 0 else np.zeros_like(mprev)
    m["masks"] = np.ascontiguousarray(np.concatenate([mprev, mcur, mprev0], 1))
    pc = np.ones((128, 4, 15), np.float32)
    if p0 == 0:
        for gi, w in enumerate((2, 4, 8, 16)):
            for t in range(15):
                pc[:, gi, t] = w / min(w, t + 1)
    m["pcorr"] = pc.reshape(128, 60)
    m["ident"] = np.eye(128, dtype=np.float32)
    rm = np.zeros((128, 128), np.float32)
    for hb in (0, 64):
        for i in range(8):
            rm[hb + i + 8, hb + i] = -1.0
            rm[hb + i, hb + i + 8] = 1.0
    m["rmat"] = rm
    return m


_NC_CACHE = {}


def kernel(x_prompt, x_sample, state_pool, cache_k_win, cache_v_win, norm_g, ffn_w_gate, ffn_w_up,
           ffn_w_down, pool_w, pool_scale, kv_norm_g, w_kv, w_q, w_o, attn_sinks, final_norm_g,
           _stages=99, _debug=False):
    inputs = dict(x_prompt=np.asarray(x_prompt), x_sample=np.asarray(x_sample), state_pool=np.asarray(state_pool),
                  cache_k_win=np.asarray(cache_k_win), cache_v_win=np.asarray(cache_v_win), norm_g=np.asarray(norm_g),
                  pool_scale=np.asarray(pool_scale), kv_norm_g=np.asarray(kv_norm_g), attn_sinks=np.asarray(attn_sinks),
                  final_norm_g=np.asarray(final_norm_g))
    shared = {
        "wg": np.ascontiguousarray(np.asarray(ffn_w_gate, dtype=np.float32)),
        "wu": np.ascontiguousarray(np.asarray(ffn_w_up, dtype=np.float32)),
        "wd": np.ascontiguousarray(np.asarray(ffn_w_down, dtype=np.float32)),
        "pw": np.ascontiguousarray(np.asarray(pool_w, dtype=np.float32)[0]),
        "wkv": np.ascontiguousarray(np.asarray(w_kv, dtype=np.float32)),
        "wq": np.ascontiguousarray(np.asarray(w_q, dtype=np.float32)[0]),
        "wo": np.ascontiguousarray(np.asarray(w_o, dtype=np.float32)[0]),
    }
    key = (_stages, _debug)
    if key not in _NC_CACHE:
        _NC_CACHE[key] = build_program(_stages, _debug)
    nc = _NC_CACHE[key]
    in_maps = []
    for core in range(8):
        m = make_core_inputs(core, inputs)
        m.update(shared)
        in_maps.append(m)
    res = run_bass_kernel_spmd(nc, in_maps, core_ids=list(range(8)))
    R = res.results
    y_prompt = np.zeros((4, 4096, D), np.float32)
    y_sample = np.zeros((128, 1, D), np.float32)
    pool_prompt = np.zeros((1, 4, 15, D), np.float32)
    pool_sample = np.zeros((1, 128, 15, D), np.float32)
    kwp = np.zeros((4, 128, 4, 64), np.float32)
    vwp = np.zeros((4, 128, 4, 64), np.float32)
    kws = np.zeros((128, 128, 4, 64), np.float32)
    vws = np.zeros((128, 128, 4, 64), np.float32)
    for core in range(8):
        seq, half = core // 2, core % 2
        p0 = half * MAIN
        b0 = core * NS
        r = R[core]
        y_prompt[seq, p0:p0 + MAIN] = r["y"][0:MAIN]
        y_sample[b0:b0 + NS, 0] = r["y"][MAIN:MAIN + NS]
        pool_sample[0, b0:b0 + NS] = r["pool_s"]
        kws[b0:b0 + NS] = r["kws"].reshape(NS, 128, 4, 64)
        vws[b0:b0 + NS] = r["vws"].reshape(NS, 128, 4, 64)
        if half == 1:
            pool_prompt[0, seq] = r["pool_o"][0:15]
            kwp[seq] = r["kwp"].reshape(128, 4, 64)
            vwp[seq] = r["vwp"].reshape(128, 4, 64)
    if _debug:
        return (y_prompt, y_sample, pool_prompt, pool_sample, kwp, vwp, kws, vws), R
    return (y_prompt, y_sample, pool_prompt, pool_sample, kwp, vwp, kws, vws)
```

```python
import numpy as np
from contextlib import ExitStack
import concourse.bass as bass
import concourse.mybir as mybir
from concourse.bass_utils import run_bass_kernel_spmd

F32 = mybir.dt.float32
BF16 = mybir.dt.bfloat16
AF = mybir.ActivationFunctionType
ALU = mybir.AluOpType
AX = mybir.AxisListType

D = 1024
DFF = 2816
NJ = DFF // 128
NC8 = 8
HALO = 143
MAIN = 2048
NS = 16
NCOL = HALO + MAIN + NS
CM0 = HALO
CS0 = HALO + MAIN
EPS = 1e-5
PAST_LEN = 16384
ROPE_THETA = 500000.0
QA = [0, 1, 2, 3, 8, 9, 10, 11]
QB = [4, 5, 6, 7, 12, 13, 14, 15]
G_N = lambda l, i: l * 3 + i
G_KV, G_FIN, G_PS = 6, 7, 8
import os
KVOPT = int(os.environ.get('KVOPT', '9'))
ATTN_S = int(os.environ.get('ATTN_S', '1'))
PIPE = int(os.environ.get('PIPE', '1'))
POOLPE = int(os.environ.get('POOLPE', '1'))
POOLC = int(os.environ.get('POOLC', '8'))
ENGS = ["pe", "act", "dve", "pool", "sp"]


def split(a, b, n):
    tot = b - a
    out = []
    s = a
    for i in range(n):
        e = a + (tot * (i + 1)) // n
        out.append((s, e))
        s = e
    return out


class Buf:
    __slots__ = ("name", "w", "r", "excl")

    def __init__(self, name, excl=False):
        self.name = name
        self.w = None
        self.r = {}
        self.excl = excl


class Prog:
    def __init__(self, nc, stack):
        self.nc = nc
        self.stack = stack
        self.sems = {e: stack.enter_context(nc.semaphore("q_" + e)) for e in ENGS}
        self.cnt = {e: 0 for e in ENGS}
        self.seen = {e: {} for e in ENGS}
        self.ops = {e: [] for e in ENGS}
        self.dsem = {}
        self.dcnt = {}

    def semobj(self, k):
        return self.sems[k] if k in self.sems else self.dsem[k]

    def _waits(self, eng, reads, writes):
        deps = {}

        def add(tok, same_ok):
            if tok is None:
                return
            k, v = tok
            if k == eng and not same_ok:
                return
            if deps.get(k, 0) < v:
                deps[k] = v

        for b in reads:
            add(b.w, True)
            if b.excl:
                for k, v in b.r.items():
                    add((k, v), False)
        for b in writes:
            add(b.w, False)
            for k, v in b.r.items():
                add((k, v), False)
        waits = []
        for k, v in deps.items():
            if self.seen[eng].get(k, 0) >= v:
                continue
            self.seen[eng][k] = v
            waits.append((k, v))
        return waits

    def op(self, eng, fn, reads=(), writes=(), inc=True):
        waits = self._waits(eng, reads, writes)
        val = self.cnt[eng] + 1
        if inc:
            self.cnt[eng] = val
        for b in reads:
            if b.r.get(eng, 0) < val:
                b.r[eng] = val
        for b in writes:
            b.w = (eng, val)
            b.r = {}
        self.ops[eng].append((waits, fn, inc))

    def dma(self, eng, sem, out, in_, reads=(), writes=()):
        if sem not in self.dsem:
            self.dsem[sem] = self.stack.enter_context(self.nc.semaphore("d_" + sem))
            self.dcnt[sem] = 0
        waits = self._waits(eng, reads, writes)
        self.dcnt[sem] += 16
        val = self.dcnt[sem]
        for b in reads:
            if b.r.get(sem, 0) < val:
                b.r[sem] = val
        for b in writes:
            b.w = (sem, val)
            b.r = {}
        so = self.dsem[sem]

        def fn(e, out=out, in_=in_, so=so):
            e.dma_start(out=out, in_=in_).then_inc(so, 16)
            return None

        self.ops[eng].append((waits, fn, False))

    def wait_all_dma(self, eng):
        waits = []
        for k, v in self.dcnt.items():
            if self.seen[eng].get(k, 0) < v:
                self.seen[eng][k] = v
                waits.append((k, v))
        self.ops[eng].append((waits, None, False))

    def flush(self):
        self.wait_all_dma("sp")
        with self.nc.Block() as blk:
            decos = {"pe": blk.tensor, "act": blk.scalar, "dve": blk.vector,
                     "pool": blk.gpsimd, "sp": blk.sync}
            for eng in ENGS:
                ops = self.ops[eng]
                sem = self.sems[eng]

                def body(e, ops=ops, sem=sem):
                    for waits, fn, inc in ops:
                        for k, v in waits:
                            e.wait_ge(self.semobj(k), v)
                        if fn is None:
                            continue
                        ins = fn(e)
                        if inc:
                            ins.then_inc(sem, 1)

                decos[eng](body)
                self.ops[eng] = []


def build_program(stages=99, debug=False):
    nc = bass.Bass("TRN2", target_bir_lowering=False)

    def din(name, shape, dt=F32):
        return nc.dram_tensor(name, list(shape), dt, kind="ExternalInput").ap()

    def dout(name, shape, dt=F32):
        return nc.dram_tensor(name, list(shape), dt, kind="ExternalOutput").ap()

    xin = din("xin", [NCOL, D])
    wg = din("wg", [2, 2, D, DFF])
    wu = din("wu", [2, 2, D, DFF])
    wd = din("wd", [2, 2, DFF, D])
    pw = din("pw", [4, 256, 256])
    wkv = din("wkv", [D, 512])
    wq = din("wq", [D, D])
    wo = din("wo", [D, D])
    gains_d = din("gains", [128, 9 * 8])
    sinks_d = din("sinks", [128, 16])
    state_d = din("state", [NS * 15, D])
    ck_d = din("ck", [NS, 128, 256])
    cv_d = din("cv", [NS, 128, 256])
    ropef_d = din("ropef", [128, 2, NCOL])
    ropet_d = din("ropet", [144, 2, 4 * 8])
    masks_d = din("masks", [128, 3 * 128])
    pcorr_d = din("pcorr", [128, 4 * 15])
    ident_d = din("ident", [128, 128])
    rmat_d = din("rmat", [128, 128])

    y_o = dout("y", [MAIN + NS, D])
    poolo_o = dout("pool_o", [31, D])
    pools_o = dout("pool_s", [NS, 15, D])
    kwp_o = dout("kwp", [128, 256])
    vwp_o = dout("vwp", [128, 256])
    kws_o = dout("kws", [NS, 128, 256])
    vws_o = dout("vws", [NS, 128, 256])
    dbg_o = dout("dbg", [128, 8 * NCOL]) if debug else None

    with ExitStack() as top:
        P = Prog(nc, top)

        uid = [0]

        def sb(st, name, shape, dt):
            uid[0] += 1
            t = st.enter_context(nc.sbuf_tensor(f"s{uid[0]}_{name}", list(shape), dt))
            return t, Buf(name)

        xT, xT_b = sb(top, "xT", [128, NC8, NCOL], F32)
        xTb = [Buf(f"xT{c}") for c in range(NC8)]
        gains, gains_b = sb(top, "gains", [128, 9, 8], F32)
        ident, ident_b = sb(top, "ident", [128, 128], F32)
        onesm, onesm_b = sb(top, "onesm", [128, 128], BF16)
        ones1, ones1_b = sb(top, "ones1", [128, 128], BF16)
        epst, epst_b = sb(top, "epst", [128, 1], F32)
        psum = top.enter_context(nc.psum_tensor("ps", [128, 8, 512], F32))
        pb = [Buf(f"bank{k}", excl=True) for k in range(8)]
        bank_rr = [0]

        def next_bank():
            k = bank_rr[0]
            bank_rr[0] = (k + 1) % 8
            return k

        P.dma("sp", "c_gains", gains[:].rearrange("p a b -> p (a b)"), gains_d[:, :], writes=[gains_b])
        P.dma("sp", "c_ident", ident[:], ident_d[:, :], writes=[ident_b])
        P.op("dve", lambda e: e.memset(onesm[:], 1.0 / 1024.0), writes=[onesm_b])
        P.op("dve", lambda e: e.memset(ones1[:], 1.0), writes=[ones1_b])
        P.op("dve", lambda e: e.memset(epst[:], EPS), writes=[epst_b])

        def p0_tiles(st):
            NST = 3
            stg = [sb(st, f"stg{k}", [128, D], F32) for k in range(NST)]
            nrt = (NCOL + 127) // 128
            out = []
            for r in range(nrt):
                out.append((r * 128, lambda r=r: p0_tile(stg, r)))
            return out

        def p0_tile(stg, r):
            NST = 3
            if True:
                r0 = r * 128
                rows = min(128, NCOL - r0)
                s_t, s_b = stg[r % NST]
                P.dma("sp", f"stg{r % NST}", s_t[0:rows, :], xin[r0:r0 + rows, :], writes=[s_b])
                for hf in range(2):
                    k = next_bank()
                    for cc in range(4):
                        c = hf * 4 + cc
                        P.op("pe", lambda e, k=k, cc=cc, c=c, s_t=s_t, rows=rows: e.transpose(
                            out=psum[:, k, cc * 128:cc * 128 + rows], in_=s_t[0:rows, c * 128:(c + 1) * 128],
                            identity=ident[0:rows, 0:rows]),
                            reads=[s_b, ident_b], writes=[pb[k]], inc=(cc == 3))
                    src = psum[:, k, :].rearrange("p (c n) -> p c n", c=4)[:, :, 0:rows]
                    dst = xT[:, hf * 4:hf * 4 + 4, r0:r0 + rows]
                    wr = [xTb[c] for c in range(hf * 4, hf * 4 + 4)]
                    if (r + hf) % 2 == 0:
                        P.op("dve", lambda e, dst=dst, src=src: e.tensor_copy(out=dst, in_=src),
                             reads=[pb[k]], writes=wr)
                    else:
                        P.op("act", lambda e, dst=dst, src=src: e.copy(out=dst, in_=src),
                             reads=[pb[k]], writes=wr)

        def norm_block(st_bufs, c0, c1, grow, out_t, out_b, out_off):
            norm_sq(st_bufs, c0, c1)
            norm_rest(st_bufs, c0, c1, grow, out_t, out_b, out_off)

        def norm_sq(st_bufs, c0, c1):
            sq_t, sq_b, rs_t, rs_b = st_bufs
            n = c1 - c0
            P.op("act", lambda e: e.activation(out=sq_t[:, :, 0:n], in_=xT[:, :, c0:c1], func=AF.Square),
                 reads=xTb, writes=[sq_b])

        def norm_rest(st_bufs, c0, c1, grow, out_t, out_b, out_off):
            sq_t, sq_b, rs_t, rs_b = st_bufs
            n = c1 - c0
            k = next_bank()
            for c in range(NC8):
                P.op("pe", lambda e, c=c, k=k: e.matmul(psum[:, k, 0:n], lhsT=onesm[:], rhs=sq_t[:, c, 0:n],
                                                        start=(c == 0), stop=(c == NC8 - 1)),
                     reads=[sq_b, onesm_b], writes=[pb[k]], inc=(c == NC8 - 1))
            P.op("act", lambda e, k=k: e.activation(out=rs_t[:, 0:n], in_=psum[:, k, 0:n], func=AF.Ln,
                                                    bias=epst[:, 0:1], scale=1.0),
                 reads=[pb[k], epst_b], writes=[rs_b])
            P.op("act", lambda e: e.activation(out=rs_t[:, 0:n], in_=rs_t[:, 0:n], func=AF.Exp, scale=-0.5),
                 reads=[rs_b], writes=[rs_b])
            for c in range(NC8):
                P.op("dve", lambda e, c=c: e.scalar_tensor_tensor(
                    out=out_t[:, c, out_off:out_off + n], in0=xT[:, c, c0:c1], scalar=gains[:, grow, c:c + 1],
                    in1=rs_t[:, 0:n], op0=ALU.mult, op1=ALU.mult),
                    reads=[xTb[c], rs_b, gains_b], writes=[out_b])

        def ffn_phase(l, i, ca, cb, ngroups=3, nblk=2, pre_tiles=None):
            grow = G_N(l, 0 if i == 0 else 2)
            groups = [split(a, b, nblk) for (a, b) in split(ca, cb, ngroups)]
            gmax = max(g[-1][1] - g[0][0] for g in groups)
            bmax = max(b1 - b0 for g in groups for (b0, b1) in g)
            Wg = wg[l, i].rearrange("(c p) f -> p c f", p=128)
            Wu = wu[l, i].rearrange("(c p) f -> p c f", p=128)
            Wd = wd[l, i].rearrange("(j p) d -> p j d", p=128)
            with ExitStack() as st:
                xn_t, xn_b = sb(st, "xn", [128, NC8, gmax], BF16)
                h_t, h_b0 = sb(st, "h", [128, NJ, gmax], BF16)
                hb = [Buf(f"h{j}") for j in range(NJ)]
                NGU = 5
                gu = [sb(st, f"gu{k}", [128, 2, NC8, 128], BF16) for k in range(NGU)]
                wdr = [sb(st, f"wd{k}", [128, NJ, 256], BF16) for k in range(2)]
                sg = [sb(st, f"sg{k}", [128, bmax], F32) for k in range(4)]
                rs_t, rs_b = sb(st, "rs", [128, bmax], F32)
                nbs_ = []
                for t_ in range(nblk):
                    sq_t, sq_b = sb(st, f"sq{t_}", [128, NC8, bmax], BF16)
                    nbs_.append((sq_t, sq_b, rs_t, rs_b))
                sgi = [0]
                gu_loads = [(g, j) for g in range(ngroups) for j in range(NJ)]
                wd_loads = [(g, cp) for g in range(ngroups) for cp in range(4)]
                gu_next = [0]
                wd_next = [0]

                def issue_gu():
                    idx = gu_next[0]
                    if idx >= len(gu_loads):
                        return
                    gu_next[0] += 1
                    g, j = gu_loads[idx]
                    t, b = gu[idx % NGU]
                    P.dma("pool", f"gu{idx % NGU}", t[:, 0, :, :], Wg[:, :, j * 128:(j + 1) * 128], writes=[b])
                    P.dma("pool", f"gu{idx % NGU}", t[:, 1, :, :], Wu[:, :, j * 128:(j + 1) * 128], writes=[])
                    b.w = (f"gu{idx % NGU}", P.dcnt[f"gu{idx % NGU}"])

                def issue_wd():
                    idx = wd_next[0]
                    if idx >= len(wd_loads):
                        return
                    wd_next[0] += 1
                    g, cp = wd_loads[idx]
                    t, b = wdr[idx % 2]
                    P.dma("pool", f"wd{idx % 2}", t[:, :, :], Wd[:, :, cp * 256:(cp + 1) * 256], writes=[b])

                for _ in range(NGU):
                    issue_gu()

                def do_norm_sq(g):
                    for t_, (b0, b1) in enumerate(groups[g]):
                        norm_sq(nbs_[t_], b0, b1)

                def do_norm_rest(g):
                    off0 = groups[g][0][0]
                    for t_, (b0, b1) in enumerate(groups[g]):
                        norm_rest(nbs_[t_], b0, b1, grow, xn_t, xn_b, b0 - off0)

                pend_tiles = []
                if pre_tiles is not None:
                    g0_end = groups[0][-1][1]
                    for (r0_, em) in pre_tiles(st):
                        if r0_ < g0_end:
                            em()
                        else:
                            pend_tiles.append(em)
                do_norm_sq(0)
                do_norm_rest(0)
                gu_idx = 0
                wd_idx = 0
                for g in range(ngroups):
                    off0 = groups[g][0][0]
                    blks = groups[g]
                    for j in range(NJ):
                        gt, gb = gu[gu_idx % NGU]
                        gu_idx += 1
                        base = (j % 2) * 4
                        for which in range(2):
                            for t, (b0, b1) in enumerate(blks):
                                k = base + which * 2 + t
                                n = b1 - b0
                                for c in range(NC8):
                                    P.op("pe", lambda e, k=k, n=n, gt=gt, which=which, c=c, b0=b0, b1=b1, off0=off0: e.matmul(
                                        psum[:, k, 0:n], lhsT=gt[:, which, c, :], rhs=xn_t[:, c, b0 - off0:b1 - off0],
                                        start=(c == 0), stop=(c == NC8 - 1)),
                                        reads=[gb, xn_b], writes=[pb[k]], inc=(c == NC8 - 1))
                        issue_gu()
                        if g == 0 and j == 5:
                            issue_wd()
                            issue_wd()
                        for t, (b0, b1) in enumerate(blks):
                            n = b1 - b0
                            kg = base + t
                            ku = base + 2 + t
                            s_t, s_b = sg[sgi[0] % 4]
                            sgi[0] += 1
                            P.op("act", lambda e, s_t=s_t, kg=kg, n=n: e.activation(out=s_t[:, 0:n], in_=psum[:, kg, 0:n], func=AF.Silu),
                                 reads=[pb[kg]], writes=[s_b])
                            P.op("dve", lambda e, s_t=s_t, ku=ku, n=n, j=j, b0=b0, b1=b1, off0=off0: e.tensor_tensor(
                                out=h_t[:, j, b0 - off0:b1 - off0], in0=psum[:, ku, 0:n], in1=s_t[:, 0:n], op=ALU.mult),
                                reads=[pb[ku], s_b], writes=[hb[j]])
                        if pend_tiles:
                            pend_tiles.pop(0)()
                    while pend_tiles:
                        pend_tiles.pop(0)()
                    if g + 1 < ngroups:
                        do_norm_sq(g + 1)
                    for cp in range(4):
                        wt, wb = wdr[wd_idx % 2]
                        wd_idx += 1
                        for cc in range(2):
                            c = cp * 2 + cc
                            for t, (b0, b1) in enumerate(blks):
                                n = b1 - b0
                                k = next_bank()
                                for j in range(NJ):
                                    P.op("pe", lambda e, k=k, n=n, wt=wt, j=j, cc=cc, b0=b0, b1=b1, off0=off0: e.matmul(
                                        psum[:, k, 0:n], lhsT=wt[:, j, cc * 128:(cc + 1) * 128], rhs=h_t[:, j, b0 - off0:b1 - off0],
                                        start=(j == 0), stop=(j == NJ - 1)),
                                        reads=[wb, hb[j]], writes=[pb[k]], inc=(j == NJ - 1))
                                P.op("dve", lambda e, k=k, n=n, c=c, b0=b0, b1=b1: e.scalar_tensor_tensor(
                                    out=xT[:, c, b0:b1], in0=psum[:, k, 0:n], scalar=0.5, in1=xT[:, c, b0:b1],
                                    op0=ALU.mult, op1=ALU.add),
                                    reads=[pb[k], xTb[c]], writes=[xTb[c]])
                        issue_wd()
                        if cp == 0 and g + 1 < ngroups:
                            do_norm_rest(g + 1)
                P.flush()

        if stages >= 1:
            ffn_phase(0, 0, 0, NCOL, pre_tiles=p0_tiles)

        def pool_phase():
            grow = G_N(0, 1)
            blocks = split(15, NCOL, 7)
            PW = 336
            with ExitStack() as st:
                pwt, pw_b = sb(st, "pw", [128, 4, 2, 256], BF16)
                P.dma("pool", "pw", pwt[:].rearrange("p g c d -> p (g c) d"),
                      pw.rearrange("g (c p) d -> p (g c) d", p=128), writes=[pw_b])
                pc_t, pc_b = sb(st, "pc", [128, 4, 15], F32)
                P.dma("sp", "pc", pc_t[:].rearrange("p g t -> p (g t)"), pcorr_d[:, :], writes=[pc_b])
                hbs = [sb(st, f"hb{k}", [128, NC8, PW], F32) for k in range(2)]
                tsets = []
                for k_ in range(2):
                    tA_, _ = sb(st, f"tA{k_}", [128, NC8, PW], F32)
                    tB_, _ = sb(st, f"tB{k_}", [128, NC8, PW], F32)
                    db_, _ = sb(st, f"db{k_}", [128, NC8, PW], BF16)
                    tsets.append((tA_, [Buf(f"tA{k_}_{c}") for c in range(NC8)], tB_, [Buf(f"tB{k_}_{c}") for c in range(NC8)],
                                  db_, [Buf(f"db{k_}_{c}") for c in range(NC8)]))
                pwn, pwn_b = sb(st, "pwn", [128, 4, 2, 256], BF16)
                P.op("dve", lambda e: e.tensor_scalar(out=pwn[:].rearrange("p g c d -> p (g c d)"),
                                                      in0=pwt[:].rearrange("p g c d -> p (g c d)"),
                                                      scalar1=-1.0, scalar2=None, op0=ALU.mult), reads=[pw_b], writes=[pwn_b])
                hbfs = [sb(st, f"hbf{k}", [128, NC8, PW], BF16) for k in range(2)]
                rs_t, rs_b = sb(st, "rsp", [128, PW], F32)
                nbp = []
                for k_ in range(2):
                    sq_t, sq_b = sb(st, f"sqp{k_}", [128, NC8, PW], BF16)
                    nbp.append((sq_t, sq_b, rs_t, rs_b))

                def pool_cols(bi):
                    s0_, e0_ = blocks[bi]
                    return (0, e0_) if bi == 0 else (s0_, e0_)
                hist_t, hist_b = sb(st, "hist", [128, NC8, NS * 15], F32)
                red_t, red_b = sb(st, "red", [128, NC8, NS], F32)
                po_t, po_b = sb(st, "po", [31, D], F32)
                sst = [sb(st, f"sst{k}", [128, D], F32) for k in range(2)]
                P.dma("sp", "pools", pools_o[:, 0:14, :], state_d.rearrange("(b r) d -> b r d", r=15)[:, 1:15, :])
                for r, (r0, rows) in enumerate(((0, 128), (128, NS * 15 - 128))):
                    s_t, s_b = sst[r]
                    P.dma("sp", f"sst{r}", s_t[0:rows, :], state_d[r0:r0 + rows, :], writes=[s_b])
                    for hf in range(2):
                        k = next_bank()
                        for cc in range(4):
                            c = hf * 4 + cc
                            P.op("pe", lambda e, k=k, cc=cc, c=c, s_t=s_t, rows=rows: e.transpose(
                                out=psum[:, k, cc * 128:cc * 128 + rows], in_=s_t[0:rows, c * 128:(c + 1) * 128],
                                identity=ident[0:rows, 0:rows]),
                                reads=[s_b, ident_b], writes=[pb[k]], inc=(cc == 3))
                        src = psum[:, k, :].rearrange("p (c n) -> p c n", c=4)[:, :, 0:rows]
                        dst = hist_t[:, hf * 4:hf * 4 + 4, r0:r0 + rows]
                        P.op("act", lambda e, dst=dst, src=src: e.copy(out=dst, in_=src), reads=[pb[k]], writes=[hist_b])
                for c in range(NC8):
                    w = 2 << (c // 2)
                    P.op("dve", lambda e, c=c, w=w: e.tensor_reduce(
                        out=red_t[:, c, :], in_=hist_t[:, c, :].rearrange("p (b r) -> p b r", r=15)[:, :, 16 - w:15],
                        axis=AX.X, op=ALU.add), reads=[hist_b], writes=[red_b])
                state = {"prev": None}

                def stage_A(bi):
                    s0, e0 = blocks[bi]
                    hb_t, hb_b = hbs[bi % 2]
                    tA, tAb, tB, tBb, db_t, dbb = tsets[bi % 2]
                    if bi == 0:
                        a = 0
                        n = e0
                        norm_rest(nbp[bi % 2], 0, e0, grow, hb_t, hb_b, 0)
                    else:
                        a = s0 - 15
                        n = e0 - a
                        norm_rest(nbp[bi % 2], s0, e0, grow, hb_t, hb_b, 15)
                    if bi + 2 < len(blocks):
                        norm_sq(nbp[bi % 2], *pool_cols(bi + 2))
                    if bi > 0:
                        p_t, p_b, pn = state["prev"]
                        P.op("act", lambda e, hb_t=hb_t, p_t=p_t, pn=pn: e.copy(out=hb_t[:, :, 0:15], in_=p_t[:, :, pn - 15:pn]),
                             reads=[p_b], writes=[hb_b])
                    state["prev"] = (hb_t, hb_b, n)
                    hbf_t, hbf_b = hbfs[bi % 2]
                    if POOLPE:
                        P.op("act", lambda e, hbf_t=hbf_t, hb_t=hb_t, n=n: e.copy(out=hbf_t[:, :, 15:n], in_=hb_t[:, :, 15:n]),
                             reads=[hb_b], writes=[hbf_b])
                    for step in range(4):
                        sh = 1 << step
                        for c in range(2 * step, NC8):
                            if step == 0:
                                src_t, src_b = hb_t, hb_b
                            else:
                                src_t, src_b = (tA, tAb[c]) if step % 2 == 1 else (tB, tBb[c])
                            dst_t, dst_b = (tA, tAb[c]) if step % 2 == 0 else (tB, tBb[c])
                            lo = 2 * sh - 1
                            P.op("pool" if (c >= POOLC) else "dve", lambda e, dst_t=dst_t, src_t=src_t, c=c, lo=lo, sh=sh, n=n: e.tensor_tensor(
                                out=dst_t[:, c, lo:n], in0=src_t[:, c, lo:n], in1=src_t[:, c, lo - sh:n - sh], op=ALU.add),
                                reads=[src_b], writes=[dst_b])
                    for c in range(NC8):
                        g = c // 2
                        w = 2 << g
                        S_t, S_b = (tA, tAb[c]) if g % 2 == 0 else (tB, tBb[c])
                        if a <= CM0 and CM0 + 15 <= e0:
                            u0 = CM0 - a
                            P.op("dve", lambda e, S_t=S_t, c=c, u0=u0, g=g: e.tensor_tensor(
                                out=S_t[:, c, u0:u0 + 15], in0=S_t[:, c, u0:u0 + 15], in1=pc_t[:, g, :], op=ALU.mult),
                                reads=[S_b, pc_b], writes=[S_b])
                        if POOLPE:
                            P.op("act", lambda e, S_t=S_t, c=c, w=w, n=n, db_t=db_t: e.activation(
                                out=db_t[:, c, 15:n], in_=S_t[:, c, 15:n], func=AF.Copy, scale=1.0 / w),
                                reads=[S_b], writes=[dbb[c]])
                        else:
                            P.op("dve", lambda e, S_t=S_t, c=c, w=w, n=n, hb_t=hb_t, db_t=db_t: e.scalar_tensor_tensor(
                                out=db_t[:, c, 15:n], in0=S_t[:, c, 15:n], scalar=1.0 / w, in1=hb_t[:, c, 15:n],
                                op0=ALU.mult, op1=ALU.subtract), reads=[S_b, hb_b], writes=[dbb[c]])
                        if e0 == NCOL:
                            us = CS0 - a
                            P.op("dve", lambda e, c=c, us=us, hb_t=hb_t: e.tensor_tensor(
                                out=red_t[:, c, :], in0=red_t[:, c, :], in1=hb_t[:, c, us:us + NS], op=ALU.add),
                                reads=[red_b, hb_b], writes=[red_b])
                            if POOLPE:
                                P.op("dve", lambda e, c=c, us=us, w=w, db_t=db_t: e.tensor_scalar(
                                    out=db_t[:, c, us:us + NS], in0=red_t[:, c, :], scalar1=1.0 / w, scalar2=None, op0=ALU.mult),
                                    reads=[red_b], writes=[dbb[c]])
                            else:
                                P.op("dve", lambda e, c=c, us=us, w=w, hb_t=hb_t, db_t=db_t: e.scalar_tensor_tensor(
                                    out=db_t[:, c, us:us + NS], in0=red_t[:, c, :], scalar=1.0 / w, in1=hb_t[:, c, us:us + NS],
                                    op0=ALU.mult, op1=ALU.subtract), reads=[red_b, hb_b], writes=[dbb[c]])
                    return (hb_t, hb_b, n)

                def stage_B(bi, hb_t, hb_b, n):
                    s0, e0 = blocks[bi]
                    tA, tAb, tB, tBb, db_t, dbb = tsets[bi % 2]
                    m = n - 15
                    for g in range(4):
                        for dc in range(2):
                            c2 = 2 * g + dc
                            k = next_bank()
                            hbf_t, hbf_b = hbfs[bi % 2]
                            for cc in range(2):
                                P.op("pe", lambda e, k=k, m=m, g=g, cc=cc, dc=dc, n=n, db_t=db_t: e.matmul(
                                    psum[:, k, 0:m], lhsT=pwt[:, g, cc, dc * 128:(dc + 1) * 128], rhs=db_t[:, 2 * g + cc, 15:n],
                                    start=(cc == 0), stop=(cc == 1 and not POOLPE)),
                                    reads=[pw_b, dbb[2 * g + cc]], writes=[pb[k]], inc=(cc == 1 and not POOLPE))
                            for cc in range(2 if POOLPE else 0):
                                P.op("pe", lambda e, k=k, m=m, g=g, cc=cc, dc=dc, n=n, hbf_t=hbf_t: e.matmul(
                                    psum[:, k, 0:m], lhsT=pwn[:, g, cc, dc * 128:(dc + 1) * 128], rhs=hbf_t[:, 2 * g + cc, 15:n],
                                    start=False, stop=(cc == 1)),
                                    reads=[pwn_b, hbf_b], writes=[pb[k]], inc=(cc == 1))
                            P.op("dve", lambda e, k=k, m=m, c2=c2, s0=s0, e0=e0: e.scalar_tensor_tensor(
                                out=xT[:, c2, s0:e0], in0=psum[:, k, 0:m], scalar=gains[:, G_PS, c2:c2 + 1], in1=xT[:, c2, s0:e0],
                                op0=ALU.mult, op1=ALU.add), reads=[pb[k], xTb[c2], gains_b], writes=[xTb[c2]])
                    if e0 == NCOL:
                        u31 = n - 31
                        for hf in range(2):
                            k = next_bank()
                            for cc in range(4):
                                c = hf * 4 + cc
                                P.op("pe", lambda e, k=k, cc=cc, c=c, hb_t=hb_t, u31=u31: e.transpose(
                                    out=psum[0:31, k, cc * 128:(cc + 1) * 128], in_=hb_t[:, c, u31:u31 + 31], identity=ident[:]),
                                    reads=[hb_b, ident_b], writes=[pb[k]], inc=(cc == 3))
                            P.op("act", lambda e, k=k, hf=hf: e.copy(out=po_t[0:31, hf * 512:(hf + 1) * 512], in_=psum[0:31, k, :]),
                                 reads=[pb[k]], writes=[po_b])
                        P.dma("sp", "po", poolo_o[:, :], po_t[:, :], reads=[po_b])
                        P.dma("sp", "po", pools_o[:, 14, :], po_t[15:31, :], reads=[po_b])

                norm_sq(nbp[0], *pool_cols(0))
                norm_sq(nbp[1], *pool_cols(1))
                infoA = {0: stage_A(0)}
                for bi in range(len(blocks)):
                    if bi + 1 < len(blocks):
                        infoA[bi + 1] = stage_A(bi + 1)
                    stage_B(bi, *infoA[bi])
                P.flush()

        kwsd_b = Buf("kws_dram")
        vwsd_b = Buf("vws_dram")

        def kv_phase(KT, KT_b, V_t, V_b):
            blocks = [(15, 527), (527, 1039), (1039, 1551), (1551, 2063), (2063, NCOL)]
            with ExitStack() as st:
                wkv_t, wkv_b = sb(st, "wkv", [128, NC8, 512], BF16)
                P.dma("pool", "wkv", wkv_t[:], wkv.rearrange("(c p) f -> p c f", p=128), writes=[wkv_b])
                rm_t, rm_b = sb(st, "rmat", [128, 128], F32)
                P.dma("sp", "rmat", rm_t[:], rmat_d[:, :], writes=[rm_b])
                rtm_t, rtm_b = sb(st, "rtm", [128, 64], F32)
                rts_t, rts_b = sb(st, "rts", [NS, 64], F32)
                P.dma("sp", "rtm", rtm_t[:], ropet_d[0:128].rearrange("t a b -> t (a b)"), writes=[rtm_b])
                P.dma("sp", "rts", rts_t[:], ropet_d[128:144].rearrange("t a b -> t (a b)"), writes=[rts_b])
                P.dma("sp", "kws", kws_o[:, 0:127, :], ck_d[:, 1:128, :], writes=[kwsd_b])
                P.dma("sp", "vws", vws_o[:, 0:127, :], cv_d[:, 1:128, :], writes=[vwsd_b])
                kns = [sb(st, f"kn{k}", [128, NC8, 512], BF16) for k in range(2)]
                sq_t, sq_b = sb(st, "sqk", [128, NC8, 512], BF16)
                rs_t, rs_b = sb(st, "rsk", [128, 512], F32)
                rts2 = [sb(st, f"rt{k}", [128, 2, 512], F32) for k in range(2)]
                kfs = [sb(st, f"kf{k}", [128, 512], F32) for k in range(2)]
                t1s = [sb(st, f"t1{k}", [128, 512], F32) for k in range(2)]
                t2s = [sb(st, f"t2{k}", [128, 512], F32) for k in range(2)]
                ko_t, ko_b = sb(st, "ko", [128, 256], F32)
                vo_t, vo_b = sb(st, "vo", [128, 256], F32)
                kso_t, kso_b = sb(st, "kso", [NS, 256], F32)
                vso_t, vso_b = sb(st, "vso", [NS, 256], F32)
                tm = [sb(st, f"tm{k}", [128, 32], F32) for k in range(4)]
                krm_t, krm_b = sb(st, "krm", [128, 256], F32)
                krs_t, krs_b = sb(st, "krs", [NS, 256], F32)
                ri = 0
                nbk = (sq_t, sq_b, rs_t, rs_b)
                norm_block(nbk, blocks[0][0], blocks[0][1], G_KV, kns[0][0], kns[0][1], 0)
                for bi, (b0, b1) in enumerate(blocks):
                    n = b1 - b0
                    kn_t, kn_b = kns[bi % 2]
                    if bi + 1 < len(blocks):
                        norm_sq(nbk, blocks[bi + 1][0], blocks[bi + 1][1])
                    rt_t, rt_b = rts2[bi % 2]
                    P.dma("sp", f"rt{bi % 2}", rt_t[:, :, 0:n], ropef_d[:, :, b0:b1], writes=[rt_b])
                    for kc in range(2):
                        kf_t, kf_b = kfs[kc]
                        k = next_bank()
                        for c in range(NC8):
                            P.op("pe", lambda e, k=k, n=n, c=c, kc=kc, kn_t=kn_t: e.matmul(
                                psum[:, k, 0:n], lhsT=wkv_t[:, c, kc * 128:(kc + 1) * 128], rhs=kn_t[:, c, 0:n],
                                start=(c == 0), stop=(c == NC8 - 1)),
                                reads=[wkv_b, kn_b], writes=[pb[k]], inc=(c == NC8 - 1))
                        P.op("act", lambda e, k=k, n=n, kf_t=kf_t: e.copy(out=kf_t[:, 0:n], in_=psum[:, k, 0:n]),
                             reads=[pb[k]], writes=[kf_b])

                    def k_stage2(kc, n=n, b0=b0, b1=b1, rt_t=rt_t, rt_b=rt_b):
                        kf_t, kf_b = kfs[kc]
                        t1_t, t1_b = t1s[kc]
                        t2_t, t2_b = t2s[kc]
                        k2 = next_bank()
                        P.op("pe", lambda e, k2=k2, n=n, kf_t=kf_t: e.matmul(psum[:, k2, 0:n], lhsT=rm_t[:, :], rhs=kf_t[:, 0:n],
                                                                            start=True, stop=True),
                             reads=[rm_b, kf_b], writes=[pb[k2]])
                        P.op("dve", lambda e, n=n, kf_t=kf_t, t1_t=t1_t, rt_t=rt_t: e.tensor_tensor(
                            out=t1_t[:, 0:n], in0=kf_t[:, 0:n], in1=rt_t[:, 0, 0:n], op=ALU.mult),
                            reads=[kf_b, rt_b], writes=[t1_b])
                        P.op("dve", lambda e, n=n, k2=k2, t2_t=t2_t, rt_t=rt_t: e.tensor_tensor(
                            out=t2_t[:, 0:n], in0=psum[:, k2, 0:n], in1=rt_t[:, 1, 0:n], op=ALU.mult),
                            reads=[pb[k2], rt_b], writes=[t2_b])
                        P.op("dve", lambda e, n=n, t1_t=t1_t, t2_t=t2_t, kc=kc, b0=b0, b1=b1: e.tensor_tensor(
                            out=KT[:, kc, b0:b1], in0=t1_t[:, 0:n], in1=t2_t[:, 0:n], op=ALU.add),
                            reads=[t1_b, t2_b], writes=[KT_b])

                    k_pending = [lambda: k_stage2(0), lambda: k_stage2(1)]
                    assert (b1 - b0 + 127) // 128 >= 2
                    if bi + 1 < len(blocks):
                        norm_rest(nbk, blocks[bi + 1][0], blocks[bi + 1][1], G_KV, kns[(bi + 1) % 2][0], kns[(bi + 1) % 2][1], 0)
                    for t0 in range(b0, b1, 128):
                        if k_pending:
                            k_pending.pop(0)()
                        m = min(128, b1 - t0)
                        vb = (t0 - 15) // 128
                        k = next_bank()
                        for c in range(NC8):
                            P.op("pe", lambda e, k=k, m=m, c=c, t0=t0, b0=b0, kn_t=kn_t: e.matmul(
                                psum[0:m, k, 0:256], lhsT=kn_t[:, c, t0 - b0:t0 - b0 + m], rhs=wkv_t[:, c, 256:512],
                                start=(c == 0), stop=(c == NC8 - 1)),
                                reads=[wkv_b, kn_b], writes=[pb[k]], inc=(c == NC8 - 1))
                        P.op("act", lambda e, k=k, m=m, vb=vb: e.copy(out=V_t[0:m, vb, :, 0:64], in_=psum[0:m, k, 0:256].rearrange("p (h d) -> p h d", h=4)),
                             reads=[pb[k]], writes=[V_b])
                        if t0 >= 2063 and KVOPT >= 2:
                            is_s = (m == NS)
                            o_t, o_b = (vso_t, vso_b) if is_s else (vo_t, vo_b)
                            P.op("dve", lambda e, k=k, m=m, o_t=o_t: e.tensor_copy(out=o_t[0:m, :], in_=psum[0:m, k, 0:256]),
                                 reads=[pb[k]], writes=[o_b])
                            if is_s:
                                P.dma("sp", "vws", vws_o[:, 127, :], o_t[0:m, :], reads=[o_b], writes=[vwsd_b])
                            else:
                                P.dma("sp", "vo", vwp_o[:, :], o_t[0:m, :], reads=[o_b])
                            if KVOPT < 3:
                                continue
                            k = next_bank()
                            for c in range(NC8):
                                P.op("pe", lambda e, k=k, m=m, c=c, t0=t0, b0=b0, kn_t=kn_t: e.matmul(
                                    psum[0:m, k, 0:256], lhsT=kn_t[:, c, t0 - b0:t0 - b0 + m], rhs=wkv_t[:, c, 0:256],
                                    start=(c == 0), stop=(c == NC8 - 1)),
                                    reads=[wkv_b, kn_b], writes=[pb[k]], inc=(c == NC8 - 1))
                            o_t, o_b = (kso_t, kso_b) if is_s else (ko_t, ko_b)
                            tb_t, tb_b = (rts_t, rts_b) if is_s else (rtm_t, rtm_b)
                            P.op("act", lambda e, k=k, m=m, o_t=o_t: e.copy(out=o_t[0:m, :], in_=psum[0:m, k, 0:256]),
                                 reads=[pb[k]], writes=[o_b])
                            kr_t, kr_b = (krs_t, krs_b) if is_s else (krm_t, krm_b)
                            P.op("dve", lambda e, k=k, m=m, kr_t=kr_t: e.tensor_copy(out=kr_t[0:m, :], in_=psum[0:m, k, 0:256]),
                                 reads=[pb[k]], writes=[kr_b])
                            pv = kr_t[0:m, :].rearrange("p (h d) -> p h d", h=4)
                            ov = o_t[0:m, :].rearrange("p (h d) -> p h d", h=4)
                            cosv = tb_t[0:m, 0:32].rearrange("p (h d) -> p h d", h=4)
                            sinv = tb_t[0:m, 32:64].rearrange("p (h d) -> p h d", h=4)
                            tmv = [t[0][0:m, :].rearrange("p (h d) -> p h d", h=4) for t in tm]
                            x1 = pv[:, :, 0:8]
                            x2 = pv[:, :, 8:16]
                            for (ti, xa, tab) in ((0, x1, cosv), (1, x2, sinv), (2, x2, cosv), (3, x1, sinv)) if KVOPT >= 4 else ():
                                P.op("dve", lambda e, ti=ti, xa=xa, tab=tab, tmv=tmv: e.tensor_tensor(
                                    out=tmv[ti], in0=xa, in1=tab, op=ALU.mult),
                                    reads=[kr_b, tb_b], writes=[tm[ti][1]])
                            if KVOPT >= 4:
                                P.op("dve", lambda e, ov=ov, tmv=tmv: e.tensor_tensor(out=ov[:, :, 0:8], in0=tmv[0], in1=tmv[1], op=ALU.subtract),
                                     reads=[tm[0][1], tm[1][1], o_b], writes=[o_b])
                                P.op("dve", lambda e, ov=ov, tmv=tmv: e.tensor_tensor(out=ov[:, :, 8:16], in0=tmv[2], in1=tmv[3], op=ALU.add),
                                     reads=[tm[2][1], tm[3][1], o_b], writes=[o_b])
                            if is_s:
                                P.dma("sp", "kws", kws_o[:, 127, :], o_t[0:m, :], reads=[o_b], writes=[kwsd_b])
                            else:
                                P.dma("sp", "ko", kwp_o[:, :], o_t[0:m, :], reads=[o_b])
                P.flush()

        if stages >= 2:
            pool_phase()
        if stages >= 3:
            ffn_phase(0, 1, 0, NCOL)
        kvst = ExitStack()
        top.enter_context(kvst)
        if stages >= 4:
            KT, KT_b = sb(kvst, "KT", [128, 2, NCOL], BF16)
            V_t, V_b = sb(kvst, "V", [128, 18, 4, 65], BF16)
            P.op("dve", lambda e: e.memset(V_t[:, :, :, 64:65], 1.0), writes=[V_b])
            kv_phase(KT, KT_b, V_t, V_b)


        def attn_phase(KT, KT_b, V_t, V_b):
            grow = G_N(1, 1)
            with ExitStack() as st:
                wq_t, wq_b = sb(st, "wq", [128, NC8, D], BF16)
                Wq = wq.rearrange("(c p) f -> p c f", p=128)
                wq_bs = [Buf(f"wq{qc}") for qc in range(8)]
                for qc in range(8):
                    for half in range(2):
                        head = QA[qc] if half == 0 else QB[qc]
                        P.dma("pool", f"wq{qc}", wq_t[:, :, qc * 128 + half * 64:qc * 128 + half * 64 + 64],
                              Wq[:, :, head * 64:(head + 1) * 64], writes=[])
                    wq_bs[qc].w = (f"wq{qc}", P.dcnt[f"wq{qc}"])
                rm_t, rm_b = sb(st, "rmat2", [128, 128], F32)
                P.dma("sp", "rmat2", rm_t[:], rmat_d[:, :], writes=[rm_b])
                mk_t, mk_b = sb(st, "mk", [128, 3, 128], F32)
                P.dma("sp", "mk", mk_t[:].rearrange("p a b -> p (a b)"), masks_d[:, :], writes=[mk_b])
                sk_t, sk_b = sb(st, "sinks", [128, 16], F32)
                P.dma("sp", "sinks", sk_t[:], sinks_d[:, :], writes=[sk_b])
                es_t, es_b = sb(st, "es", [128, 16], F32)
                P.op("act", lambda e: e.activation(out=es_t[:], in_=sk_t[:], func=AF.Exp), reads=[sk_b], writes=[es_b])
                identb, identb_b = sb(st, "identb", [128, 128], BF16)
                mb4, mb4_b = sb(st, "mb4", [128, 3, 4, 128], BF16)

                def build_masks():
                    P.op("dve", lambda e: e.tensor_copy(out=identb[:], in_=ident[:]), reads=[ident_b], writes=[identb_b])
                    for mi_ in range(3):
                        P.op("dve", lambda e, mi_=mi_: e.tensor_scalar(
                            out=mb4[:, mi_, :, :], in0=mk_t[:, mi_, :].unsqueeze(1).broadcast_to([128, 4, 128]),
                            scalar1=-1.0, scalar2=30000.0, op0=ALU.add, op1=ALU.mult), reads=[mk_b], writes=[mb4_b])
                hq_t, hq_b = sb(st, "hq", [128, NC8, 512], BF16)
                sq_t, sq_b = sb(st, "sqa", [128, NC8, 512], BF16)
                rs_t, rs_b = sb(st, "rsa", [128, 512], F32)
                rt_t, rt_b = sb(st, "art0", [128, 2, 512], F32)
                qfs = [sb(st, f"qf{k}", [128, 512], F32) for k in range(2)]
                t1s = [sb(st, f"at1{k}", [128, 512], F32) for k in range(1)]
                t2s = [sb(st, f"at2{k}", [128, 512], F32) for k in range(1)]
                QZ = [sb(st, f"QZ{k}", [128, 8, 512], BF16) for k in range(2)]
                for hfz in range(2):
                    P.op("pool", lambda e, hfz=hfz: e.memset(QZ[hfz][0][:], 0.0), writes=[QZ[hfz][1]])
                cnt = {"ri": 0, "et": 0, "pt": 0, "ot": 0}

                def q_norm(c0, c1):
                    n = c1 - c0
                    norm_block((sq_t, sq_b, rs_t, rs_b), c0, c1, grow, hq_t, hq_b, 0)
                    P.dma("sp", "art0", rt_t[:, :, 0:n], ropef_d[:, :, c0:c1], writes=[rt_b])

                def q_project(c0, c1):
                    q_norm(c0, c1)
                    q_stages(c0, c1)

                def q_stages(c0, c1):
                    n = c1 - c0

                    def stage1(qc):
                        qf_t, qf_b = qfs[qc % 2]
                        k = next_bank()
                        kc2, hh_ = qc // 4, qc % 4
                        for c in range(NC8):
                            P.op("pe", lambda e, k=k, c=c, kc2=kc2, hh_=hh_: e.matmul(
                                psum[:, k, 0:n],
                                lhsT=wq_t[:, c, (kc2 * 4 + hh_) * 128:(kc2 * 4 + hh_ + 1) * 128],
                                rhs=hq_t[:, c, 0:n], start=(c == 0), stop=(c == NC8 - 1)),
                                reads=[wq_bs[qc], hq_b], writes=[pb[k]], inc=(c == NC8 - 1))
                        P.op("act", lambda e, k=k, qf_t=qf_t: e.copy(out=qf_t[:, 0:n], in_=psum[:, k, 0:n]),
                             reads=[pb[k]], writes=[qf_b])

                    def stage2(qc):
                        qf_t, qf_b = qfs[qc % 2]
                        t1_t, t1_b = t1s[0]
                        t2_t, t2_b = t2s[0]
                        k2 = next_bank()
                        P.op("pe", lambda e, k2=k2, qf_t=qf_t: e.matmul(psum[:, k2, 0:n], lhsT=rm_t[:, :], rhs=qf_t[:, 0:n],
                                                                      start=True, stop=True),
                             reads=[rm_b, qf_b], writes=[pb[k2]])
                        P.op("pool", lambda e, qf_t=qf_t, t1_t=t1_t: e.tensor_tensor(
                            out=t1_t[:, 0:n], in0=qf_t[:, 0:n], in1=rt_t[:, 0, 0:n], op=ALU.mult),
                            reads=[qf_b, rt_b], writes=[t1_b])
                        P.op("dve", lambda e, k2=k2, t2_t=t2_t: e.tensor_tensor(
                            out=t2_t[:, 0:n], in0=psum[:, k2, 0:n], in1=rt_t[:, 1, 0:n], op=ALU.mult),
                            reads=[pb[k2], rt_b], writes=[t2_b])
                        for hfz in range(2):
                            r0_, r1_ = 64 * hfz, 64 * hfz + 64
                            P.op("dve", lambda e, t1_t=t1_t, t2_t=t2_t, qc=qc, hfz=hfz, r0_=r0_, r1_=r1_: e.tensor_tensor(
                                out=QZ[hfz][0][r0_:r1_, qc, 0:n], in0=t1_t[r0_:r1_, 0:n], in1=t2_t[r0_:r1_, 0:n], op=ALU.add),
                                reads=[t1_b, t2_b], writes=[QZ[hfz][1]])

                    for qc in range(8):
                        stage1(qc)
                        if qc >= 1:
                            stage2(qc - 1)
                    stage2(7)

                st2 = ExitStack()
                st.enter_context(st2)
                wo_t, wo_b = sb(st2, "wo", [128, NC8, D], BF16)
                P.dma("pool", "wo", wo_t[:], wo.rearrange("(c p) f -> p c f", p=128), writes=[wo_b])
                OTs = [sb(st2, f"OT{k}", [128, 4, 8, 128], BF16) for k in range(2)]
                pts = [sb(st2, f"pt{k}", [128, 512], BF16) for k in range(6)]
                ots = [sb(st2, f"otm{k}", [128, D], F32) for k in range(2)]
                dn_t, dn_b = sb(st2, "dn", [128, 16], F32)

                def o_project(c0, c1, w_t, w_b, rhs_of, OT_b):
                    for ch in o_project_chunks(c0, c1, w_t, w_b, rhs_of, OT_b):
                        ch()

                def o_project_chunks(c0, c1, w_t, w_b, rhs_of, OT_b):
                    return [(lambda dc=dc: o_project_dc(c0, c1, w_t, w_b, rhs_of, OT_b, dc)) for dc in range(NC8)]

                def o_project_dc(c0, c1, w_t, w_b, rhs_of, OT_b, dc):
                    n = c1 - c0
                    if True:
                        k = nbt()
                        for qc in range(8):
                            P.op("pe", lambda e, k=k, qc=qc, dc=dc: e.matmul(
                                psum[:, k, 0:n], lhsT=w_t[:, qc, dc * 128:(dc + 1) * 128], rhs=rhs_of(qc),
                                start=(qc == 0), stop=(qc == 7)),
                                reads=[w_b, OT_b], writes=[pb[k]], inc=(qc == 7))
                        P.op("dve", lambda e, k=k, dc=dc: e.tensor_tensor(
                            out=xT[:, dc, c0:c1], in0=psum[:, k, 0:n], in1=xT[:, dc, c0:c1], op=ALU.add),
                            reads=[pb[k], xTb[dc]], writes=[xTb[dc]])

                sbank = [0]
                vbank = [0]
                tbank = [0]

                def nbs():
                    sbank[0] = (sbank[0] + 1) % 4
                    return sbank[0]

                def nbv():
                    vbank[0] = (vbank[0] + 1) % 2
                    return 4 + vbank[0]

                def nbt():
                    tbank[0] = (tbank[0] + 1) % 2
                    return 6 + tbank[0]

                def emit_S(u):
                    sbi, i, kv = u
                    nb = sbi * 4 + i
                    qcol0 = CM0 + 512 * sbi + 128 * i
                    ql = 128 * i
                    kc, half = kv // 2, kv % 2
                    ptk = []
                    for kb in range(2):
                        kcol0 = qcol0 - 128 + 128 * kb
                        mi = (2 if nb == 0 else 0) if kb == 0 else 1
                        ks = nbs()
                        P.op("pe", lambda e, ks=ks, kc=kc, half=half, kcol0=kcol0, ql=ql: e.matmul(
                            psum[:, ks, :], lhsT=KT[:, kc, kcol0:kcol0 + 128],
                            rhs=QZ[half][0][:, 4 * kc:4 * kc + 4, ql:ql + 128], start=True, stop=False),
                            reads=[KT_b, QZ[half][1]], writes=[pb[ks]], inc=False)
                        P.op("pe", lambda e, ks=ks, mi=mi: e.matmul(
                            psum[:, ks, :], lhsT=identb[:, :], rhs=mb4[:, mi, :, :].rearrange("p h q -> p (h q)"),
                            start=False, stop=True),
                            reads=[identb_b, mb4_b], writes=[pb[ks]])
                        p_t, p_b = pts[cnt["pt"] % 6]
                        cnt["pt"] += 1
                        P.op("act", lambda e, ks=ks, p_t=p_t: e.activation(out=p_t[:, :], in_=psum[:, ks, :], func=AF.Exp, scale=0.125),
                             reads=[pb[ks]], writes=[p_b])
                        ptk.append((p_t, p_b))
                    return ptk

                def emit_V(u, ptk, o_t, o_b):
                    sbi, i, kv = u
                    OT_t, OT_b = OTs[sbi % 2]
                    nb = sbi * 4 + i
                    ko = nbv()
                    for hh in range(4):
                        for kb in range(2):
                            p_t, p_b = ptk[kb]
                            vb = nb + kb
                            P.op("pe", lambda e, ko=ko, hh=hh, kb=kb, p_t=p_t, vb=vb, kv=kv: e.matmul(
                                psum[:, ko, hh * 65:(hh + 1) * 65], lhsT=p_t[:, hh * 128:(hh + 1) * 128], rhs=V_t[:, vb, kv, :],
                                start=(kb == 0), stop=(kb == 1)),
                                reads=[p_b, V_b], writes=[pb[ko]], inc=(hh == 3 and kb == 1))
                    P.op("dve", lambda e, ko=ko, kv=kv: e.tensor_tensor(
                        out=dn_t[:, 4 * kv:4 * kv + 4],
                        in0=psum[:, ko, 0:260].rearrange("p (h e) -> p h e", e=65)[:, :, 64],
                        in1=es_t[:, 4 * kv:4 * kv + 4], op=ALU.add),
                        reads=[pb[ko], es_b], writes=[dn_b])
                    P.op("dve", lambda e, kv=kv: e.reciprocal(out=dn_t[:, 4 * kv:4 * kv + 4], in_=dn_t[:, 4 * kv:4 * kv + 4]),
                         reads=[dn_b], writes=[dn_b])
                    for hh in range(4):
                        h = 4 * kv + hh
                        if hh % 2 == 0:
                            P.op("dve", lambda e, ko=ko, hh=hh, h=h, o_t=o_t: e.tensor_scalar(
                                out=o_t[:, h * 64:(h + 1) * 64], in0=psum[:, ko, hh * 65:hh * 65 + 64],
                                scalar1=dn_t[:, h:h + 1], scalar2=None, op0=ALU.mult),
                                reads=[pb[ko], dn_b], writes=[o_b])
                        else:
                            P.op("act", lambda e, ko=ko, hh=hh, h=h, o_t=o_t: e.activation(
                                out=o_t[:, h * 64:(h + 1) * 64], in_=psum[:, ko, hh * 65:hh * 65 + 64],
                                func=AF.Copy, scale=dn_t[:, h:h + 1]),
                                reads=[pb[ko], dn_b], writes=[o_b])
                def emit_T(sbi, i, o_t, o_b):
                    OT_t, OT_b = OTs[sbi % 2]
                    if True:
                        for hf in range(2):
                            kT = nbt()
                            for cc in range(4):
                                qc = hf * 4 + cc
                                P.op("pe", lambda e, kT=kT, cc=cc, qc=qc, o_t=o_t: e.transpose(
                                    out=psum[:, kT, cc * 128:(cc + 1) * 128], in_=o_t[:, qc * 128:(qc + 1) * 128], identity=ident[:]),
                                    reads=[o_b, ident_b], writes=[pb[kT]], inc=(cc == 3))
                            P.op("act", lambda e, kT=kT, hf=hf, i=i: e.copy(
                                out=OT_t[:, i, hf * 4:hf * 4 + 4, :].rearrange("p c q -> p (c q)"), in_=psum[:, kT, :]),
                                reads=[pb[kT]], writes=[OT_b])

                q_norm(CM0, CM0 + 512)
                build_masks()
                q_stages(CM0, CM0 + 512)
                oproj_pend = []
                for sbi in range(4):
                    c0 = CM0 + 512 * sbi
                    c1 = c0 + 512
                    if sbi + 1 < 4:
                        q_norm(c1, c1 + 512)
                    units = [(sbi, i, kv) for i in range(4) for kv in range(4)]
                    otl = {}
                    for i in range(4):
                        otl[i] = ots[cnt["ot"] % 2]
                        cnt["ot"] += 1
                    pend = []
                    tpend = []
                    for ui, u in enumerate(units):
                        pend.append((u, emit_S(u)))
                        if len(pend) > PIPE:
                            u0, ptk0 = pend.pop(0)
                            if tpend and u0[2] == 1:
                                tpend.pop(0)()
                            emit_V(u0, ptk0, *otl[u0[1]])
                            if u0[2] == 3:
                                tpend.append(lambda u0=u0: emit_T(u0[0], u0[1], *otl[u0[1]]))
                        if ui % 2 == 1 and oproj_pend:
                            oproj_pend.pop(0)()
                    while pend:
                        u0, ptk0 = pend.pop(0)
                        if tpend and u0[2] == 1:
                            tpend.pop(0)()
                        emit_V(u0, ptk0, *otl[u0[1]])
                        if u0[2] == 3:
                            tpend.append(lambda u0=u0: emit_T(u0[0], u0[1], *otl[u0[1]]))
                    while tpend:
                        tpend.pop(0)()
                    while oproj_pend:
                        oproj_pend.pop(0)()
                    OTc_t, OTc_b = OTs[sbi % 2]
                    oproj_pend = o_project_chunks(c0, c1, wo_t, wo_b, lambda qc, OTc_t=OTc_t: OTc_t[:, :, qc, :], OTc_b)
                    if sbi + 1 < 4:
                        q_stages(c1, c1 + 512)
                while oproj_pend:
                    oproj_pend.pop(0)()

                P.flush()
                st2.close()
                if ATTN_S:
                    q_project(CS0, NCOL)
                    wos_t, wos_b = sb(st, "wos", [128, NC8, D], BF16)
                    for qc in range(8):
                        for half in range(2):
                            head = QA[qc] if half == 0 else QB[qc]
                            P.dma("pool", "wos", wos_t[half * 64:(half + 1) * 64, qc, :], wo[head * 64:(head + 1) * 64, :], writes=[])
                    wos_b.w = ("wos", P.dcnt["wos"])
                    esr_t, esr_b = sb(st, "esr", [128, NS, 16], BF16)
                    P.op("dve", lambda e: e.tensor_scalar(out=esr_t[:, :, :], in0=es_t[:, :].unsqueeze(1).broadcast_to([128, NS, 16]),
                                                          scalar1=1.0 / 128.0, scalar2=None, op0=ALU.mult),
                         reads=[es_b], writes=[esr_b])
                    OS_t, OS_b = sb(st, "OTs", [128, 8, NS], BF16)
                    SK_t, SK_b = sb(st, "SK", [128, NS, 256], F32)
                    SVb_t, SVb_b = sb(st, "SVb", [128, NS, 256], BF16)
                    SKT_t, SKT_b = sb(st, "SKT", [128, NS, 2, 128], BF16)
                    Es_t, Es_b = sb(st, "Es", [128, NS * 16], BF16)
                    os_t, os_b = sb(st, "os", [128, NS * 16], F32)
                    rdS_t, rdS_b = sb(st, "rdS", [128, NS * 16], F32)
                    P.dma("sp", "skl", SK_t[:, :, :], kws_o.rearrange("b k f -> k b f"), reads=[kwsd_b], writes=[SK_b])
                    P.dma("pool", "svl", SVb_t[:, :, :], vws_o.rearrange("b k f -> k b f"), reads=[vwsd_b], writes=[SVb_b])
                    for b in range(NS):
                        k = next_bank()
                        for kc in range(2):
                            P.op("pe", lambda e, k=k, b=b, kc=kc: e.transpose(
                                out=psum[:, k, kc * 128:(kc + 1) * 128], in_=SK_t[:, b, kc * 128:(kc + 1) * 128], identity=ident[:]),
                                reads=[SK_b, ident_b], writes=[pb[k]], inc=(kc == 1))
                        P.op("dve", lambda e, k=k, b=b: e.tensor_copy(
                            out=SKT_t[:, b, :, :].rearrange("p a k -> p (a k)"), in_=psum[:, k, 0:256]),
                            reads=[pb[k]], writes=[SKT_b])
                    kss = next_bank()
                    for b in range(NS):
                        for kv in range(4):
                            kc, half = kv // 2, kv % 2
                            P.op("pe", lambda e, b=b, kv=kv, kc=kc, half=half: e.matmul(
                                psum[:, kss, b * 16 + 4 * kv:b * 16 + 4 * kv + 4], lhsT=SKT_t[:, b, kc, :],
                                rhs=QZ[half][0][:, 4 * kc:4 * kc + 4, b], start=True, stop=True),
                                reads=[SKT_b, QZ[half][1]], writes=[pb[kss]], inc=(b == NS - 1 and kv == 3))
                    P.op("act", lambda e: e.activation(out=Es_t[:, :], in_=psum[:, kss, 0:NS * 16], func=AF.Exp, scale=0.125),
                         reads=[pb[kss]], writes=[Es_b])
                    kds = next_bank()
                    P.op("pe", lambda e: e.matmul(psum[:, kds, 0:NS * 16], lhsT=ones1[:, :], rhs=Es_t[:, :], start=True, stop=False),
                         reads=[ones1_b, Es_b], writes=[pb[kds]], inc=False)
                    P.op("pe", lambda e: e.matmul(psum[:, kds, 0:NS * 16], lhsT=ones1[:, :], rhs=esr_t[:, :, :], start=False, stop=True),
                         reads=[ones1_b, esr_b], writes=[pb[kds]])
                    kos = next_bank()
                    for b in range(NS):
                        for kv in range(4):
                            kc = kv // 2
                            P.op("pe", lambda e, b=b, kv=kv, kc=kc: e.matmul(
                                psum[:, kos, b * 16 + 4 * kv:b * 16 + 4 * kv + 4], lhsT=SVb_t[:, b, kc * 128:(kc + 1) * 128],
                                rhs=Es_t[:, b * 16 + 4 * kv:b * 16 + 4 * kv + 4], start=True, stop=True),
                                reads=[SVb_b, Es_b], writes=[pb[kos]], inc=(b == NS - 1 and kv == 3))
                    P.op("dve", lambda e: e.reciprocal(out=rdS_t[:, :], in_=psum[:, kds, 0:NS * 16]), reads=[pb[kds]], writes=[rdS_b])
                    P.op("act", lambda e: e.copy(out=os_t[:, :], in_=psum[:, kos, 0:NS * 16]), reads=[pb[kos]], writes=[os_b])
                    for kv in range(4):
                        kc, half = kv // 2, kv % 2
                        p0_, p1_ = 64 * half, 64 * half + 64
                        P.op("dve", lambda e, kv=kv, kc=kc, p0_=p0_, p1_=p1_: e.tensor_tensor(
                            out=OS_t[p0_:p1_, 4 * kc:4 * kc + 4, :],
                            in0=os_t[p0_:p1_, :].rearrange("p (b v h) -> p v h b", v=4, h=4)[:, kv],
                            in1=rdS_t[p0_:p1_, :].rearrange("p (b v h) -> p v h b", v=4, h=4)[:, kv], op=ALU.mult),
                            reads=[os_b, rdS_b], writes=[OS_b])
                    o_project(CS0, NCOL, wos_t, wos_b, lambda qc: OS_t[:, qc, :], OS_b)
                P.flush()

        if stages >= 5:
            ffn_phase(1, 0, CM0, NCOL)
        if stages >= 6:
            attn_phase(KT, KT_b, V_t, V_b)
        kvst.close()
        if stages >= 7:
            ffn_phase(1, 1, CM0, NCOL)

        with ExitStack() as st:
            yTs = [sb(st, f"yT{k}", [128, NC8, 512], F32) for k in range(2)]
            sq_t, sq_b = sb(st, "sqf", [128, NC8, 512], BF16)
            rs_t, rs_b = sb(st, "rsf", [128, 512], F32)
            yst = [sb(st, f"yst{k}", [128, D], F32) for k in range(3)]
            yi = 0
            fblocks = split(CM0, NCOL, 5)
            nbf = (sq_t, sq_b, rs_t, rs_b)
            norm_block(nbf, fblocks[0][0], fblocks[0][1], G_FIN, yTs[0][0], yTs[0][1], 0)
            for fbi, (b0, b1) in enumerate(fblocks):
                yT_t, yT_b = yTs[fbi % 2]
                if fbi + 1 < len(fblocks):
                    norm_sq(nbf, fblocks[fbi + 1][0], fblocks[fbi + 1][1])
                for t0 in range(b0, b1, 128):
                    if t0 == b0 + 128 and fbi + 1 < len(fblocks):
                        norm_rest(nbf, fblocks[fbi + 1][0], fblocks[fbi + 1][1], G_FIN, yTs[(fbi + 1) % 2][0], yTs[(fbi + 1) % 2][1], 0)
                    t1 = min(b1, t0 + 128)
                    rows = t1 - t0
                    y_t, y_b = yst[yi % 3]
                    for hf in range(2):
                        k = next_bank()
                        for cc in range(4):
                            c = hf * 4 + cc
                            P.op("pe", lambda e, k=k, cc=cc, c=c, t0=t0, t1=t1, b0=b0, rows=rows, yT_t=yT_t: e.transpose(
                                out=psum[0:rows, k, cc * 128:(cc + 1) * 128], in_=yT_t[:, c, t0 - b0:t1 - b0], identity=ident[:]),
                                reads=[yT_b, ident_b], writes=[pb[k]], inc=(cc == 3))
                        if hf == 0:
                            P.op("dve", lambda e, y_t=y_t, k=k, rows=rows, hf=hf: e.tensor_copy(
                                out=y_t[0:rows, hf * 512:(hf + 1) * 512], in_=psum[0:rows, k, :]),
                                reads=[pb[k]], writes=[y_b])
                        else:
                            P.op("act", lambda e, y_t=y_t, k=k, rows=rows, hf=hf: e.copy(
                                out=y_t[0:rows, hf * 512:(hf + 1) * 512], in_=psum[0:rows, k, :]),
                                reads=[pb[k]], writes=[y_b])
                    P.dma("sp", f"yst{yi % 3}", y_o[t0 - CM0:t1 - CM0, :], y_t[0:rows, :], reads=[y_b])
                    yi += 1
            if debug:
                P.dma("sp", "dbg", dbg_o[:, :], xT[:].rearrange("p c n -> p (c n)"), reads=xTb)
            P.wait_all_dma("sp")
            P.flush()
    return nc


def _rope_tables(pos_cols):
    half = 8
    inv = np.power(np.float32(ROPE_THETA), -np.arange(half, dtype=np.float32) * np.float32(2.0 / 16)).astype(np.float32)
    ang = pos_cols.astype(np.float32)[:, None] * inv[None, :]
    return np.cos(ang).astype(np.float32), np.sin(ang).astype(np.float32)


def make_core_inputs(core, inputs):
    seq, half = core // 2, core % 2
    p0 = half * MAIN
    xp = inputs["x_prompt"]
    xs = inputs["x_sample"]
    b0 = core * NS
    xin = np.zeros((NCOL, D), np.float32)
    if p0 > 0:
        xin[0:HALO] = xp[seq, p0 - HALO:p0]
    xin[CM0:CS0] = xp[seq, p0:p0 + MAIN]
    xin[CS0:] = xs[b0:b0 + NS, 0]
    m = {"xin": xin}
    vecs = [inputs["norm_g"][l, i] for l in range(2) for i in range(3)] + [
        inputs["kv_norm_g"], inputs["final_norm_g"], inputs["pool_scale"][0]]
    g = np.stack(vecs, 0).reshape(9, 8, 128)
    m["gains"] = np.ascontiguousarray(g.transpose(2, 0, 1).reshape(128, 72)).astype(np.float32)
    m["sinks"] = np.ascontiguousarray(np.broadcast_to(inputs["attn_sinks"][0][None, :], (128, 16))).astype(np.float32)
    m["state"] = np.ascontiguousarray(inputs["state_pool"][0, b0:b0 + NS].reshape(NS * 15, D))
    m["ck"] = np.ascontiguousarray(inputs["cache_k_win"][b0:b0 + NS].reshape(NS, 128, 256))
    m["cv"] = np.ascontiguousarray(inputs["cache_v_win"][b0:b0 + NS].reshape(NS, 128, 256))
    pos = np.concatenate([p0 - HALO + np.arange(HALO), p0 + np.arange(MAIN), np.full(NS, PAST_LEN)]).astype(np.int64)
    cos, sin = _rope_tables(pos)
    ropef = np.zeros((128, 2, NCOL), np.float32)
    ropef[:, 0, :] = 1.0
    for hb in (0, 64):
        for i in range(16):
            ropef[hb + i, 0, :] = cos[:, i % 8]
            ropef[hb + i, 1, :] = sin[:, i % 8]
    m["ropef"] = ropef
    ropet = np.zeros((144, 2, 4, 8), np.float32)
    ropet[:, 0] = cos[NCOL - 144:, None, :]
    ropet[:, 1] = sin[NCOL - 144:, None, :]
    m["ropet"] = ropet.reshape(144, 2, 32)
    kk = np.arange(128)[:, None]
    qq = np.arange(128)[None, :]
    mprev = (kk > qq).astype(np.float32)
    mcur = (kk <= qq).astype(np.float32)
    mprev0 = mprev if p0 > 0 else np.zeros_like(mprev)
    m["masks"] = np.ascontiguousarray(np.concatenate([mprev, mcur, mprev0], 1))
    pc = np.ones((128, 4, 15), np.float32)
    if p0 == 0:
        for gi, w in enumerate((2, 4, 8, 16)):
            for t in range(15):
                pc[:, gi, t] = w / min(w, t + 1)
    m["pcorr"] = pc.reshape(128, 60)
    m["ident"] = np.eye(128, dtype=np.float32)
    rm = np.zeros((128, 128), np.float32)
    for hb in (0, 64):
        for i in range(8):
            rm[hb + i + 8, hb + i] = -1.0
            rm[hb + i, hb + i + 8] = 1.0
    m["rmat"] = rm
    return m


_NC_CACHE = {}


def kernel(x_prompt, x_sample, state_pool, cache_k_win, cache_v_win, norm_g, ffn_w_gate, ffn_w_up,
           ffn_w_down, pool_w, pool_scale, kv_norm_g, w_kv, w_q, w_o, attn_sinks, final_norm_g,
           _stages=99, _debug=False):
    inputs = dict(x_prompt=np.asarray(x_prompt), x_sample=np.asarray(x_sample), state_pool=np.asarray(state_pool),
                  cache_k_win=np.asarray(cache_k_win), cache_v_win=np.asarray(cache_v_win), norm_g=np.asarray(norm_g),
                  pool_scale=np.asarray(pool_scale), kv_norm_g=np.asarray(kv_norm_g), attn_sinks=np.asarray(attn_sinks),
                  final_norm_g=np.asarray(final_norm_g))
    shared = {
        "wg": np.ascontiguousarray(np.asarray(ffn_w_gate, dtype=np.float32)),
        "wu": np.ascontiguousarray(np.asarray(ffn_w_up, dtype=np.float32)),
        "wd": np.ascontiguousarray(np.asarray(ffn_w_down, dtype=np.float32)),
        "pw": np.ascontiguousarray(np.asarray(pool_w, dtype=np.float32)[0]),
        "wkv": np.ascontiguousarray(np.asarray(w_kv, dtype=np.float32)),
        "wq": np.ascontiguousarray(np.asarray(w_q, dtype=np.float32)[0]),
        "wo": np.ascontiguousarray(np.asarray(w_o, dtype=np.float32)[0]),
    }
    key = (_stages, _debug)
    if key not in _NC_CACHE:
        _NC_CACHE[key] = build_program(_stages, _debug)
    nc = _NC_CACHE[key]
    in_maps = []
    for core in range(8):
        m = make_core_inputs(core, inputs)
        m.update(shared)
        in_maps.append(m)
    res = run_bass_kernel_spmd(nc, in_maps, core_ids=list(range(8)))
    R = res.results
    y_prompt = np.zeros((4, 4096, D), np.float32)
    y_sample = np.zeros((128, 1, D), np.float32)
    pool_prompt = np.zeros((1, 4, 15, D), np.float32)
    pool_sample = np.zeros((1, 128, 15, D), np.float32)
    kwp = np.zeros((4, 128, 4, 64), np.float32)
    vwp = np.zeros((4, 128, 4, 64), np.float32)
    kws = np.zeros((128, 128, 4, 64), np.float32)
    vws = np.zeros((128, 128, 4, 64), np.float32)
    for core in range(8):
        seq, half = core // 2, core % 2
        p0 = half * MAIN
        b0 = core * NS
        r = R[core]
        y_prompt[seq, p0:p0 + MAIN] = r["y"][0:MAIN]
        y_sample[b0:b0 + NS, 0] = r["y"][MAIN:MAIN + NS]
        pool_sample[0, b0:b0 + NS] = r["pool_s"]
        kws[b0:b0 + NS] = r["kws"].reshape(NS, 128, 4, 64)
        vws[b0:b0 + NS] = r["vws"].reshape(NS, 128, 4, 64)
        if half == 1:
            pool_prompt[0, seq] = r["pool_o"][0:15]
            kwp[seq] = r["kwp"].reshape(128, 4, 64)
            vwp[seq] = r["vwp"].reshape(128, 4, 64)
    if _debug:
        return (y_prompt, y_sample, pool_prompt, pool_sample, kwp, vwp, kws, vws), R
    return (y_prompt, y_sample, pool_prompt, pool_sample, kwp, vwp, kws, vws)
```
